# Optimizing a Trainium2 kernel written in Bass

```python
import math
import jax, jax.numpy as jnp
from jax import lax
import numpy as np

D_MODEL = 1024
BATCH = 8
SEQ = 4096
DEPTH = 2
DEC_BATCH = 128
DEC_SEQ = 1
PAST_LEN = 16384
PAGE_SIZE = 128

HEAD_DIM = 64
N_HEADS = 8
N_KV_HEADS = 2
GQA_GROUP = N_HEADS // N_KV_HEADS
ATTN_WIDTH = N_HEADS * HEAD_DIM
KV_WIDTH = N_KV_HEADS * HEAD_DIM
WINDOW = 128
BLOCK = WINDOW
POOL_SIZES = (2, 4, 8, 16)
POOL_GROUP_WIDTH = 128
POOL_WIDTH = len(POOL_SIZES) * POOL_GROUP_WIDTH
POOL_STATE = max(POOL_SIZES) - 1
MIX_WIDTH = ATTN_WIDTH + POOL_WIDTH
IN_WIDTH = ATTN_WIDTH + 2 * KV_WIDTH + POOL_WIDTH
D_FF = 2816
N_BUCKETS = 32
MAX_DISTANCE = 128
N_SUBLAYERS = 3
N_MOD = 3 * N_SUBLAYERS
EPS = 1e-6

kernel_name = "hybrid_swa_pool_macaron_adaln_step"


def rms_norm(x, g):
    xf = x.astype(jnp.float32)
    y = xf * lax.rsqrt(jnp.mean(xf * xf, axis=-1, keepdims=True) + EPS)
    return (y * g.astype(jnp.float32)).astype(x.dtype)


def swiglu(h, wg, wu, wd):
    return (jax.nn.silu(h @ wg) * (h @ wu)) @ wd


def t5_bucket(dist):
    n = jnp.maximum(dist, 0)
    max_exact = N_BUCKETS // 2
    nf = jnp.maximum(n, 1).astype(jnp.float32)
    large = max_exact + (jnp.log(nf / max_exact) / math.log(MAX_DISTANCE / max_exact)
                         * (N_BUCKETS - max_exact)).astype(jnp.int32)
    large = jnp.minimum(large, N_BUCKETS - 1)
    return jnp.where(n < max_exact, n, large)


def window_attention(q, k, v, key_valid, offset, sinks, rel_bias):
    B, N, Q = q.shape[:3]
    K = k.shape[2]
    qg = q.reshape(B, N, Q, N_KV_HEADS, GQA_GROUP, HEAD_DIM)
    s = jnp.einsum("bnqkgd,bnskd->bnkgqs", qg, k,
                   preferred_element_type=jnp.float32) * (HEAD_DIM ** -0.5)
    dist = jnp.arange(Q)[:, None] + offset - jnp.arange(K)[None, :]
    bias = rel_bias[t5_bucket(dist)].astype(jnp.float32)
    bias = bias.transpose(2, 0, 1).reshape(N_KV_HEADS, GQA_GROUP, Q, K)
    valid = ((dist >= 0) & (dist <= WINDOW))[None] & key_valid[:, None, :]
    s = jnp.where(valid[None, :, None, None], s + bias, -jnp.inf)
    sink = sinks.astype(jnp.float32).reshape(N_KV_HEADS, GQA_GROUP, 1, 1)
    m = jnp.maximum(jnp.max(s, axis=-1, keepdims=True), sink)
    p = jnp.exp(s - m)
    denom = jnp.sum(p, axis=-1, keepdims=True) + jnp.exp(sink - m)
    o = jnp.einsum("bnkgqs,bnskd->bnqkgd", (p / denom).astype(v.dtype), v)
    return o.reshape(B, N, Q, ATTN_WIDTH)


def prompt_attention(q, k, v, sinks, rel_bias):
    B, S = q.shape[:2]
    nb = S // BLOCK
    qb = q.reshape(B, nb, BLOCK, N_HEADS, HEAD_DIM)
    kb = k.reshape(B, nb, BLOCK, N_KV_HEADS, HEAD_DIM)
    vb = v.reshape(B, nb, BLOCK, N_KV_HEADS, HEAD_DIM)

    def with_prev(xb):
        prev = jnp.pad(xb, ((0, 0), (1, 0), (0, 0), (0, 0), (0, 0)))[:, :-1]
        return jnp.concatenate([prev, xb], axis=2)

    key_valid = jnp.concatenate(
        [jnp.broadcast_to(jnp.arange(nb)[:, None] > 0, (nb, BLOCK)),
         jnp.ones((nb, BLOCK), dtype=bool)], axis=1)
    o = window_attention(qb, with_prev(kb), with_prev(vb), key_valid, BLOCK, sinks, rel_bias)
    return o.reshape(B, S, ATTN_WIDTH)


def multiscale_pool(u_ext, pos0, n_prev, pool_w, pool_scale):
    L = u_ext.shape[1]
    uf = u_ext.astype(jnp.float32)
    cs = jnp.cumsum(uf, axis=1)
    pos = pos0 + jnp.arange(L)
    outs = []
    for g, w in enumerate(POOL_SIZES):
        lo, hi = g * POOL_GROUP_WIDTH, (g + 1) * POOL_GROUP_WIDTH
        cg = cs[..., lo:hi]
        lagged = jnp.pad(cg, ((0, 0), (w, 0), (0, 0)))[:, :L]
        count = jnp.minimum(pos + 1, w).astype(jnp.float32)[None, :, None]
        pooled = (cg - lagged) / count - uf[..., lo:hi]
        pooled = pooled[:, n_prev:].astype(u_ext.dtype)
        outs.append(pooled @ pool_w[g])
    return jnp.concatenate(outs, axis=-1) * pool_scale


def split_projection(z):
    return jnp.split(z, [ATTN_WIDTH, ATTN_WIDTH + KV_WIDTH, ATTN_WIDTH + 2 * KV_WIDTH], axis=-1)


def run_trunk(x, c, mixer, w_ada, b_ada, norm_gain, w_in, w_out,
              ffn1_wg, ffn1_wu, ffn1_wd, ffn2_wg, ffn2_wu, ffn2_wd, final_gain):
    states = []
    for l in range(DEPTH):
        mod = (jax.nn.silu(c) @ w_ada[l] + b_ada[l]).reshape(c.shape[0], 1, N_MOD, D_MODEL)

        def modulate(h, i):
            return rms_norm(h, norm_gain[l, i]) * (1 + mod[:, :, 3 * i + 1]) + mod[:, :, 3 * i]

        def gate(i):
            return mod[:, :, 3 * i + 2]

        x = x + 0.5 * gate(0) * swiglu(modulate(x, 0), ffn1_wg[l], ffn1_wu[l], ffn1_wd[l])
        z = modulate(x, 1) @ w_in[l]
        mixed, st = mixer(l, z)
        x = x + gate(1) * (mixed @ w_out[l])
        x = x + 0.5 * gate(2) * swiglu(modulate(x, 2), ffn2_wg[l], ffn2_wu[l], ffn2_wd[l])
        states.append(st)
    return rms_norm(x, final_gain), states


def setup_inputs(seed: int = 0) -> dict:
    key = jax.random.key(seed)
    ks = jax.random.split(key, 24)
    f32 = jnp.float32
    nrm = lambda k, shape, s: jax.random.normal(k, shape, f32) * s
    return {
        "x_prompt": nrm(ks[0], (BATCH, SEQ, D_MODEL), 1.0),
        "x_sample": nrm(ks[1], (DEC_BATCH, DEC_SEQ, D_MODEL), 1.0),
        "c_prompt": nrm(ks[2], (BATCH, D_MODEL), 1.0),
        "c_sample": nrm(ks[3], (DEC_BATCH, D_MODEL), 1.0),
        "cache_k": nrm(ks[4], (DEPTH, DEC_BATCH, WINDOW, N_KV_HEADS, HEAD_DIM), 1.0),
        "cache_v": nrm(ks[5], (DEPTH, DEC_BATCH, WINDOW, N_KV_HEADS, HEAD_DIM), 1.0),
        "state_pool": nrm(ks[6], (DEPTH, DEC_BATCH, POOL_STATE, POOL_WIDTH), 1.0),
        "w_ada": nrm(ks[7], (DEPTH, D_MODEL, N_MOD * D_MODEL), D_MODEL ** -0.5),
        "b_ada": nrm(ks[8], (DEPTH, N_MOD * D_MODEL), 0.02),
        "norm_gain": 1.0 + nrm(ks[9], (DEPTH, N_SUBLAYERS, D_MODEL), 0.02),
        "w_in": nrm(ks[10], (DEPTH, D_MODEL, IN_WIDTH), D_MODEL ** -0.5),
        "w_out": nrm(ks[11], (DEPTH, MIX_WIDTH, D_MODEL), MIX_WIDTH ** -0.5),
        "sinks": nrm(ks[12], (DEPTH, N_HEADS), 1.0),
        "rel_bias": nrm(ks[13], (N_BUCKETS, N_HEADS), 0.5),
        "pool_w": nrm(ks[14], (DEPTH, len(POOL_SIZES), POOL_GROUP_WIDTH, POOL_GROUP_WIDTH),
                      POOL_GROUP_WIDTH ** -0.5),
        "pool_scale": 1.0 + nrm(ks[15], (DEPTH, POOL_WIDTH), 0.1),
        "ffn1_wg": nrm(ks[16], (DEPTH, D_MODEL, D_FF), D_MODEL ** -0.5),
        "ffn1_wu": nrm(ks[17], (DEPTH, D_MODEL, D_FF), D_MODEL ** -0.5),
        "ffn1_wd": nrm(ks[18], (DEPTH, D_FF, D_MODEL), D_FF ** -0.5),
        "ffn2_wg": nrm(ks[19], (DEPTH, D_MODEL, D_FF), D_MODEL ** -0.5),
        "ffn2_wu": nrm(ks[20], (DEPTH, D_MODEL, D_FF), D_MODEL ** -0.5),
        "ffn2_wd": nrm(ks[21], (DEPTH, D_FF, D_MODEL), D_FF ** -0.5),
        "final_gain": 1.0 + nrm(ks[22], (D_MODEL,), 0.02),
    }


def reference(x_prompt, x_sample, c_prompt, c_sample, cache_k, cache_v, state_pool,
              w_ada, b_ada, norm_gain, w_in, w_out, sinks, rel_bias, pool_w, pool_scale,
              ffn1_wg, ffn1_wu, ffn1_wd, ffn2_wg, ffn2_wu, ffn2_wd, final_gain):

    def prompt_mixer(l, z):
        B, S = z.shape[:2]
        q, k, v, u = split_projection(z)
        k = k.reshape(B, S, N_KV_HEADS, HEAD_DIM)
        v = v.reshape(B, S, N_KV_HEADS, HEAD_DIM)
        attn = prompt_attention(q.reshape(B, S, N_HEADS, HEAD_DIM), k, v, sinks[l], rel_bias)
        pool = multiscale_pool(u, 0, 0, pool_w[l], pool_scale[l])
        return (jnp.concatenate([attn, pool], axis=-1),
                (k[:, -WINDOW:], v[:, -WINDOW:], u[:, -POOL_STATE:]))

    def sample_mixer(l, z):
        B, T = z.shape[:2]
        q, k, v, u = split_projection(z)
        k_all = jnp.concatenate([cache_k[l], k.reshape(B, T, N_KV_HEADS, HEAD_DIM)], axis=1)
        v_all = jnp.concatenate([cache_v[l], v.reshape(B, T, N_KV_HEADS, HEAD_DIM)], axis=1)
        key_valid = jnp.ones((1, WINDOW + T), dtype=bool)
        attn = window_attention(q.reshape(B, 1, T, N_HEADS, HEAD_DIM), k_all[:, None],
                                v_all[:, None], key_valid, WINDOW, sinks[l], rel_bias)[:, 0]
        u_ext = jnp.concatenate([state_pool[l], u], axis=1)
        pool = multiscale_pool(u_ext, PAST_LEN - POOL_STATE, POOL_STATE, pool_w[l], pool_scale[l])
        return (jnp.concatenate([attn, pool], axis=-1),
                (k_all[:, -WINDOW:], v_all[:, -WINDOW:], u_ext[:, -POOL_STATE:]))

    y_prompt, st_p = run_trunk(x_prompt, c_prompt, prompt_mixer, w_ada, b_ada, norm_gain, w_in,
                               w_out, ffn1_wg, ffn1_wu, ffn1_wd, ffn2_wg, ffn2_wu, ffn2_wd,
                               final_gain)
    y_sample, st_s = run_trunk(x_sample, c_sample, sample_mixer, w_ada, b_ada, norm_gain, w_in,
                               w_out, ffn1_wg, ffn1_wu, ffn1_wd, ffn2_wg, ffn2_wu, ffn2_wd,
                               final_gain)
    new_k_prompt = jnp.stack([s[0] for s in st_p])
    new_v_prompt = jnp.stack([s[1] for s in st_p])
    new_pool_prompt = jnp.stack([s[2] for s in st_p])
    new_k_sample = jnp.stack([s[0] for s in st_s])
    new_v_sample = jnp.stack([s[1] for s in st_s])
    new_pool_sample = jnp.stack([s[2] for s in st_s])
    return (y_prompt, y_sample, new_k_prompt, new_v_prompt, new_pool_prompt,
            new_k_sample, new_v_sample, new_pool_sample)
```

```python
import math
from contextlib import ExitStack

import numpy as np
import concourse.bass as bass
import concourse.mybir as mybir
from concourse.bass_utils import run_bass_kernel_spmd

F32 = mybir.dt.float32
BF16 = mybir.dt.bfloat16
AF = mybir.ActivationFunctionType
ALU = mybir.AluOpType
AX = mybir.AxisListType

NCORES = 8
D = 1024
SEQ = 4096
T = 1024
NT = SEQ // T
NS = 16
TS = T + NS
FF = 2816
NFF = FF // 128
DEPTH = 2
EPS = 1e-6
NEG = -30000.0
WSLOT = 4096
NSLOT = 3
POOL_W = (2, 4, 8, 16)
DEBUG_PHASES = 10 ** 6
DEBUG_MIX = 10 ** 6
DEBUG_SMIX = 10 ** 6


def t5_bucket_np(dist):
    n = np.maximum(dist, 0)
    nf = np.maximum(n, 1).astype(np.float32)
    large = 16 + (np.log(nf / np.float32(16)) / np.float32(math.log(128 / 16)) * np.float32(16)).astype(np.int32)
    large = np.minimum(large, 31)
    return np.where(n < 16, n, large)


class Sched:
    ENGS = ("pe", "act", "dve", "pool", "sp")

    def __init__(self):
        self.q = {e: [] for e in self.ENGS}
        self.cnt = {}
        self.waited = {e: {} for e in self.ENGS}
        self.lastw = {}
        self.readers = {}
        self.semnames = set(self.ENGS)

    def _deps(self, reads, writes):
        toks = []
        for k in reads:
            if k in self.lastw:
                toks.append(self.lastw[k])
        for k in writes:
            if k in self.lastw:
                toks.append(self.lastw[k])
            toks.extend(self.readers.get(k, {}).items())
        return toks

    def _waits(self, eng, toks):
        best = {}
        for s, v in toks:
            if v > best.get(s, 0):
                best[s] = v
        for s, v in best.items():
            if s == "pe" and eng == "pe":
                continue
            if s not in self.ENGS:
                v = self.cnt[s]
            if self.waited[eng].get(s, 0) >= v:
                continue
            self.waited[eng][s] = v
            self.q[eng].append(("wait", s, v))

    def _record(self, tok, reads, writes):
        for k in reads:
            d = self.readers.setdefault(k, {})
            if tok[1] > d.get(tok[0], 0):
                d[tok[0]] = tok[1]
        for k in writes:
            self.lastw[k] = tok
            self.readers[k] = {}

    def op(self, eng, fn, reads=(), writes=(), extra=()):
        toks = self._deps(reads, writes) + list(extra)
        self._waits(eng, toks)
        self.cnt[eng] = self.cnt.get(eng, 0) + 1
        tok = (eng, self.cnt[eng])
        self.q[eng].append(("inst", fn, eng, 1))
        self._record(tok, reads, writes)
        return tok

    def dma(self, queue, sem, fn, reads=(), writes=(), extra=()):
        self.semnames.add(sem)
        toks = self._deps(reads, writes) + list(extra)
        self._waits(queue, toks)
        self.cnt[sem] = self.cnt.get(sem, 0) + 16
        tok = (sem, self.cnt[sem])
        self.q[queue].append(("inst", fn, sem, 16))
        self._record(tok, reads, writes)
        return tok

    def fence(self, src, dst):
        best = {}
        for k in src:
            if k in self.lastw:
                s_, v_ = self.lastw[k]
                if v_ > best.get(s_, 0):
                    best[s_] = v_
            for s_, v_ in self.readers.get(k, {}).items():
                if v_ > best.get(s_, 0):
                    best[s_] = v_
        for k in dst:
            d = self.readers.setdefault(k, {})
            for s_, v_ in best.items():
                if v_ > d.get(s_, 0):
                    d[s_] = v_

    def wait_all(self, eng, toks):
        self._waits(eng, toks)


def CALL(name, *args, **kw):
    def run(e):
        return getattr(e, name)(*args, **kw)
    return run


def seq(fs):
    def run(e):
        last = None
        for f in fs:
            last = f(e)
        return last
    return run


def bcast(ap, pos, count):
    dims = [list(d) for d in ap.ap]
    dims.insert(pos, [0, count])
    return bass.AP(ap.tensor, ap.offset, dims)


def build_program():
    nc = bass.Bass("TRN2", target_bir_lowering=False)
    S = Sched()

    def din(name, shape):
        return nc.dram_tensor(name, list(shape), F32, kind="ExternalInput").ap()

    def dout(name, shape):
        return nc.dram_tensor(name, list(shape), F32, kind="ExternalOutput").ap()

    xp = din("xp", [SEQ, D])
    xs = din("xs", [NS, D])
    cp = din("cp", [8, 128])
    cs = din("cs", [NS, D])
    ck = din("ck", [DEPTH, NS, 128, 128])
    cv = din("cv", [DEPTH, NS, 128, 128])
    spool = din("spool", [DEPTH, NS, 15, 512])
    w_ada = din("w_ada", [DEPTH, D, 9 * D])
    b_ada = din("b_ada", [DEPTH, 72, 128])
    ngain = din("ngain", [DEPTH, 24, 128])
    w_in = din("w_in", [DEPTH, D, 1280])
    w_out = din("w_out", [DEPTH, D, D])
    sinks = din("sinks", [DEPTH, 8])
    relb = din("relb", [32, 8])
    pool_w = din("pool_w", [DEPTH, 4, 128, 128])
    pscale = din("pscale", [DEPTH, 4, 128])
    wg = [din("f1g", [DEPTH, D, FF]), din("f2g", [DEPTH, D, FF])]
    wu = [din("f1u", [DEPTH, D, FF]), din("f2u", [DEPTH, D, FF])]
    wd = [din("f1d", [DEPTH, FF, D]), din("f2d", [DEPTH, FF, D])]
    fgain = din("fgain", [8, 128])
    c_ident = din("c_ident", [128, 128])
    c_J = din("c_J", [128, 128])
    c_neg = din("c_neg", [128, 2, 128])
    c_invc = din("c_invc", [128, 4, 16])
    c_oh = din("c_oh", [32, 640])
    c_sel = din("c_sel", [16, 16 * 128])

    y_p = dout("y_p", [SEQ, D])
    y_s = dout("y_s", [NS, D])
    o_kp = dout("o_kp", [DEPTH, 128, 128])
    o_vp = dout("o_vp", [DEPTH, 128, 128])
    o_pp = dout("o_pp", [DEPTH, 15, 512])
    o_ks = dout("o_ks", [DEPTH, NS, 128, 128])
    o_vs = dout("o_vs", [DEPTH, NS, 128, 128])
    o_ps = dout("o_ps", [DEPTH, NS, 15, 512])
    scr = nc.dram_tensor("scr", [2, 8, 256], F32, kind="Internal").ap()

    es = ExitStack()

    def sb(name, shape, dt):
        return es.enter_context(nc.sbuf_tensor(name, list(shape), dt))

    xT = sb("xT", [128, 8, TS], F32)
    hT = sb("hT", [128, 8, TS], BF16)
    big = sb("big", [128, 23040], BF16)
    tmpf = sb("tmpf", [128, 2, 1056], F32)
    rstd = sb("rstd", [128, TS], F32)
    sil = sb("sil", [128, 2, TS], BF16)
    kT = sb("kT", [128, T + NS], BF16)
    kTf = sb("kTf", [128, 128], F32)
    vtok = sb("vtok", [128, 8, 128], BF16)
    vtokf = sb("vtokf", [128, 128], F32)
    kcar = sb("kcar", [128, DEPTH, 128], BF16)
    vcar = sb("vcar", [128, DEPTH, 128], BF16)
    ucar = sb("ucar", [128, DEPTH, 4, 16], F32)
    pT = sb("pT", [128, 2, 2, 512], BF16)
    biasT = sb("biasT", [128, 2, 8, 128], F32)
    rden = sb("rden", [128, 2, 512], F32)
    wsl = sb("wsl", [128, NSLOT, WSLOT], BF16)
    modT = sb("modT", [128, DEPTH, 72, 17], F32)
    gmul = sb("gmul", [128, DEPTH, 3, 8, 17], F32)
    gateT = sb("gateT", [128, DEPTH, 3, 8, 17], F32)
    colsT = sb("colsT", [128, DEPTH, 116], F32)
    rows = sb("rows", [128, DEPTH, 128], F32)
    cT = sb("cT", [128, 8, 17], F32)
    scT = sb("scT", [128, 8, 17], BF16)
    ident = sb("ident", [128, 128], F32)
    Jm = sb("Jm", [128, 128], F32)
    onesb = sb("onesb", [128, 128], BF16)
    blk1 = sb("blk1", [128, 128], F32)
    negm = sb("negm", [128, 2, 128], F32)
    invc = sb("invc", [128, 4, 16], F32)
    oh = sb("oh", [32, 640], F32)
    rb = sb("rb", [32, 8], F32)
    selb = sb("selb", [16, 16, 128], BF16)
    esink = sb("esink", [128, DEPTH, 4], F32)
    bias0 = sb("bias0", [128, 4], F32)
    bias_s = sb("bias_s", [128, 8], F32)
    epst = sb("epst", [128, 1], F32)
    kc = sb("kc", [128, NS, 128], BF16)
    vc = sb("vc", [128, NS, 128], BF16)
    uhist = sb("uhist", [128, 4, NS * 15], F32)
    zsT = sb("zsT", [128, 10, NS], F32)
    ztok = sb("ztok", [NS, 1280], F32)
    qtokb = sb("qtokb", [NS, 512], BF16)
    s_s = sb("s_s", [128, NS, 8], F32)
    pT_s = sb("pT_s", [128, NS, 8], BF16)
    smt = sb("smt", [128, 6, 4, NS], F32)

    psum = es.enter_context(nc.psum_tensor("psum", [128, 8, 512], F32))

    bigf = big.bitcast(F32)
    aT = big[:, 0:NFF * TS].rearrange("p (c t) -> p c t", c=NFF)
    xin = bigf[:, 0:8 * T].rearrange("p (b f) -> p b f", b=8)
    uT = bigf[:, 0:4 * 1056].rearrange("p (g t) -> p g t", g=4)
    pooledT = big[:, 8448:8448 + 4 * TS].rearrange("p (g t) -> p g t", g=4)
    sbf = bigf[:, 6304:6304 + 2048].rearrange("p (a b c) -> p a b c", a=2, b=2)
    qT = big[:, 16704:16704 + 4 * TS].rearrange("p (g t) -> p g t", g=4)
    mixedT = hT
    Hk = bigf[:, 0:2048].rearrange("p (t h q) -> p t h q", t=2, h=8)
    xsr = bigf[0:NS, 2048:3072]
    csr = bigf[0:NS, 3072:4096]
    hrows = bigf[0:120, 10432:10432 + 1024].rearrange("p (a f) -> p a f", a=2)
    utok_rows = bigf[0:NS, 10432:10432 + 512]
    self32 = bigf[0:16, 4096:6144]
    bvs = bigf[0:8, 6144:6656].rearrange("p (t e) -> p t e", t=2)
    yrow_s = bigf[0:NS, 8192:9216]
    prod = tmpf[:, 0, 0:512]

    AT_KEYS = ["aT%d" % j for j in range(NFF)] + ["aTs"]
    XSQ_KEYS = ["xsq%d" % c for c in range(8)] + ["xsqs%d" % c for c in range(8)]
    MIX_KEYS = ["uT", "uTs", "pooledT", "pooledTs", "sbf0", "sbf1", "qT", "qTs", "hrows", "utok_rows"] + XSQ_KEYS
    XIN_KEYS = ["xin%d" % b for b in range(8)] + ["yrow_s"]
    SETUP_KEYS = ["Hk", "xsr", "csr", "self32", "bvs"]
    H_KEYS = ["hT%d" % c for c in range(8)] + ["hTs"]
    BIG_ALL = AT_KEYS + MIX_KEYS + XIN_KEYS + SETUP_KEYS
    MX_KEYS = ["mx%d" % c for c in range(8)] + ["mxs"]

    def pbank(b):
        return psum[:, b, :]

    def ppair(p):
        return psum[:, 2 * p:2 * p + 2, :].rearrange("p a b -> p (a b)")

    PS6 = ["ps6"]

    def SL(n):
        bank = 6 + (n % 2)
        col = ((n // 2) % 8) * 32
        return psum[:, bank, col:col + NS], "ps%d" % bank

    def ps6(i, n=1):
        return psum[:, 6, i * 32:(i + n) * 32]

    wstate = {"n": 0}

    def wload(parts):
        s = wstate["n"] % NSLOT
        wstate["n"] += 1
        tok = None
        for dst, src in parts:
            d = dst(wsl[:, s, :])
            tok = S.dma("pool", "w%d" % s, (CALL("dma_start", out=d, in_=src)),
                        reads=(), writes=("ws%d" % s,))
        return s, tok

    def ld(dst, src, key, sem="cst", queue="sp"):
        return S.dma(queue, sem, (CALL("dma_start", out=dst, in_=src)), writes=(key,))

    ld(ident[:], c_ident, "ident")
    ld(Jm[:], c_J, "Jm")
    ld(negm[:], c_neg, "negm")
    ld(invc[:], c_invc, "invc")
    ld(oh[:], c_oh, "oh")
    ld(rb[:], relb, "rb")
    ld(self32[:], c_sel, "self32")
    for l in range(DEPTH):
        ld(rows[0:72, l, :], b_ada[l], "rows")
        ld(rows[72:96, l, :], ngain[l], "rows")
        ld(rows[96:100, l, :], pscale[l], "rows")
        ld(rows[100:108, l, :], fgain, "rows")
        ld(rows[108:116, l, :], cp, "rows")
    ld(xsr, xs, "xsr")
    ld(csr, cs, "csr")
    for l in range(DEPTH):
        for hh in range(2):
            src = bass.AP(sinks.tensor, l * 8 + hh * 4, [[0, 64], [1, 4]])
            ld(esink[hh * 64:(hh + 1) * 64, l, :], src, "esink")
    for hh in range(2):
        src = bass.AP(relb.tensor, hh * 4, [[0, 64], [1, 4]])
        ld(bias0[hh * 64:(hh + 1) * 64, :], src, "bias0")

    for key in ("ident", "Jm", "negm", "invc", "oh", "rb", "self32", "rows", "xsr", "csr", "esink", "bias0"):
        S.lastw[key] = ("cst", S.cnt["cst"])
    S.op("dve", CALL("memset", onesb[:], 1.0), writes=("onesb",))
    S.op("dve", CALL("memset", blk1[:], 0.0), writes=("blk1",))
    S.op("dve", CALL("memset", blk1[0:64, 0:64], 1.0), writes=("blk1",))
    S.op("dve", CALL("memset", blk1[64:128, 64:128], 1.0), writes=("blk1",))
    S.op("dve", CALL("memset", epst[:], EPS), writes=("epst",))
    S.op("dve", CALL("memset", kcar[:], 0.0), writes=("kcar0", "kcar1"))
    S.op("dve", CALL("memset", vcar[:], 0.0), writes=("vcar0", "vcar1"))
    S.op("dve", CALL("memset", ucar[:], 0.0), writes=("ucar0", "ucar1"))
    S.op("dve", CALL("tensor_copy", out=selb[:].rearrange("p a b -> p (a b)"), in_=self32[:]),
         reads=("self32",), writes=("selb",))
    S.op("act", CALL("activation", out=esink[:], in_=esink[:], func=AF.Exp), reads=("esink",), writes=("esink",))

    for l in range(DEPTH):
        S.op("pe", CALL("transpose", psum[:, 7, 0:116], rows[0:116, l, :], ident[0:116, 0:116]),
             reads=("rows", "ident"), writes=("ps7",))
        S.op("dve", CALL("tensor_copy", out=colsT[:, l, :], in_=psum[:, 7, 0:116]),
             reads=("ps7",), writes=("colsT",))
    S.op("pe", seq([(CALL("transpose", psum[:, 7, c * 16:(c + 1) * 16], csr[:, c * 128:(c + 1) * 128],
                                                  ident[0:NS, 0:NS])) for c in range(8)]),
         reads=("csr", "ident"), writes=("ps7",))
    S.op("dve", CALL("tensor_copy", out=cT[:, :, 1:17],
                                        in_=psum[:, 7, 0:128].rearrange("p (c b) -> p c b", c=8)),
         reads=("ps7",), writes=("cTs",))
    S.op("dve", CALL("tensor_copy", out=cT[:, :, 0], in_=colsT[:, 0, 108:116]), reads=("colsT",), writes=("cTp",))
    S.op("pe", seq([(CALL("transpose", psum[:, 7, c * 16:(c + 1) * 16], xsr[:, c * 128:(c + 1) * 128],
                                                  ident[0:NS, 0:NS])) for c in range(8)]),
         reads=("xsr", "ident"), writes=("ps7",))
    S.op("dve", CALL("tensor_copy", out=xT[:, :, T:TS],
                                        in_=psum[:, 7, 0:128].rearrange("p (c b) -> p c b", c=8)),
         reads=("ps7",), writes=("xTs",))
    S.op("act", CALL("activation", out=scT[:], in_=cT[:], func=AF.Silu), reads=("cTs", "cTp"), writes=("scT",))

    def ada_stage(l, st, bank):
        src = w_ada[l][:, st * 512:(st + 1) * 512].rearrange("(k p) n -> p k n", p=128)
        s, _ = wload([(lambda v: v.rearrange("p (k n) -> p k n", k=8), src)])
        wv = wsl[:, s, :].rearrange("p (k n) -> p k n", k=8)
        fs = []
        for oc in range(4):
            for k in range(8):
                fs.append(CALL("matmul", psum[:, bank, oc * 17:(oc + 1) * 17], wv[:, k, oc * 128:(oc + 1) * 128],
                               scT[:, k, :], start=(k == 0), stop=(k == 7)))
        S.op("pe", seq(fs), reads=("ws%d" % s, "scT"), writes=("ps%d" % bank,))
        S.op("dve", CALL("tensor_tensor", out=modT[:, l, st * 4:(st + 1) * 4, :],
                         in0=psum[:, bank, 0:68].rearrange("p (a b) -> p a b", a=4),
                         in1=bcast(colsT[:, l, st * 4:(st + 1) * 4], 2, 17), op=ALU.add),
             reads=("ps%d" % bank, "colsT"), writes=("modT%d" % l,))

    def ada_finish(l):
        for i in range(3):
            S.op("dve", CALL("scalar_tensor_tensor", out=gmul[:, l, i, :, :],
                             in0=modT[:, l, (3 * i + 1) * 8:(3 * i + 2) * 8, :], scalar=1.0,
                             in1=bcast(colsT[:, l, 72 + i * 8:72 + (i + 1) * 8], 2, 17), op0=ALU.add, op1=ALU.mult),
                 reads=("modT%d" % l, "colsT"), writes=("gmul%d" % l,))
            S.op("dve", CALL("tensor_scalar", out=gateT[:, l, i, :, :],
                             in0=modT[:, l, (3 * i + 2) * 8:(3 * i + 3) * 8, :],
                             scalar1=(1.0 if i == 1 else 0.5), scalar2=None, op0=ALU.mult),
                 reads=("modT%d" % l,), writes=("gateT%d" % l,))

    for st in range(18):
        ada_stage(0, st, st % 6)
    ada_finish(0)
    pending = [(lambda st=st: ada_stage(1, st, 7)) for st in range(18)]

    def pop_pending():
        if pending:
            pending.pop(0)()

    def shiftp(l, i, c):
        return modT[:, l, 3 * i * 8 + c, 0:1]

    def shifts(l, i):
        return modT[:, l, 3 * i * 8:3 * i * 8 + 8, 1:17]

    for t in range(2):
        S.op("pe", CALL("matmul", psum[0:8, 7, 0:256], rb[:, :], oh[:, t * 256:(t + 1) * 256],
                                           start=True, stop=True), reads=("rb", "oh"), writes=("ps7",))
        S.op("dve", CALL("tensor_copy", out=bvs[:, t, :], in_=psum[0:8, 7, 0:256]),
             reads=("ps7",), writes=("bvs",))
    S.dma("sp", "scrw", CALL("dma_start", out=scr.rearrange("t h e -> h t e"), in_=bvs[:]),
          reads=("bvs",), writes=("scr",))
    S.fence(SETUP_KEYS, ["Hk"])
    for t in range(2):
        src = bass.AP(scr.tensor, t * 2048, [[1, 128], [256, 8], [1, 128]])
        S.dma("sp", "scrr", CALL("dma_start", out=Hk[:, t, :, :], in_=src),
              reads=("scr",), writes=("Hk",))
    for t in range(2):
        hv = Hk[:, t, :, :].rearrange("p h q -> p (h q)")
        S.op("pe", seq([(CALL("matmul", psum[:, hf, :], Jm[:, :], hv[:, hf * 512:(hf + 1) * 512],
                                                        start=True, stop=True)) for hf in range(2)]),
             reads=("Hk", "Jm"), writes=("ps0", "ps1"))
        S.op("dve", CALL("tensor_tensor",
            out=biasT[:, t, :, :], in0=ppair(0).rearrange("p (h q) -> p h q", h=8),
            in1=bcast(negm[:, t, :], 1, 8), op=ALU.add), reads=("ps0", "ps1", "negm"), writes=("biasT",))
    S.op("pe", CALL("matmul", psum[:, 7, 0:8].rearrange("p (g hh) -> p g hh", g=4), oh[:, 512:640],
                                  rb[:, :].rearrange("b (hh g) -> b g hh", hh=2), start=True, stop=True),
         reads=("rb", "oh"), writes=("ps7",))
    S.op("dve", CALL("tensor_copy", out=bias_s[:], in_=psum[:, 7, 0:8]), reads=("ps7",), writes=("bias_s",))

    def segs(ti):
        sg = [(0, 512, "h0"), (512, 512, "h1")]
        if ti == 0:
            sg.append((T, NS, "s"))
        return sg

    def sq_buf(c, alt):
        if not alt:
            return hT[:, c, :], ("hT%d" % c,), ("hTs",)
        if c < 4:
            return pooledT[:, c, :], ("xsq%d" % c,), ("xsqs%d" % c,)
        return qT[:, c - 4, :], ("xsq%d" % c,), ("xsqs%d" % c,)

    def sq_op(ti, c, xkeys, alt=False):
        n = TS if ti == 0 else T
        buf, k1, k2 = sq_buf(c, alt)
        S.op("dve", CALL("tensor_tensor", out=buf[:, 0:n], in0=xT[:, c, 0:n], in1=xT[:, c, 0:n], op=ALU.mult),
             reads=xkeys, writes=k1 + (k2 if ti == 0 else ()))

    def stats_mm(ti, c, alt=False):
        buf, k1, k2 = sq_buf(c, alt)
        fs = []
        for (o, w, nm) in segs(ti):
            out = psum[:, 4, :] if nm == "h0" else (psum[:, 5, :] if nm == "h1" else psum[:, 7, 0:NS])
            fs.append(CALL("matmul", out, onesb[:, :], buf[:, o:o + w], start=(c == 0), stop=(c == 7)))
        wr = ("ps4", "ps5") + (("ps7",) if ti == 0 else ())
        S.op("pe", seq(fs), reads=k1 + ("onesb",) + (k2 if ti == 0 else ()), writes=wr)

    def norm_stats_ops(ti, c, xkeys):
        sq_op(ti, c, xkeys)
        stats_mm(ti, c)

    def rstd_ops(ti):
        n = TS if ti == 0 else T
        S.op("act", CALL("activation", out=rstd[:, 0:T], in_=ppair(2), func=AF.Ln, bias=epst[:, 0:1],
                                           scale=1.0 / D), reads=("ps4", "ps5", "epst"), writes=("rstd",))
        if ti == 0:
            S.op("act", CALL("activation", out=rstd[:, T:TS], in_=psum[:, 7, 0:NS], func=AF.Ln,
                                               bias=epst[:, 0:1], scale=1.0 / D),
                 reads=("ps7", "epst"), writes=("rstds",))
        S.op("act", CALL("activation", out=rstd[:, 0:n], in_=rstd[:, 0:n], func=AF.Exp, scale=-0.5),
             reads=("rstd", "rstds"), writes=("rstd", "rstds"))

    def modulate_ops(ti, l, i):
        for c in range(8):
            pb = c % 2
            S.op("dve", CALL("tensor_tensor", out=tmpf[:, pb, 0:T], in0=xT[:, c, 0:T],
                                                              in1=rstd[:, 0:T], op=ALU.mult),
                 reads=("xT%d" % c, "rstd"), writes=("tmpf%d" % pb,))
            S.op("act", CALL("activation", out=hT[:, c, 0:T], in_=tmpf[:, pb, 0:T],
                                                           func=AF.Identity, scale=gmul[:, l, i, c, 0:1],
                                                           bias=shiftp(l, i, c)),
                 reads=("tmpf%d" % pb, "gmul%d" % l, "modT%d" % l), writes=("hT%d" % c,))
        if ti == 0:
            S.op("dve", CALL("tensor_tensor", out=smt[:, 0:2, :, :].rearrange("p a g b -> p (a g) b"),
                                                  in0=xT[:, :, T:TS], in1=bcast(rstd[:, T:TS], 1, 8), op=ALU.mult),
                 reads=("xTs", "rstds"), writes=("smt",))
            S.op("dve", CALL("tensor_tensor", out=smt[:, 0:2, :, :].rearrange("p a g b -> p (a g) b"),
                                                  in0=smt[:, 0:2, :, :].rearrange("p a g b -> p (a g) b"),
                                                  in1=gmul[:, l, i, :, 1:17], op=ALU.mult),
                 reads=("smt", "gmul%d" % l), writes=("smt",))
            S.op("dve", CALL("tensor_tensor", out=hT[:, :, T:TS],
                                                  in0=smt[:, 0:2, :, :].rearrange("p a g b -> p (a g) b"),
                                                  in1=shifts(l, i), op=ALU.add),
                 reads=("smt", "modT%d" % l), writes=("hTs",))

    def hkeys(ti):
        return H_KEYS if ti == 0 else H_KEYS[:8]

    def residual_ops(ti, l, i, m, pp):
        S.op("dve", CALL("scalar_tensor_tensor", out=xT[:, m, 0:T], in0=ppair(pp), scalar=gateT[:, l, i, m, 0:1],
                                                     in1=xT[:, m, 0:T], op0=ALU.mult, op1=ALU.add),
             reads=("ps%d" % (2 * pp), "ps%d" % (2 * pp + 1), "gateT%d" % l, "xT%d" % m), writes=("xT%d" % m,))

    def residual_sample(ti, l, i):
        if ti != 0:
            return
        dsl = psum[:, 6, 256:256 + 128].rearrange("p (m b) -> p m b", m=8)
        S.op("dve", CALL("tensor_tensor", out=smt[:, 0:2, :, :].rearrange("p a g b -> p (a g) b"), in0=dsl,
                                              in1=gateT[:, l, i, :, 1:17], op=ALU.mult),
             reads=PS6 + ["gateT%d" % l], writes=("smt",))
        S.op("dve", CALL("tensor_tensor", out=xT[:, :, T:TS], in0=xT[:, :, T:TS],
                                              in1=smt[:, 0:2, :, :].rearrange("p a g b -> p (a g) b"), op=ALU.add),
             reads=("smt", "xTs"), writes=("xTs",))

    def ffn(ti, l, fi, i):
        modulate_ops(ti, l, i)
        sg = segs(ti)
        n = TS if ti == 0 else T
        S.fence(BIG_ALL, AT_KEYS)
        cnt = 0
        for st in range(NFF // 2):
            if ti == 0 and l == 0:
                pop_pending()
            srcg = wg[fi][l][:, st * 256:(st + 1) * 256].rearrange("(k p) n -> p k n", p=128)
            srcu = wu[fi][l][:, st * 256:(st + 1) * 256].rearrange("(k p) n -> p k n", p=128)
            s, _ = wload([(lambda v: v[:, 0:2048].rearrange("p (k n) -> p k n", k=8), srcg),
                          (lambda v: v[:, 2048:4096].rearrange("p (k n) -> p k n", k=8), srcu)])
            wgv = wsl[:, s, 0:2048].rearrange("p (k n) -> p k n", k=8)
            wuv = wsl[:, s, 2048:4096].rearrange("p (k n) -> p k n", k=8)
            for jj in range(2):
                j = st * 2 + jj
                outs = []
                for which, wv in ((0, wgv), (1, wuv)):
                    pp = cnt % 3
                    sl = cnt
                    cnt += 1
                    fs = []
                    for k in range(8):
                        for (o, w, nm) in sg:
                            out = (psum[:, 2 * pp, :] if nm == "h0" else psum[:, 2 * pp + 1, :] if nm == "h1"
                                   else SL(sl)[0])
                            fs.append(CALL("matmul",
                                out, wv[:, k, jj * 128:(jj + 1) * 128], hT[:, k, o:o + w],
                                start=(k == 0), stop=(k == 7)))
                    wr = ["ps%d" % (2 * pp), "ps%d" % (2 * pp + 1)] + ([SL(sl)[1]] if ti == 0 else [])
                    S.op("pe", seq(fs), reads=["ws%d" % s] + hkeys(ti), writes=wr)
                    outs.append((pp, sl))
                (pg, sg_), (pu, su_) = outs
                sb_ = j % 2
                S.op("act", CALL("activation", out=sil[:, sb_, 0:T], in_=ppair(pg), func=AF.Silu),
                     reads=("ps%d" % (2 * pg), "ps%d" % (2 * pg + 1)), writes=("sil%d" % sb_,))
                S.op("dve", CALL("tensor_tensor", out=aT[:, j, 0:T], in0=sil[:, sb_, 0:T],
                                                                       in1=ppair(pu), op=ALU.mult),
                     reads=("sil%d" % sb_, "ps%d" % (2 * pu), "ps%d" % (2 * pu + 1)), writes=("aT%d" % j,))
                if ti == 0:
                    S.op("act", CALL("activation", out=sil[:, sb_, T:TS],
                                                                         in_=SL(sg_)[0],
                                                                         func=AF.Silu),
                         reads=(SL(sg_)[1],), writes=("sils%d" % sb_,))
                    S.op("dve", CALL("tensor_tensor",
                        out=aT[:, j, T:TS], in0=sil[:, sb_, T:TS], in1=SL(su_)[0],
                        op=ALU.mult), reads=("sils%d" % sb_, SL(su_)[1]), writes=("aTs",))
        sgm = segs(ti)
        for m in range(8):
            src = wd[fi][l][:, m * 128:(m + 1) * 128].rearrange("(k p) n -> p k n", p=128)
            s1, _ = wload([(lambda v: v[:, 0:NFF * 128].rearrange("p (k n) -> p k n", k=NFF), src)])
            wv = wsl[:, s1, 0:NFF * 128].rearrange("p (k n) -> p k n", k=NFF)
            pp = m % 2
            fs = []
            for k in range(NFF):
                for (o, w, nm) in sgm:
                    out = (psum[:, 2 * pp, :] if nm == "h0" else psum[:, 2 * pp + 1, :] if nm == "h1"
                           else psum[:, 6, 256 + m * 16:256 + (m + 1) * 16])
                    fs.append(CALL("matmul",
                        out, wv[:, k, :], aT[:, k, o:o + w], start=(k == 0), stop=(k == NFF - 1)))
            wr = ["ps%d" % (2 * pp), "ps%d" % (2 * pp + 1)] + (PS6 if ti == 0 else [])
            S.op("pe", seq(fs), reads=["ws%d" % s1] + (AT_KEYS if ti == 0 else AT_KEYS[:NFF]), writes=wr)
            residual_ops(ti, l, i, m, pp)
            if ti != 0:
                sq_op(ti, m, ("xT%d" % m,))
                if m >= 1:
                    stats_mm(ti, m - 1)
        if ti != 0:
            stats_mm(ti, 7)
        if ti == 0:
            residual_sample(ti, l, i)
            for m in range(8):
                sq_op(ti, m, ("xT%d" % m, "xTs"))
            for m in range(8):
                stats_mm(ti, m)
        rstd_ops(ti)

    def load_x(ti):
        S.fence(BIG_ALL, XIN_KEYS)
        for b in range(8):
            r0 = ti * T + b * 128
            S.dma("sp", "xin%d" % b, CALL("dma_start", out=xin[:, b, :], in_=xp[r0:r0 + 128, :]),
                  writes=("xin%d" % b,), extra=([("outy%d" % (ti - 1), S.cnt["outy%d" % (ti - 1)])] if ti > 0 else []))
        for c in range(8):
            for hf in range(2):
                pb = (c * 2 + hf) % 4
                S.op("pe", seq([(CALL("transpose",
                    psum[:, pb, (b % 4) * 128:(b % 4 + 1) * 128], xin[:, b, c * 128:(c + 1) * 128], ident[:, :]))
                    for b in range(hf * 4, hf * 4 + 4)]),
                     reads=["xin%d" % b for b in range(hf * 4, hf * 4 + 4)] + ["ident"], writes=("ps%d" % pb,))
                S.op("dve" if hf == 0 else "act",
                     (CALL("tensor_copy", out=xT[:, c, hf * 512:(hf + 1) * 512], in_=psum[:, pb, :]))
                     if hf == 0 else
                     (CALL("activation", out=xT[:, c, hf * 512:(hf + 1) * 512], in_=psum[:, pb, :],
                                                                 func=AF.Identity)),
                     reads=("ps%d" % pb,), writes=("xT%d" % c,))
        for c in range(8):
            sq_op(ti, c, ("xT%d" % c, "xTs") if ti == 0 else ("xT%d" % c,))
        for c in range(8):
            stats_mm(ti, c)
        rstd_ops(ti)

    def store_y(ti):
        S.fence(BIG_ALL, XIN_KEYS)
        for c in range(8):
            pb = c % 2
            S.op("dve", CALL("tensor_tensor", out=tmpf[:, pb, 0:T], in0=xT[:, c, 0:T],
                                                              in1=rstd[:, 0:T], op=ALU.mult),
                 reads=("xT%d" % c, "rstd"), writes=("tmpf%d" % pb,))
            S.op("act", CALL("activation", out=xT[:, c, 0:T], in_=tmpf[:, pb, 0:T], func=AF.Identity,
                                                           scale=colsT[:, 0, 100 + c:101 + c]),
                 reads=("tmpf%d" % pb, "colsT"), writes=("xT%d" % c,))
        if ti == 0:
            S.op("dve", CALL("tensor_tensor", out=xT[:, :, T:TS], in0=xT[:, :, T:TS],
                                                  in1=bcast(rstd[:, T:TS], 1, 8), op=ALU.mult),
                 reads=("xTs", "rstds"), writes=("xTs",))
            S.op("dve", CALL("tensor_tensor", out=xT[:, :, T:TS], in0=xT[:, :, T:TS],
                                                  in1=bcast(colsT[:, 0, 100:108], 2, NS), op=ALU.mult),
                 reads=("xTs", "colsT"), writes=("xTs",))
            S.op("pe", seq([(CALL("transpose", psum[0:NS, 6 + c // 4, (c % 4) * 128:(c % 4 + 1) * 128],
                                                          xT[:, c, T:TS], ident[:, :])) for c in range(8)]),
                 reads=("xTs", "ident"), writes=PS6 + ["ps7"])
            S.op("dve", CALL("tensor_copy", out=yrow_s[:, :], in_=psum[0:NS, 6:8, :].rearrange("p a b -> p (a b)")),
                 reads=PS6 + ["ps7"], writes=("yrow_s",))
            S.dma("sp", "outy%d" % ti, CALL("dma_start", out=y_s, in_=yrow_s[:, :]), reads=("yrow_s",))
        for b in range(8):
            pp = b % 3
            S.op("pe", seq([(CALL("transpose",
                psum[:, 2 * pp + c // 4, (c % 4) * 128:(c % 4 + 1) * 128], xT[:, c, b * 128:(b + 1) * 128], ident[:, :]))
                for c in range(8)]),
                 reads=["xT%d" % c for c in range(8)] + ["ident"], writes=("ps%d" % (2 * pp), "ps%d" % (2 * pp + 1)))
            if b % 2 == 0:
                S.op("dve", CALL("tensor_copy", out=xin[:, b, :], in_=ppair(pp)),
                     reads=("ps%d" % (2 * pp), "ps%d" % (2 * pp + 1)), writes=("xin%d" % b,))
            else:
                S.op("act", CALL("activation", out=xin[:, b, :], in_=ppair(pp), func=AF.Identity),
                     reads=("ps%d" % (2 * pp), "ps%d" % (2 * pp + 1)), writes=("xin%d" % b,))
            r0 = ti * T + b * 128
            S.dma("sp", "outy%d" % ti, CALL("dma_start", out=y_p[r0:r0 + 128, :], in_=xin[:, b, :]),
                  reads=("xin%d" % b,))

    def mixer(ti, l):
        modulate_ops(ti, l, 1)
        sg = segs(ti)
        last = (ti == NT - 1)
        S.fence(BIG_ALL, MIX_KEYS)
        cnt = [0]

        def fm_proj(wv, col0, dst_kind, gidx):
            pp = cnt[0] % 3
            sl = cnt[0]
            cnt[0] += 1
            fs = []
            for k in range(8):
                for (o, w, nm) in sg:
                    out = (psum[:, 2 * pp, :] if nm == "h0" else psum[:, 2 * pp + 1, :] if nm == "h1"
                           else SL(sl)[0])
                    fs.append(CALL("matmul",
                        out, wv[:, k, col0:col0 + 128], hT[:, k, o:o + w], start=(k == 0), stop=(k == 7)))
            wr = ["ps%d" % (2 * pp), "ps%d" % (2 * pp + 1)] + ([SL(sl)[1]] if ti == 0 else [])
            S.op("pe", seq(fs), reads=[wkey[0]] + hkeys(ti), writes=wr)
            pk = ("ps%d" % (2 * pp), "ps%d" % (2 * pp + 1))
            if dst_kind == "q":
                S.op("act", CALL("activation", out=qT[:, gidx, 0:T], in_=ppair(pp), func=AF.Identity, scale=0.125),
                     reads=pk, writes=("qT",))
            elif dst_kind == "k":
                S.op("act", CALL("activation", out=kT[:, 0:T], in_=ppair(pp), func=AF.Identity),
                     reads=pk, writes=("kT",))
                if last:
                    S.op("act", CALL("activation", out=kTf[:, :], in_=psum[:, 2 * pp + 1, 384:512], func=AF.Identity),
                         reads=pk, writes=("kTf",))
            elif dst_kind == "u":
                S.op("act" if gidx % 2 == 0 else "dve",
                     (CALL("activation", out=uT[:, gidx, 16:16 + T], in_=ppair(pp), func=AF.Identity))
                     if gidx % 2 == 0 else
                     (CALL("tensor_copy", out=uT[:, gidx, 16:16 + T], in_=ppair(pp))),
                     reads=pk, writes=("uT",))
            if ti == 0:
                zi = {"q": gidx, "k": 4, "v": 5, "u": 6 + gidx}[dst_kind]
                S.op("dve", CALL("tensor_copy", out=zsT[:, zi, :], in_=SL(sl)[0]),
                     reads=(SL(sl)[1],), writes=("zsT",))

        def tok_proj_sample(wv, col0, ncols, zcol0):
            fs = [(CALL("matmul", psum[0:NS, 7, 0:ncols], hT[:, k, T:TS], wv[:, k, col0:col0 + ncols],
                                           start=(k == 0), stop=(k == 7))) for k in range(8)]
            S.op("pe", seq(fs), reads=[wkey[0], "hTs"], writes=("ps7",))
            S.op("dve", CALL("tensor_copy", out=ztok[:, zcol0:zcol0 + ncols], in_=psum[0:NS, 7, 0:ncols]),
                 reads=("ps7",), writes=("ztok",))

        wkey = [None]
        src = w_in[l][:, 768:1280].rearrange("(k p) n -> p k n", p=128)
        s, _ = wload([(lambda v: v.rearrange("p (k n) -> p k n", k=8), src)])
        wkey[0] = "ws%d" % s
        wuu = wsl[:, s, :].rearrange("p (k n) -> p k n", k=8)
        S.op("dve", CALL("tensor_copy", out=uT[:, :, 0:16], in_=ucar[:, l, :, :]),
             reads=("ucar%d" % l,), writes=("uT",))
        for g in range(4):
            fm_proj(wuu, g * 128, "u", g)
        if ti == 0:
            tok_proj_sample(wuu, 0, 512, 768)
        S.op("dve", CALL("tensor_copy", out=ucar[:, l, :, :], in_=uT[:, :, T:T + 16]),
             reads=("uT",), writes=("ucar%d" % l,))

        for g in range(4):
            wdw = POOL_W[g]
            cur = uT[:, g, :]
            lvl = 1
            step = 0
            while lvl < wdw:
                dstb = tmpf[:, step % 2, :]
                lo = 2 * lvl - 1
                S.op("dve", CALL("tensor_tensor",
                    out=dstb[:, lo:16 + T], in0=cur[:, lo:16 + T], in1=cur[:, lo - lvl:16 + T - lvl], op=ALU.add),
                     reads=("uT", "tmpf0", "tmpf1"), writes=("tmpf%d" % (step % 2),))
                cur = dstb
                lvl *= 2
                step += 1
            S.op("dve", CALL("scalar_tensor_tensor",
                out=pooledT[:, g, 0:T], in0=cur[:, 16:16 + T], scalar=1.0 / wdw, in1=uT[:, g, 16:16 + T],
                op0=ALU.mult, op1=ALU.subtract), reads=("uT", "tmpf0", "tmpf1"), writes=("pooledT",))
            if ti == 0:
                S.op("dve", CALL("tensor_tensor", out=smt[:, 2, 0, :], in0=cur[:, 16:32],
                                                                    in1=invc[:, g, :], op=ALU.mult),
                     reads=("tmpf0", "tmpf1", "invc"), writes=("smt",))
                S.op("dve", CALL("tensor_tensor", out=pooledT[:, g, 0:16], in0=smt[:, 2, 0, :],
                                                           in1=uT[:, g, 16:32], op=ALU.subtract),
                     reads=("smt", "uT"), writes=("pooledT",))
        if last:
            S.op("pe", seq([(CALL("transpose", psum[0:15, 7, g * 128:(g + 1) * 128],
                                                          uT[:, g, 16 + T - 15:16 + T], ident[:, :])) for g in range(4)]),
                 reads=("uT", "ident"), writes=("ps7",))
            S.op("dve", CALL("tensor_copy", out=utok_rows[0:15, :], in_=psum[0:15, 7, :]),
                 reads=("ps7",), writes=("utok_rows",))
            S.dma("sp", "outl%d" % l, CALL("dma_start", out=o_pp[l], in_=utok_rows[0:15, :]), reads=("utok_rows",))
        src = w_in[l][:, 0:512].rearrange("(k p) n -> p k n", p=128)
        s, _ = wload([(lambda v: v.rearrange("p (k n) -> p k n", k=8), src)])
        wkey[0] = "ws%d" % s
        wq = wsl[:, s, :].rearrange("p (k n) -> p k n", k=8)
        for g in range(4):
            fm_proj(wq, g * 128, "q", g)
        if ti == 0:
            tok_proj_sample(wq, 0, 512, 0)
        src = w_in[l][:, 512:768].rearrange("(k p) n -> p k n", p=128)
        s, _ = wload([(lambda v: v[:, 0:2048].rearrange("p (k n) -> p k n", k=8), src)])
        wkey[0] = "ws%d" % s
        wkv = wsl[:, s, 0:2048].rearrange("p (k n) -> p k n", k=8)
        fm_proj(wkv, 0, "k", 0)
        for hf in range(2):
            pb = 2 * (cnt[0] % 3) + hf
            fs = []
            for bb in range(4):
                b = hf * 4 + bb
                for k in range(8):
                    fs.append(CALL("matmul",
                        psum[:, pb, bb * 128:(bb + 1) * 128], hT[:, k, b * 128:(b + 1) * 128], wkv[:, k, 128:256],
                        start=(k == 0), stop=(k == 7)))
            S.op("pe", seq(fs), reads=[wkey[0]] + H_KEYS[:8], writes=("ps%d" % pb,))
            S.op("act", CALL("activation",
                out=vtok[:, hf * 4:hf * 4 + 4, :], in_=psum[:, pb, :].rearrange("p (a b) -> p a b", a=4), func=AF.Identity),
                 reads=("ps%d" % pb,), writes=("vtok",))
            if last and hf == 1:
                S.op("act", CALL("activation", out=vtokf[:, :], in_=psum[:, pb, 384:512], func=AF.Identity),
                     reads=("ps%d" % pb,), writes=("vtokf",))
        cnt[0] += 1
        if ti == 0:
            sl = cnt[0]
            cnt[0] += 1
            S.op("pe", seq([(CALL("matmul", SL(sl)[0], wkv[:, k, 128:256],
                                                           hT[:, k, T:TS], start=(k == 0), stop=(k == 7)))
                            for k in range(8)]), reads=[wkey[0], "hTs"], writes=(SL(sl)[1],))
            S.op("dve", CALL("tensor_copy", out=zsT[:, 5, :], in_=SL(sl)[0]),
                 reads=(SL(sl)[1],), writes=("zsT",))
            tok_proj_sample(wkv, 0, 256, 512)
        if last:
            S.op("pe", CALL("transpose", psum[:, 7, 0:128], kTf[:, :], ident[:, :]),
                 reads=("kTf", "ident"), writes=("ps7",))
            S.op("dve", CALL("tensor_copy", out=kTf[:, :], in_=psum[:, 7, 0:128]), reads=("ps7",), writes=("kTf",))
            S.dma("sp", "outl%d" % l, CALL("dma_start", out=o_kp[l], in_=kTf[:, :]), reads=("kTf",))
            S.dma("sp", "outl%d" % l, CALL("dma_start", out=o_vp[l], in_=vtokf[:, :]), reads=("vtokf",))
        if ti == 0:
            sample_mixer(l)

        S.fence(H_KEYS, MX_KEYS)
        src = pool_w[l].rearrange("g p n -> p g n")
        s, _ = wload([(lambda v: v[:, 0:512].rearrange("p (g n) -> p g n", g=4), src)])
        wpl = wsl[:, s, 0:512].rearrange("p (g n) -> p g n", g=4)
        for g in range(4):
            pp = cnt[0] % 3
            sl = cnt[0]
            cnt[0] += 1
            fs = []
            for (o, w, nm) in sg:
                out = (psum[:, 2 * pp, :] if nm == "h0" else psum[:, 2 * pp + 1, :] if nm == "h1"
                       else SL(sl)[0])
                fs.append(CALL("matmul", out, wpl[:, g, :], pooledT[:, g, o:o + w],
                                                                      start=True, stop=True))
            wr = ["ps%d" % (2 * pp), "ps%d" % (2 * pp + 1)] + ([SL(sl)[1]] if ti == 0 else [])
            S.op("pe", seq(fs), reads=["ws%d" % s, "pooledT"] + (["pooledTs"] if ti == 0 else []), writes=wr)
            S.op("act", CALL("activation", out=mixedT[:, 4 + g, 0:T], in_=ppair(pp), func=AF.Identity,
                                                           scale=colsT[:, l, 96 + g:97 + g]),
                 reads=("ps%d" % (2 * pp), "ps%d" % (2 * pp + 1), "colsT"), writes=("mx%d" % (4 + g),))
            if ti == 0:
                S.op("act", CALL("activation", out=mixedT[:, 4 + g, T:TS],
                                                               in_=SL(sl)[0], func=AF.Identity,
                                                               scale=colsT[:, l, 96 + g:97 + g]),
                     reads=(SL(sl)[1], "colsT"), writes=("mxs",))

        def unit_info(n):
            qb, j = n // 2, n % 2
            first_blk = (ti == 0 and qb == 0)
            types = ([] if first_blk else [0]) + [1]
            return qb, j, j * 64, (j + 1) * 64, n % 2, types

        def pbuf_of(n):
            b4 = n % 4
            if b4 < 2:
                return pT[:, b4, :, :], "pT%d" % b4
            return sil[:, b4 - 2, 0:1024].rearrange("p (a b) -> p a b", a=2), "sil%d" % (b4 - 2)

        def sbuf_of(n):
            b4 = n % 4
            if b4 < 2:
                return sbf[:, b4, :, :], "sbf%d" % b4
            return tmpf[:, b4 - 2, 0:1024].rearrange("p (a b) -> p a b", a=2), "tmpf%d" % (b4 - 2)

        def att_front(n):
            qb, j, P0, P1, u, types = unit_info(n)
            fs = []
            for t in types:
                if t == 0:
                    kl = kcar[P0:P1, l, :] if qb == 0 else kT[P0:P1, (qb - 1) * 128:qb * 128]
                else:
                    kl = kT[P0:P1, qb * 128:(qb + 1) * 128]
                fs.append(CALL("matmul", psum[:, 2 * u + t, :], kl, qT[P0:P1, :, qb * 128:(qb + 1) * 128],
                               start=True, stop=True))
            S.op("pe", seq(fs), reads=("kT", "qT", "kcar%d" % l), writes=("ps%d" % (2 * u), "ps%d" % (2 * u + 1)))
            sbv, sbk = sbuf_of(n)
            b4 = n % 4
            for t in types:
                S.op("dve", CALL("tensor_tensor", out=sbv[:, t, :], in0=psum[:, 2 * u + t, :],
                                 in1=biasT[:, t, 4 * j:4 * j + 4, :].rearrange("p h q -> p (h q)"), op=ALU.add),
                     reads=("ps%d" % (2 * u + t), "biasT"), writes=(sbk,))
            t0 = types[0]
            pbv, pbk = pbuf_of(n)
            S.op("act", CALL("activation", out=pbv[:, t0:2, :].rearrange("p a b -> p (a b)"),
                             in_=sbv[:, t0:2, :].rearrange("p a b -> p (a b)"), func=AF.Exp),
                 reads=(sbk,), writes=(pbk,))

        def att_back(n):
            qb, j, P0, P1, u, types = unit_info(n)
            ob = 4 + (qb % 2)
            db = 6 + (qb % 2)
            dkeys = ["ps%d" % db]
            fs = []
            for n_, t in enumerate(types):
                if t == 0:
                    vl = vcar[:, l, P0:P1] if qb == 0 else vtok[:, qb - 1, P0:P1]
                else:
                    vl = vtok[:, qb, P0:P1]
                fs.append(CALL("matmul", psum[P0:P1, ob, :], vl, pbuf_of(n)[0][:, t, :],
                               start=(n_ == 0), stop=(n_ == len(types) - 1)))
            for n_, t in enumerate(types):
                fs.append(CALL("matmul", psum[P0:P1, db, :], onesb[:, 0:64], pbuf_of(n)[0][:, t, :],
                               start=(n_ == 0), stop=(n_ == len(types) - 1)))
            S.op("pe", seq(fs), reads=(pbuf_of(n)[1], "vtok", "vcar%d" % l, "onesb"), writes=["ps%d" % ob] + dkeys)
            if j == 1:
                r = qb % 2
                S.op("dve", CALL("tensor_tensor", out=rden[:, r, :].rearrange("p (g q) -> p g q", g=4),
                                 in0=psum[:, db, :].rearrange("p (g q) -> p g q", g=4),
                                 in1=bcast(esink[:, l, :], 2, 128), op=ALU.add),
                     reads=dkeys + ["esink"], writes=("rden%d" % r,))
                S.op("act", CALL("activation", out=rden[:, r, :], in_=rden[:, r, :], func=AF.Ln),
                     reads=("rden%d" % r,), writes=("rden%d" % r,))
                S.op("act", CALL("activation", out=rden[:, r, :], in_=rden[:, r, :], func=AF.Exp, scale=-1.0),
                     reads=("rden%d" % r,), writes=("rden%d" % r,))
                S.op("dve", CALL("tensor_tensor", out=mixedT[:, 0:4, qb * 128:(qb + 1) * 128],
                                 in0=psum[:, ob, :].rearrange("p (g q) -> p g q", g=4),
                                 in1=rden[:, r, :].rearrange("p (g q) -> p g q", g=4), op=ALU.mult),
                     reads=("ps%d" % ob, "rden%d" % r), writes=("mx0", "mx1", "mx2", "mx3"))

        LAG = 3
        for n in range(min(LAG, 16)):
            att_front(n)
        for n in range(16):
            if n + LAG < 16:
                att_front(n + LAG)
            att_back(n)
        S.op("act", CALL("activation", out=kcar[:, l, :], in_=kT[:, T - 128:T], func=AF.Identity),
             reads=("kT",), writes=("kcar%d" % l,))
        S.op("act", CALL("activation", out=vcar[:, l, :], in_=vtok[:, 7, :], func=AF.Identity),
             reads=("vtok",), writes=("vcar%d" % l,))

        S.fence(["pooledT", "pooledTs", "qT", "qTs"], XSQ_KEYS)
        wst = {}

        def lhs_of(k, m):
            return wst[m // 2][:, k, (m % 2) * 128:(m % 2 + 1) * 128]

        sgm = segs(ti)
        for m in range(8):
            if m % 2 == 0:
                src = w_out[l][:, m * 128:(m + 2) * 128].rearrange("(k p) n -> p k n", p=128)
                s, _ = wload([(lambda v: v[:, 0:2048].rearrange("p (k n) -> p k n", k=8), src)])
                wst[m // 2] = wsl[:, s, 0:2048].rearrange("p (k n) -> p k n", k=8)
                wst[("s", m // 2)] = s
            pp = m % 2
            fs = []
            for k in range(8):
                lt = lhs_of(k, m)
                for (o, w, nm) in sgm:
                    out = (psum[:, 2 * pp, :] if nm == "h0" else psum[:, 2 * pp + 1, :] if nm == "h1"
                           else psum[:, 6, 256 + m * 16:256 + (m + 1) * 16])
                    fs.append(CALL("matmul",
                        out, lt, mixedT[:, k, o:o + w], start=(k == 0), stop=(k == 7)))
            wr = ["ps%d" % (2 * pp), "ps%d" % (2 * pp + 1)] + (PS6 if ti == 0 else [])
            S.op("pe", seq(fs), reads=["ws%d" % wst[("s", m // 2)]] + (MX_KEYS if ti == 0 else MX_KEYS[:8]), writes=wr)
            residual_ops(ti, l, 1, m, pp)
            if ti != 0:
                sq_op(ti, m, ("xT%d" % m,), alt=True)
                if m >= 1:
                    stats_mm(ti, m - 1, alt=True)
        S.fence(MX_KEYS, H_KEYS)
        if ti != 0:
            stats_mm(ti, 7, alt=True)
        if ti == 0:
            residual_sample(ti, l, 1)
            for m in range(8):
                sq_op(ti, m, ("xT%d" % m, "xTs"), alt=True)
            for m in range(8):
                stats_mm(ti, m, alt=True)
        rstd_ops(ti)

    def sample_mixer(l):
        S.dma("pool", "kvc", CALL("dma_start", out=kc[:, :, :], in_=ck[l].rearrange("b j f -> j b f")),
              writes=("kc",))
        S.dma("pool", "kvc", CALL("dma_start", out=vc[:, :, :], in_=cv[l].rearrange("b j f -> j b f")),
              writes=("vc",))
        S.lastw["kc"] = ("kvc", S.cnt["kvc"])
        S.lastw["vc"] = ("kvc", S.cnt["kvc"])
        S.dma("sp", "outs%d" % l, CALL("dma_start", out=o_ks[l, :, 0:127, :], in_=ck[l, :, 1:128, :]))
        S.dma("sp", "outs%d" % l, CALL("dma_start", out=o_vs[l, :, 0:127, :], in_=cv[l, :, 1:128, :]))
        S.dma("sp", "outs%d" % l, CALL("dma_start", out=o_ps[l, :, 0:14, :], in_=spool[l, :, 1:15, :]))
        S.dma("sp", "outs%d" % l, CALL("dma_start", out=o_ks[l, :, 127, :], in_=ztok[:, 512:640]), reads=("ztok",))
        S.dma("sp", "outs%d" % l, CALL("dma_start", out=o_vs[l, :, 127, :], in_=ztok[:, 640:768]), reads=("ztok",))
        S.dma("sp", "outs%d" % l, CALL("dma_start", out=o_ps[l, :, 14, :], in_=ztok[:, 768:1280]), reads=("ztok",))
        S.fence(BIG_ALL, ["hrows"])
        for a in range(2):
            S.dma("sp", "hist", CALL("dma_start",
                out=hrows[:, a, :], in_=spool[l, a * 8:(a + 1) * 8, :, :].rearrange("b s f -> (b s) f")),
                  writes=("hrows",))
        for g in range(4):
            S.op("pe", seq([(CALL("transpose", psum[:, 7, a * 120:(a + 1) * 120],
                                                               hrows[:, a, g * 128:(g + 1) * 128], ident[0:120, 0:120]))
                            for a in range(2)]), reads=("hrows", "ident"), writes=("ps7",))
            S.op("dve", CALL("tensor_copy", out=uhist[:, g, :], in_=psum[:, 7, 0:240]),
                 reads=("ps7",), writes=("uhist",))
        for g in range(4):
            wdw = POOL_W[g]
            hv = uhist[:, g, :].rearrange("p (b s) -> p b s", s=15)[:, :, 15 - (wdw - 1):15]
            S.op("dve", CALL("tensor_reduce", out=smt[:, 2, 0, :], in_=hv, axis=AX.X, op=ALU.add),
                 reads=("uhist",), writes=("smt",))
            S.op("dve", CALL("tensor_tensor", out=smt[:, 2, 1, :], in0=smt[:, 2, 0, :], in1=zsT[:, 6 + g, :],
                                                       op=ALU.add), reads=("smt", "zsT"), writes=("smt",))
            S.op("dve", CALL("scalar_tensor_tensor",
                out=pooledT[:, g, T:TS], in0=smt[:, 2, 1, :], scalar=1.0 / wdw, in1=zsT[:, 6 + g, :],
                op0=ALU.mult, op1=ALU.subtract), reads=("smt", "zsT"), writes=("pooledTs",))
        S.op("dve", CALL("tensor_copy", out=qtokb[:, :], in_=ztok[:, 0:512]), reads=("ztok",), writes=("qtokb",))
        for b in range(NS):
            pb = b % 2
            S.op("pe", CALL("matmul", psum[:, pb, :], selb[:, b, :], qtokb[:, :], start=True, stop=True),
                 reads=("selb", "qtokb"), writes=("ps%d" % pb,))
            S.op("dve", CALL("tensor_tensor",
                out=prod[:, :].rearrange("p (g hh d) -> p g hh d", g=4, hh=2),
                in0=psum[:, pb, :].rearrange("p (g hh d) -> p g hh d", g=4, hh=2),
                in1=bcast(kc[:, b, :].rearrange("p (hh d) -> p hh d", hh=2), 1, 4), op=ALU.mult),
                 reads=("ps%d" % pb, "kc"), writes=("tmpf0",))
            S.op("dve", CALL("tensor_reduce", out=s_s[:, b, :], in_=prod[:, :].rearrange("p (h d) -> p h d", d=64),
                                                       axis=AX.X, op=ALU.add), reads=("tmpf0",), writes=("s_s",))
        S.op("dve", CALL("scalar_tensor_tensor", out=s_s[:, :, :], in0=s_s[:, :, :], scalar=0.125,
                                                     in1=bcast(bias_s[:, :], 1, NS), op0=ALU.mult, op1=ALU.add),
             reads=("s_s", "bias_s"), writes=("s_s",))
        S.op("act", CALL("activation", out=pT_s[:, :, :], in_=s_s[:, :, :], func=AF.Exp),
             reads=("s_s",), writes=("pT_s",))
        S.op("pe", seq([(CALL("matmul", psum[:, 2, b * 8:(b + 1) * 8], vc[:, b, :], pT_s[:, b, :],
                                                  start=True, stop=True)) for b in range(NS)]
                       + [CALL("matmul", psum[:, 3, 0:128], onesb[:, :], pT_s[:, :, :].rearrange("p b h -> p (b h)"),
                                             start=True, stop=True)]),
             reads=("vc", "pT_s", "onesb"), writes=("ps2", "ps3"))
        S.op("dve", CALL("tensor_tensor", out=smt[:, 3, :, :], in0=zsT[:, 0:4, :], in1=bcast(zsT[:, 4, :], 1, 4),
                                              op=ALU.mult), reads=("zsT",), writes=("smt3",))
        S.op("pe", CALL("matmul", psum[:, 7, 0:64], blk1[:, :], smt[:, 3, :, :].rearrange("p g b -> p (g b)"),
                                      start=True, stop=True), reads=("blk1", "smt3"), writes=("ps7",))
        for g in range(4):
            S.op("act", CALL("activation", out=smt[:, 4, g, :], in_=psum[:, 7, g * 16:(g + 1) * 16], func=AF.Exp,
                                                    scale=0.125, bias=bias0[:, g:g + 1]),
                 reads=("ps7", "bias0"), writes=("smt4",))
        for hh in range(2):
            P = slice(hh * 64, (hh + 1) * 64)
            oc_sel = psum[P, 2, 0:128].rearrange("p (b g hh) -> p b g hh", b=NS, g=4)[:, :, :, hh].rearrange("p b g -> p g b")
            dc_sel = psum[P, 3, 0:128].rearrange("p (b g hh) -> p b g hh", b=NS, g=4)[:, :, :, hh].rearrange("p b g -> p g b")
            S.op("dve", CALL("tensor_tensor", out=smt[P, 5, :, :], in0=smt[P, 4, :, :],
                                                       in1=bcast(zsT[P, 5, :], 1, 4), op=ALU.mult),
                 reads=("smt4", "zsT"), writes=("smt5",))
            S.op("dve", CALL("tensor_tensor", out=smt[P, 5, :, :], in0=smt[P, 5, :, :],
                                                                      in1=oc_sel, op=ALU.add),
                 reads=("smt5", "ps2"), writes=("smt5",))
            S.op("dve", CALL("tensor_tensor", out=smt[P, 3, :, :], in0=smt[P, 4, :, :],
                                                                      in1=dc_sel, op=ALU.add),
                 reads=("smt4", "ps3", "smt3"), writes=("smt3",))
            S.op("dve", CALL("tensor_tensor", out=smt[P, 3, :, :], in0=smt[P, 3, :, :],
                                                       in1=bcast(esink[P, l, :], 2, NS), op=ALU.add),
                 reads=("smt3", "esink"), writes=("smt3",))
            S.op("dve", CALL("reciprocal", out=smt[P, 3, :, :], in_=smt[P, 3, :, :]),
                 reads=("smt3",), writes=("smt3",))
            S.fence(H_KEYS, ["mxs"])
            S.op("dve", CALL("tensor_tensor", out=mixedT[P, 0:4, T:TS], in0=smt[P, 5, :, :],
                                                       in1=smt[P, 3, :, :], op=ALU.mult),
                 reads=("smt5", "smt3"), writes=("mxs",))

    phases = []
    for ti in range(NT):
        phases.append(lambda ti=ti: load_x(ti))
        for l in range(DEPTH):
            phases.append(lambda ti=ti, l=l: ffn(ti, l, 0, 0))
            phases.append(lambda ti=ti, l=l: mixer(ti, l))
            phases.append(lambda ti=ti, l=l: ffn(ti, l, 1, 2))
        phases.append(lambda ti=ti: store_y(ti))
    def flush_ada():
        while pending:
            pop_pending()
        ada_finish(1)

    phases.insert(4, flush_ada)
    for ph in phases[:DEBUG_PHASES]:
        ph()

    fin = [(s, v) for s, v in S.cnt.items() if s.startswith("out")]
    S.wait_all("sp", fin)

    sems = {}
    for name in sorted(S.semnames):
        sems[name] = es.enter_context(nc.semaphore("s_" + name))
    block = es.enter_context(nc.Block())

    def replay(e, items):
        for it in items:
            if it[0] == "wait":
                e.wait_ge(sems[it[1]], it[2])
            else:
                ins = it[1](e)
                ins.then_inc(sems[it[2]], it[3])

    @block.tensor
    def _(e):
        replay(e, S.q["pe"])

    @block.scalar
    def _(e):
        replay(e, S.q["act"])

    @block.vector
    def _(e):
        replay(e, S.q["dve"])

    @block.gpsimd
    def _(e):
        replay(e, S.q["pool"])

    @block.sync
    def _(e):
        replay(e, S.q["sp"])

    es.close()
    return nc


_CACHE = {}


def _consts():
    c = {}
    c["c_ident"] = np.eye(128, dtype=np.float32)
    c["c_J"] = np.ascontiguousarray(np.eye(128, dtype=np.float32)[::-1])
    k = np.arange(128)[:, None]
    q = np.arange(128)[None, :]
    neg = np.zeros((128, 2, 128), np.float32)
    neg[:, 0, :] = np.where(q <= k, 0.0, NEG)
    neg[:, 1, :] = np.where(q >= k, 0.0, NEG)
    c["c_neg"] = neg
    invc = np.zeros((128, 4, 16), np.float32)
    for g, w in enumerate(POOL_W):
        invc[:, g, :] = 1.0 / np.minimum(np.arange(16) + 1, w).astype(np.float32)
    c["c_invc"] = invc
    oh = np.zeros((32, 640), np.float32)
    for e in range(255):
        d_prev = e + 1
        if d_prev <= 128:
            oh[int(t5_bucket_np(np.array(d_prev))), e] = 1.0
        d_own = e - 127
        if d_own >= 0:
            oh[int(t5_bucket_np(np.array(d_own))), 256 + e] = 1.0
    for j in range(128):
        oh[int(t5_bucket_np(np.array(128 - j))), 512 + j] = 1.0
    c["c_oh"] = oh
    sel = np.zeros((16, 16, 128), np.float32)
    for b in range(16):
        sel[b, b, :] = 1.0
    c["c_sel"] = sel.reshape(16, 2048)
    return c


def kernel(x_prompt, x_sample, c_prompt, c_sample, cache_k, cache_v, state_pool,
           w_ada, b_ada, norm_gain, w_in, w_out, sinks, rel_bias, pool_w, pool_scale,
           ffn1_wg, ffn1_wu, ffn1_wd, ffn2_wg, ffn2_wu, ffn2_wd, final_gain):
    f = lambda a: np.ascontiguousarray(np.asarray(a, dtype=np.float32))
    if "nc" not in _CACHE:
        _CACHE["nc"] = build_program()
    nc = _CACHE["nc"]
    qperm = np.concatenate([np.concatenate([np.arange(g * 64, g * 64 + 64), np.arange((4 + g) * 64, (4 + g) * 64 + 64)])
                            for g in range(4)])
    w_in_p = f(w_in).copy()
    w_in_p[:, :, 0:512] = f(w_in)[:, :, qperm]
    w_out_p = f(w_out).copy()
    w_out_p[:, 0:512, :] = f(w_out)[:, qperm, :]
    shared = {
        "w_ada": f(w_ada), "b_ada": f(b_ada).reshape(DEPTH, 72, 128), "ngain": f(norm_gain).reshape(DEPTH, 24, 128),
        "w_in": w_in_p, "w_out": w_out_p, "sinks": f(sinks), "relb": f(rel_bias), "pool_w": f(pool_w),
        "pscale": f(pool_scale).reshape(DEPTH, 4, 128),
        "f1g": f(ffn1_wg), "f1u": f(ffn1_wu), "f1d": f(ffn1_wd), "f2g": f(ffn2_wg), "f2u": f(ffn2_wu), "f2d": f(ffn2_wd),
        "fgain": f(final_gain).reshape(8, 128),
    }
    shared.update(_consts())
    xp_, xs_, cp_, cs_ = f(x_prompt), f(x_sample), f(c_prompt), f(c_sample)
    ck_, cv_, sp_ = f(cache_k), f(cache_v), f(state_pool)
    in_maps = []
    for i in range(NCORES):
        m = dict(shared)
        sl = slice(i * NS, (i + 1) * NS)
        m["xp"] = xp_[i]
        m["xs"] = xs_[sl, 0, :]
        m["cp"] = cp_[i].reshape(8, 128)
        m["cs"] = cs_[sl]
        m["ck"] = np.ascontiguousarray(ck_[:, sl].reshape(DEPTH, NS, 128, 128))
        m["cv"] = np.ascontiguousarray(cv_[:, sl].reshape(DEPTH, NS, 128, 128))
        m["spool"] = np.ascontiguousarray(sp_[:, sl])
        in_maps.append(m)
    res = run_bass_kernel_spmd(nc, in_maps, core_ids=list(range(NCORES)))
    R = res.results
    y_prompt = np.stack([R[i]["y_p"] for i in range(NCORES)], axis=0)
    y_sample = np.concatenate([R[i]["y_s"] for i in range(NCORES)], axis=0).reshape(NCORES * NS, 1, D)
    nkp = np.stack([R[i]["o_kp"] for i in range(NCORES)], axis=1).reshape(DEPTH, NCORES, 128, 2, 64)
    nvp = np.stack([R[i]["o_vp"] for i in range(NCORES)], axis=1).reshape(DEPTH, NCORES, 128, 2, 64)
    npp = np.stack([R[i]["o_pp"] for i in range(NCORES)], axis=1)
    nks = np.concatenate([R[i]["o_ks"] for i in range(NCORES)], axis=1).reshape(DEPTH, NCORES * NS, 128, 2, 64)
    nvs = np.concatenate([R[i]["o_vs"] for i in range(NCORES)], axis=1).reshape(DEPTH, NCORES * NS, 128, 2, 64)
    nps = np.concatenate([R[i]["o_ps"] for i in range(NCORES)], axis=1)
    return (y_prompt.astype(np.float32), y_sample.astype(np.float32), nkp.astype(np.float32), nvp.astype(np.float32),
            npp.astype(np.float32), nks.astype(np.float32), nvs.astype(np.float32), nps.astype(np.float32))
```

```python
import math
from contextlib import ExitStack

import numpy as np
import concourse.bass as bass
import concourse.mybir as mybir
from concourse.bass_utils import run_bass_kernel_spmd

F32 = mybir.dt.float32
BF16 = mybir.dt.bfloat16
AF = mybir.ActivationFunctionType
ALU = mybir.AluOpType
AX = mybir.AxisListType

NCORES = 8
D = 1024
SEQ = 4096
T = 1024
NT = SEQ // T
NS = 16
TS = T + NS
FF = 2816
NFF = FF // 128
DEPTH = 2
EPS = 1e-6
NEG = -30000.0
WSLOT = 4096
NSLOT = 3
POOL_W = (2, 4, 8, 16)
DEBUG_PHASES = 10 ** 6
DEBUG_MIX = 10 ** 6
DEBUG_SMIX = 10 ** 6


def t5_bucket_np(dist):
    n = np.maximum(dist, 0)
    nf = np.maximum(n, 1).astype(np.float32)
    large = 16 + (np.log(nf / np.float32(16)) / np.float32(math.log(128 / 16)) * np.float32(16)).astype(np.int32)
    large = np.minimum(large, 31)
    return np.where(n < 16, n, large)


class Sched:
    ENGS = ("pe", "act", "dve", "pool", "sp")

    def __init__(self):
        self.q = {e: [] for e in self.ENGS}
        self.cnt = {}
        self.waited = {e: {} for e in self.ENGS}
        self.lastw = {}
        self.readers = {}
        self.semnames = set(self.ENGS)

    def _deps(self, reads, writes):
        toks = []
        for k in reads:
            if k in self.lastw:
                toks.append(self.lastw[k])
        for k in writes:
            if k in self.lastw:
                toks.append(self.lastw[k])
            toks.extend(self.readers.get(k, {}).items())
        return toks

    def _waits(self, eng, toks):
        best = {}
        for s, v in toks:
            if v > best.get(s, 0):
                best[s] = v
        for s, v in best.items():
            if s == "pe" and eng == "pe":
                continue
            if s not in self.ENGS:
                v = self.cnt[s]
            if self.waited[eng].get(s, 0) >= v:
                continue
            self.waited[eng][s] = v
            self.q[eng].append(("wait", s, v))

    def _record(self, tok, reads, writes):
        for k in reads:
            d = self.readers.setdefault(k, {})
            if tok[1] > d.get(tok[0], 0):
                d[tok[0]] = tok[1]
        for k in writes:
            self.lastw[k] = tok
            self.readers[k] = {}

    def op(self, eng, fn, reads=(), writes=(), extra=()):
        toks = self._deps(reads, writes) + list(extra)
        self._waits(eng, toks)
        self.cnt[eng] = self.cnt.get(eng, 0) + 1
        tok = (eng, self.cnt[eng])
        self.q[eng].append(("inst", fn, eng, 1))
        self._record(tok, reads, writes)
        return tok

    def dma(self, queue, sem, fn, reads=(), writes=(), extra=()):
        self.semnames.add(sem)
        toks = self._deps(reads, writes) + list(extra)
        self._waits(queue, toks)
        self.cnt[sem] = self.cnt.get(sem, 0) + 16
        tok = (sem, self.cnt[sem])
        self.q[queue].append(("inst", fn, sem, 16))
        self._record(tok, reads, writes)
        return tok

    def fence(self, src, dst):
        best = {}
        for k in src:
            if k in self.lastw:
                s_, v_ = self.lastw[k]
                if v_ > best.get(s_, 0):
                    best[s_] = v_
            for s_, v_ in self.readers.get(k, {}).items():
                if v_ > best.get(s_, 0):
                    best[s_] = v_
        for k in dst:
            d = self.readers.setdefault(k, {})
            for s_, v_ in best.items():
                if v_ > d.get(s_, 0):
                    d[s_] = v_

    def wait_all(self, eng, toks):
        self._waits(eng, toks)


def CALL(name, *args, **kw):
    def run(e):
        return getattr(e, name)(*args, **kw)
    return run


def seq(fs):
    def run(e):
        last = None
        for f in fs:
            last = f(e)
        return last
    return run


def bcast(ap, pos, count):
    dims = [list(d) for d in ap.ap]
    dims.insert(pos, [0, count])
    return bass.AP(ap.tensor, ap.offset, dims)


def build_program():
    nc = bass.Bass("TRN2", target_bir_lowering=False)
    S = Sched()

    def din(name, shape):
        return nc.dram_tensor(name, list(shape), F32, kind="ExternalInput").ap()

    def dout(name, shape):
        return nc.dram_tensor(name, list(shape), F32, kind="ExternalOutput").ap()

    xp = din("xp", [SEQ, D])
    xs = din("xs", [NS, D])
    cp = din("cp", [8, 128])
    cs = din("cs", [NS, D])
    ck = din("ck", [DEPTH, NS, 128, 128])
    cv = din("cv", [DEPTH, NS, 128, 128])
    spool = din("spool", [DEPTH, NS, 15, 512])
    w_ada = din("w_ada", [DEPTH, D, 9 * D])
    b_ada = din("b_ada", [DEPTH, 72, 128])
    ngain = din("ngain", [DEPTH, 24, 128])
    w_in = din("w_in", [DEPTH, D, 1280])
    w_out = din("w_out", [DEPTH, D, D])
    sinks = din("sinks", [DEPTH, 8])
    relb = din("relb", [32, 8])
    pool_w = din("pool_w", [DEPTH, 4, 128, 128])
    pscale = din("pscale", [DEPTH, 4, 128])
    wg = [din("f1g", [DEPTH, D, FF]), din("f2g", [DEPTH, D, FF])]
    wu = [din("f1u", [DEPTH, D, FF]), din("f2u", [DEPTH, D, FF])]
    wd = [din("f1d", [DEPTH, FF, D]), din("f2d", [DEPTH, FF, D])]
    fgain = din("fgain", [8, 128])
    c_ident = din("c_ident", [128, 128])
    c_J = din("c_J", [128, 128])
    c_neg = din("c_neg", [128, 2, 128])
    c_invc = din("c_invc", [128, 4, 16])
    c_oh = din("c_oh", [32, 640])
    c_sel = din("c_sel", [16, 16 * 128])

    y_p = dout("y_p", [SEQ, D])
    y_s = dout("y_s", [NS, D])
    o_kp = dout("o_kp", [DEPTH, 128, 128])
    o_vp = dout("o_vp", [DEPTH, 128, 128])
    o_pp = dout("o_pp", [DEPTH, 15, 512])
    o_ks = dout("o_ks", [DEPTH, NS, 128, 128])
    o_vs = dout("o_vs", [DEPTH, NS, 128, 128])
    o_ps = dout("o_ps", [DEPTH, NS, 15, 512])
    scr = nc.dram_tensor("scr", [2, 8, 256], F32, kind="Internal").ap()

    es = ExitStack()

    def sb(name, shape, dt):
        return es.enter_context(nc.sbuf_tensor(name, list(shape), dt))

    xT = sb("xT", [128, 8, TS], F32)
    hT = sb("hT", [128, 8, TS], BF16)
    big = sb("big", [128, 23040], BF16)
    tmpf = sb("tmpf", [128, 2, 1056], F32)
    rstd = sb("rstd", [128, TS], F32)
    sil = sb("sil", [128, 2, TS], BF16)
    kT = sb("kT", [128, T + NS], BF16)
    kTf = sb("kTf", [128, 128], F32)
    vtok = sb("vtok", [128, 8, 128], BF16)
    vtokf = sb("vtokf", [128, 128], F32)
    kcar = sb("kcar", [128, DEPTH, 128], BF16)
    vcar = sb("vcar", [128, DEPTH, 128], BF16)
    ucar = sb("ucar", [128, DEPTH, 4, 16], F32)
    pT = sb("pT", [128, 2, 2, 512], BF16)
    biasTh = sb("biasTh", [128, 2, 8, 128], BF16)
    biasTl = sb("biasTl", [128, 2, 8, 128], BF16)
    identb = sb("identb", [128, 128], BF16)
    rden = sb("rden", [128, 2, 512], F32)
    wsl = sb("wsl", [128, NSLOT, WSLOT], BF16)
    modT = sb("modT", [128, DEPTH, 72, 17], F32)
    gmul = sb("gmul", [128, DEPTH, 3, 8, 17], F32)
    gateT = sb("gateT", [128, DEPTH, 3, 8, 17], F32)
    colsT = sb("colsT", [128, DEPTH, 116], F32)
    rows = sb("rows", [128, DEPTH, 128], F32)
    cT = sb("cT", [128, 8, 17], F32)
    scT = sb("scT", [128, 8, 17], BF16)
    ident = sb("ident", [128, 128], F32)
    Jm = sb("Jm", [128, 128], F32)
    onesb = sb("onesb", [128, 128], BF16)
    blk1 = sb("blk1", [128, 128], F32)
    negm = sb("negm", [128, 2, 128], F32)
    invc = sb("invc", [128, 4, 16], F32)
    oh = sb("oh", [32, 640], F32)
    rb = sb("rb", [32, 8], F32)
    selb = sb("selb", [16, 16, 128], BF16)
    esink = sb("esink", [128, DEPTH, 4], F32)
    bias0 = sb("bias0", [128, 4], F32)
    bias_s = sb("bias_s", [128, 8], F32)
    epst = sb("epst", [128, 1], F32)
    kc = sb("kc", [128, NS, 128], BF16)
    vc = sb("vc", [128, NS, 128], BF16)
    uhist = sb("uhist", [128, 4, NS * 15], F32)
    zsT = sb("zsT", [128, 10, NS], F32)
    ztok = sb("ztok", [NS, 1280], F32)
    qtokb = sb("qtokb", [NS, 512], BF16)
    s_s = sb("s_s", [128, NS, 8], F32)
    pT_s = sb("pT_s", [128, NS, 8], BF16)
    smt = sb("smt", [128, 6, 4, NS], F32)

    psum = es.enter_context(nc.psum_tensor("psum", [128, 8, 512], F32))

    bigf = big.bitcast(F32)
    aT = big[:, 0:NFF * TS].rearrange("p (c t) -> p c t", c=NFF)
    xin = bigf[:, 0:8 * T].rearrange("p (b f) -> p b f", b=8)
    uT = bigf[:, 0:4 * 1056].rearrange("p (g t) -> p g t", g=4)
    pooledT = big[:, 8448:8448 + 4 * TS].rearrange("p (g t) -> p g t", g=4)
    sbf = bigf[:, 6304:6304 + 2048].rearrange("p (a b c) -> p a b c", a=2, b=2)
    qT = big[:, 16704:16704 + 4 * TS].rearrange("p (g t) -> p g t", g=4)
    mixedT = hT
    Hk = bigf[:, 0:2048].rearrange("p (t h q) -> p t h q", t=2, h=8)
    xsr = bigf[0:NS, 2048:3072]
    csr = bigf[0:NS, 3072:4096]
    hrows = bigf[0:120, 10432:10432 + 1024].rearrange("p (a f) -> p a f", a=2)
    utok_rows = bigf[0:NS, 10432:10432 + 512]
    self32 = bigf[0:16, 4096:6144]
    bvs = bigf[0:8, 6144:6656].rearrange("p (t e) -> p t e", t=2)
    yrow_s = bigf[0:NS, 8192:9216]
    biasT = bigf[:, 7168:9216].rearrange("p (t h q) -> p t h q", t=2, h=8)
    prod = tmpf[:, 0, 0:512]

    AT_KEYS = ["aT%d" % j for j in range(NFF)] + ["aTs"]
    XSQ_KEYS = ["xsq%d" % c for c in range(8)] + ["xsqs%d" % c for c in range(8)]
    MIX_KEYS = ["uT", "uTs", "pooledT", "pooledTs", "sbf0", "sbf1", "qT", "qTs", "hrows", "utok_rows"] + XSQ_KEYS
    XIN_KEYS = ["xin%d" % b for b in range(8)] + ["yrow_s"]
    SETUP_KEYS = ["Hk", "xsr", "csr", "self32", "bvs", "biasT"]
    H_KEYS = ["hT%d" % c for c in range(8)] + ["hTs"]
    BIG_ALL = AT_KEYS + MIX_KEYS + XIN_KEYS + SETUP_KEYS
    MX_KEYS = ["mx%d" % c for c in range(8)] + ["mxs"]

    def pbank(b):
        return psum[:, b, :]

    def ppair(p):
        return psum[:, 2 * p:2 * p + 2, :].rearrange("p a b -> p (a b)")

    PS6 = ["ps6"]

    def SL(n):
        bank = 6 + (n % 2)
        col = ((n // 2) % 8) * 32
        return psum[:, bank, col:col + NS], "ps%d" % bank

    def ps6(i, n=1):
        return psum[:, 6, i * 32:(i + n) * 32]

    wstate = {"n": 0}

    def wload(parts):
        s = wstate["n"] % NSLOT
        wstate["n"] += 1
        tok = None
        for dst, src in parts:
            d = dst(wsl[:, s, :])
            tok = S.dma("pool", "w%d" % s, (CALL("dma_start", out=d, in_=src)),
                        reads=(), writes=("ws%d" % s,))
        return s, tok

    def ld(dst, src, key, sem="cst", queue="sp"):
        return S.dma(queue, sem, (CALL("dma_start", out=dst, in_=src)), writes=(key,))

    ld(ident[:], c_ident, "ident")
    ld(Jm[:], c_J, "Jm")
    ld(negm[:], c_neg, "negm")
    ld(invc[:], c_invc, "invc")
    ld(oh[:], c_oh, "oh")
    ld(rb[:], relb, "rb")
    ld(self32[:], c_sel, "self32")
    for l in range(DEPTH):
        ld(rows[0:72, l, :], b_ada[l], "rows")
        ld(rows[72:96, l, :], ngain[l], "rows")
        ld(rows[96:100, l, :], pscale[l], "rows")
        ld(rows[100:108, l, :], fgain, "rows")
        ld(rows[108:116, l, :], cp, "rows")
    ld(xsr, xs, "xsr")
    ld(csr, cs, "csr")
    for l in range(DEPTH):
        for hh in range(2):
            src = bass.AP(sinks.tensor, l * 8 + hh * 4, [[0, 64], [1, 4]])
            ld(esink[hh * 64:(hh + 1) * 64, l, :], src, "esink")
    for hh in range(2):
        src = bass.AP(relb.tensor, hh * 4, [[0, 64], [1, 4]])
        ld(bias0[hh * 64:(hh + 1) * 64, :], src, "bias0")

    for key in ("ident", "Jm", "negm", "invc", "oh", "rb", "self32", "rows", "xsr", "csr", "esink", "bias0"):
        S.lastw[key] = ("cst", S.cnt["cst"])
    S.op("dve", CALL("memset", onesb[:], 1.0), writes=("onesb",))
    S.op("dve", CALL("memset", blk1[:], 0.0), writes=("blk1",))
    S.op("dve", CALL("memset", blk1[0:64, 0:64], 1.0), writes=("blk1",))
    S.op("dve", CALL("memset", blk1[64:128, 64:128], 1.0), writes=("blk1",))
    S.op("dve", CALL("memset", epst[:], EPS), writes=("epst",))
    S.op("dve", CALL("memset", kcar[:], 0.0), writes=("kcar0", "kcar1"))
    S.op("dve", CALL("memset", vcar[:], 0.0), writes=("vcar0", "vcar1"))
    S.op("dve", CALL("memset", ucar[:], 0.0), writes=("ucar0", "ucar1"))
    S.op("dve", CALL("tensor_copy", out=selb[:].rearrange("p a b -> p (a b)"), in_=self32[:]),
         reads=("self32",), writes=("selb",))
    S.op("act", CALL("activation", out=esink[:], in_=esink[:], func=AF.Exp), reads=("esink",), writes=("esink",))

    for l in range(DEPTH):
        S.op("pe", CALL("transpose", psum[:, 7, 0:116], rows[0:116, l, :], ident[0:116, 0:116]),
             reads=("rows", "ident"), writes=("ps7",))
        S.op("dve", CALL("tensor_copy", out=colsT[:, l, :], in_=psum[:, 7, 0:116]),
             reads=("ps7",), writes=("colsT",))
    S.op("pe", seq([(CALL("transpose", psum[:, 7, c * 16:(c + 1) * 16], csr[:, c * 128:(c + 1) * 128],
                                                  ident[0:NS, 0:NS])) for c in range(8)]),
         reads=("csr", "ident"), writes=("ps7",))
    S.op("dve", CALL("tensor_copy", out=cT[:, :, 1:17],
                                        in_=psum[:, 7, 0:128].rearrange("p (c b) -> p c b", c=8)),
         reads=("ps7",), writes=("cTs",))
    S.op("dve", CALL("tensor_copy", out=cT[:, :, 0], in_=colsT[:, 0, 108:116]), reads=("colsT",), writes=("cTp",))
    S.op("pe", seq([(CALL("transpose", psum[:, 7, c * 16:(c + 1) * 16], xsr[:, c * 128:(c + 1) * 128],
                                                  ident[0:NS, 0:NS])) for c in range(8)]),
         reads=("xsr", "ident"), writes=("ps7",))
    S.op("dve", CALL("tensor_copy", out=xT[:, :, T:TS],
                                        in_=psum[:, 7, 0:128].rearrange("p (c b) -> p c b", c=8)),
         reads=("ps7",), writes=("xTs",))
    S.op("act", CALL("activation", out=scT[:], in_=cT[:], func=AF.Silu), reads=("cTs", "cTp"), writes=("scT",))

    def ada_stage(l, st, bank):
        src = w_ada[l][:, st * 512:(st + 1) * 512].rearrange("(k p) n -> p k n", p=128)
        s, _ = wload([(lambda v: v.rearrange("p (k n) -> p k n", k=8), src)])
        wv = wsl[:, s, :].rearrange("p (k n) -> p k n", k=8)
        fs = []
        for oc in range(4):
            for k in range(8):
                fs.append(CALL("matmul", psum[:, bank, oc * 17:(oc + 1) * 17], wv[:, k, oc * 128:(oc + 1) * 128],
                               scT[:, k, :], start=(k == 0), stop=(k == 7)))
        S.op("pe", seq(fs), reads=("ws%d" % s, "scT"), writes=("ps%d" % bank,))
        S.op("dve", CALL("tensor_tensor", out=modT[:, l, st * 4:(st + 1) * 4, :],
                         in0=psum[:, bank, 0:68].rearrange("p (a b) -> p a b", a=4),
                         in1=bcast(colsT[:, l, st * 4:(st + 1) * 4], 2, 17), op=ALU.add),
             reads=("ps%d" % bank, "colsT"), writes=("modT%d" % l,))

    def ada_finish(l):
        for i in range(3):
            S.op("dve", CALL("scalar_tensor_tensor", out=gmul[:, l, i, :, :],
                             in0=modT[:, l, (3 * i + 1) * 8:(3 * i + 2) * 8, :], scalar=1.0,
                             in1=bcast(colsT[:, l, 72 + i * 8:72 + (i + 1) * 8], 2, 17), op0=ALU.add, op1=ALU.mult),
                 reads=("modT%d" % l, "colsT"), writes=("gmul%d" % l,))
            S.op("dve", CALL("tensor_scalar", out=gateT[:, l, i, :, :],
                             in0=modT[:, l, (3 * i + 2) * 8:(3 * i + 3) * 8, :],
                             scalar1=(1.0 if i == 1 else 0.5), scalar2=None, op0=ALU.mult),
                 reads=("modT%d" % l,), writes=("gateT%d" % l,))

    for st in range(18):
        ada_stage(0, st, st % 6)
    ada_finish(0)
    pending = [(lambda st=st: ada_stage(1, st, 7)) for st in range(18)]

    def pop_pending():
        if pending:
            pending.pop(0)()

    def shiftp(l, i, c):
        return modT[:, l, 3 * i * 8 + c, 0:1]

    def shifts(l, i):
        return modT[:, l, 3 * i * 8:3 * i * 8 + 8, 1:17]

    for t in range(2):
        S.op("pe", CALL("matmul", psum[0:8, 7, 0:256], rb[:, :], oh[:, t * 256:(t + 1) * 256],
                                           start=True, stop=True), reads=("rb", "oh"), writes=("ps7",))
        S.op("dve", CALL("tensor_copy", out=bvs[:, t, :], in_=psum[0:8, 7, 0:256]),
             reads=("ps7",), writes=("bvs",))
    S.dma("sp", "scrw", CALL("dma_start", out=scr.rearrange("t h e -> h t e"), in_=bvs[:]),
          reads=("bvs",), writes=("scr",))
    S.fence(SETUP_KEYS, ["Hk"])
    for t in range(2):
        src = bass.AP(scr.tensor, t * 2048, [[1, 128], [256, 8], [1, 128]])
        S.dma("sp", "scrr", CALL("dma_start", out=Hk[:, t, :, :], in_=src),
              reads=("scr",), writes=("Hk",))
    for t in range(2):
        hv = Hk[:, t, :, :].rearrange("p h q -> p (h q)")
        S.op("pe", seq([(CALL("matmul", psum[:, hf, :], Jm[:, :], hv[:, hf * 512:(hf + 1) * 512],
                                                        start=True, stop=True)) for hf in range(2)]),
             reads=("Hk", "Jm"), writes=("ps0", "ps1"))
        S.op("dve", CALL("tensor_tensor",
            out=biasT[:, t, :, :], in0=ppair(0).rearrange("p (h q) -> p h q", h=8),
            in1=bcast(negm[:, t, :], 1, 8), op=ALU.add), reads=("ps0", "ps1", "negm"), writes=("biasT",))
    S.op("pe", CALL("matmul", psum[:, 7, 0:8].rearrange("p (g hh) -> p g hh", g=4), oh[:, 512:640],
                                  rb[:, :].rearrange("b (hh g) -> b g hh", hh=2), start=True, stop=True),
         reads=("rb", "oh"), writes=("ps7",))
    S.op("dve", CALL("tensor_copy", out=bias_s[:], in_=psum[:, 7, 0:8]), reads=("ps7",), writes=("bias_s",))
    S.op("dve", CALL("tensor_copy", out=biasTh[:], in_=biasT), reads=("biasT",), writes=("biasTh",))
    S.op("dve", CALL("tensor_tensor", out=biasT, in0=biasT, in1=biasTh[:], op=ALU.subtract),
         reads=("biasT", "biasTh"), writes=("biasT",))
    S.op("dve", CALL("tensor_copy", out=biasTl[:], in_=biasT), reads=("biasT",), writes=("biasTl",))
    S.op("dve", CALL("tensor_copy", out=identb[:], in_=ident[:]), reads=("ident",), writes=("identb",))

    def segs(ti):
        sg = [(0, 512, "h0"), (512, 512, "h1")]
        if ti == 0:
            sg.append((T, NS, "s"))
        return sg

    def sq_buf(c, alt):
        if not alt:
            return hT[:, c, :], ("hT%d" % c,), ("hTs",)
        if c < 4:
            return pooledT[:, c, :], ("xsq%d" % c,), ("xsqs%d" % c,)
        return qT[:, c - 4, :], ("xsq%d" % c,), ("xsqs%d" % c,)

    def sq_op(ti, c, xkeys, alt=False):
        n = TS if ti == 0 else T
        buf, k1, k2 = sq_buf(c, alt)
        S.op("dve", CALL("tensor_tensor", out=buf[:, 0:n], in0=xT[:, c, 0:n], in1=xT[:, c, 0:n], op=ALU.mult),
             reads=xkeys, writes=k1 + (k2 if ti == 0 else ()))

    def stats_mm(ti, c, alt=False):
        buf, k1, k2 = sq_buf(c, alt)
        fs = []
        for (o, w, nm) in segs(ti):
            out = psum[:, 4, :] if nm == "h0" else (psum[:, 5, :] if nm == "h1" else psum[:, 7, 0:NS])
            fs.append(CALL("matmul", out, onesb[:, :], buf[:, o:o + w], start=(c == 0), stop=(c == 7)))
        wr = ("ps4", "ps5") + (("ps7",) if ti == 0 else ())
        S.op("pe", seq(fs), reads=k1 + ("onesb",) + (k2 if ti == 0 else ()), writes=wr)

    def norm_stats_ops(ti, c, xkeys):
        sq_op(ti, c, xkeys)
        stats_mm(ti, c)

    def rstd_ops(ti):
        n = TS if ti == 0 else T
        S.op("act", CALL("activation", out=rstd[:, 0:T], in_=ppair(2), func=AF.Ln, bias=epst[:, 0:1],
                                           scale=1.0 / D), reads=("ps4", "ps5", "epst"), writes=("rstd",))
        if ti == 0:
            S.op("act", CALL("activation", out=rstd[:, T:TS], in_=psum[:, 7, 0:NS], func=AF.Ln,
                                               bias=epst[:, 0:1], scale=1.0 / D),
                 reads=("ps7", "epst"), writes=("rstds",))
        S.op("act", CALL("activation", out=rstd[:, 0:n], in_=rstd[:, 0:n], func=AF.Exp, scale=-0.5),
             reads=("rstd", "rstds"), writes=("rstd", "rstds"))

    def modulate_ops(ti, l, i):
        for c in range(8):
            pb = c % 2
            S.op("dve", CALL("tensor_tensor", out=tmpf[:, pb, 0:T], in0=xT[:, c, 0:T],
                                                              in1=rstd[:, 0:T], op=ALU.mult),
                 reads=("xT%d" % c, "rstd"), writes=("tmpf%d" % pb,))
            S.op("act", CALL("activation", out=hT[:, c, 0:T], in_=tmpf[:, pb, 0:T],
                                                           func=AF.Identity, scale=gmul[:, l, i, c, 0:1],
                                                           bias=shiftp(l, i, c)),
                 reads=("tmpf%d" % pb, "gmul%d" % l, "modT%d" % l), writes=("hT%d" % c,))
        if ti == 0:
            S.op("dve", CALL("tensor_tensor", out=smt[:, 0:2, :, :].rearrange("p a g b -> p (a g) b"),
                                                  in0=xT[:, :, T:TS], in1=bcast(rstd[:, T:TS], 1, 8), op=ALU.mult),
                 reads=("xTs", "rstds"), writes=("smt",))
            S.op("dve", CALL("tensor_tensor", out=smt[:, 0:2, :, :].rearrange("p a g b -> p (a g) b"),
                                                  in0=smt[:, 0:2, :, :].rearrange("p a g b -> p (a g) b"),
                                                  in1=gmul[:, l, i, :, 1:17], op=ALU.mult),
                 reads=("smt", "gmul%d" % l), writes=("smt",))
            S.op("dve", CALL("tensor_tensor", out=hT[:, :, T:TS],
                                                  in0=smt[:, 0:2, :, :].rearrange("p a g b -> p (a g) b"),
                                                  in1=shifts(l, i), op=ALU.add),
                 reads=("smt", "modT%d" % l), writes=("hTs",))

    def hkeys(ti):
        return H_KEYS if ti == 0 else H_KEYS[:8]

    def residual_ops(ti, l, i, m, pp):
        S.op("dve", CALL("scalar_tensor_tensor", out=xT[:, m, 0:T], in0=ppair(pp), scalar=gateT[:, l, i, m, 0:1],
                                                     in1=xT[:, m, 0:T], op0=ALU.mult, op1=ALU.add),
             reads=("ps%d" % (2 * pp), "ps%d" % (2 * pp + 1), "gateT%d" % l, "xT%d" % m), writes=("xT%d" % m,))

    def residual_sample(ti, l, i):
        if ti != 0:
            return
        dsl = psum[:, 6, 256:256 + 128].rearrange("p (m b) -> p m b", m=8)
        S.op("dve", CALL("tensor_tensor", out=smt[:, 0:2, :, :].rearrange("p a g b -> p (a g) b"), in0=dsl,
                                              in1=gateT[:, l, i, :, 1:17], op=ALU.mult),
             reads=PS6 + ["gateT%d" % l], writes=("smt",))
        S.op("dve", CALL("tensor_tensor", out=xT[:, :, T:TS], in0=xT[:, :, T:TS],
                                              in1=smt[:, 0:2, :, :].rearrange("p a g b -> p (a g) b"), op=ALU.add),
             reads=("smt", "xTs"), writes=("xTs",))

    def ffn(ti, l, fi, i):
        modulate_ops(ti, l, i)
        sg = segs(ti)
        n = TS if ti == 0 else T
        S.fence(BIG_ALL, AT_KEYS)
        cnt = 0
        for st in range(NFF // 2):
            if ti == 0 and l == 0:
                pop_pending()
            srcg = wg[fi][l][:, st * 256:(st + 1) * 256].rearrange("(k p) n -> p k n", p=128)
            srcu = wu[fi][l][:, st * 256:(st + 1) * 256].rearrange("(k p) n -> p k n", p=128)
            s, _ = wload([(lambda v: v[:, 0:2048].rearrange("p (k n) -> p k n", k=8), srcg),
                          (lambda v: v[:, 2048:4096].rearrange("p (k n) -> p k n", k=8), srcu)])
            wgv = wsl[:, s, 0:2048].rearrange("p (k n) -> p k n", k=8)
            wuv = wsl[:, s, 2048:4096].rearrange("p (k n) -> p k n", k=8)
            for jj in range(2):
                j = st * 2 + jj
                outs = []
                for which, wv in ((0, wgv), (1, wuv)):
                    pp = cnt % 3
                    sl = cnt
                    cnt += 1
                    fs = []
                    for k in range(8):
                        for (o, w, nm) in sg:
                            out = (psum[:, 2 * pp, :] if nm == "h0" else psum[:, 2 * pp + 1, :] if nm == "h1"
                                   else SL(sl)[0])
                            fs.append(CALL("matmul",
                                out, wv[:, k, jj * 128:(jj + 1) * 128], hT[:, k, o:o + w],
                                start=(k == 0), stop=(k == 7)))
                    wr = ["ps%d" % (2 * pp), "ps%d" % (2 * pp + 1)] + ([SL(sl)[1]] if ti == 0 else [])
                    S.op("pe", seq(fs), reads=["ws%d" % s] + hkeys(ti), writes=wr)
                    outs.append((pp, sl))
                (pg, sg_), (pu, su_) = outs
                sb_ = j % 2
                S.op("act", CALL("activation", out=sil[:, sb_, 0:T], in_=ppair(pg), func=AF.Silu),
                     reads=("ps%d" % (2 * pg), "ps%d" % (2 * pg + 1)), writes=("sil%d" % sb_,))
                S.op("dve", CALL("tensor_tensor", out=aT[:, j, 0:T], in0=sil[:, sb_, 0:T],
                                                                       in1=ppair(pu), op=ALU.mult),
                     reads=("sil%d" % sb_, "ps%d" % (2 * pu), "ps%d" % (2 * pu + 1)), writes=("aT%d" % j,))
                if ti == 0:
                    S.op("act", CALL("activation", out=sil[:, sb_, T:TS],
                                                                         in_=SL(sg_)[0],
                                                                         func=AF.Silu),
                         reads=(SL(sg_)[1],), writes=("sils%d" % sb_,))
                    S.op("dve", CALL("tensor_tensor",
                        out=aT[:, j, T:TS], in0=sil[:, sb_, T:TS], in1=SL(su_)[0],
                        op=ALU.mult), reads=("sils%d" % sb_, SL(su_)[1]), writes=("aTs",))
        sgm = segs(ti)
        for m in range(8):
            src = wd[fi][l][:, m * 128:(m + 1) * 128].rearrange("(k p) n -> p k n", p=128)
            s1, _ = wload([(lambda v: v[:, 0:NFF * 128].rearrange("p (k n) -> p k n", k=NFF), src)])
            wv = wsl[:, s1, 0:NFF * 128].rearrange("p (k n) -> p k n", k=NFF)
            pp = m % 2
            fs = []
            for k in range(NFF):
                for (o, w, nm) in sgm:
                    out = (psum[:, 2 * pp, :] if nm == "h0" else psum[:, 2 * pp + 1, :] if nm == "h1"
                           else psum[:, 6, 256 + m * 16:256 + (m + 1) * 16])
                    fs.append(CALL("matmul",
                        out, wv[:, k, :], aT[:, k, o:o + w], start=(k == 0), stop=(k == NFF - 1)))
            wr = ["ps%d" % (2 * pp), "ps%d" % (2 * pp + 1)] + (PS6 if ti == 0 else [])
            S.op("pe", seq(fs), reads=["ws%d" % s1] + (AT_KEYS if ti == 0 else AT_KEYS[:NFF]), writes=wr)
            residual_ops(ti, l, i, m, pp)
            if ti != 0:
                sq_op(ti, m, ("xT%d" % m,))
                if m >= 1:
                    stats_mm(ti, m - 1)
        if ti != 0:
            stats_mm(ti, 7)
        if ti == 0:
            residual_sample(ti, l, i)
            for m in range(8):
                sq_op(ti, m, ("xT%d" % m, "xTs"))
            for m in range(8):
                stats_mm(ti, m)
        rstd_ops(ti)

    def load_x(ti):
        S.fence(BIG_ALL, XIN_KEYS)
        for b in range(8):
            r0 = ti * T + b * 128
            S.dma("sp", "xin%d" % b, CALL("dma_start", out=xin[:, b, :], in_=xp[r0:r0 + 128, :]),
                  writes=("xin%d" % b,), extra=([("outy%d" % (ti - 1), S.cnt["outy%d" % (ti - 1)])] if ti > 0 else []))
        for c in range(8):
            for hf in range(2):
                pb = (c * 2 + hf) % 4
                S.op("pe", seq([(CALL("transpose",
                    psum[:, pb, (b % 4) * 128:(b % 4 + 1) * 128], xin[:, b, c * 128:(c + 1) * 128], ident[:, :]))
                    for b in range(hf * 4, hf * 4 + 4)]),
                     reads=["xin%d" % b for b in range(hf * 4, hf * 4 + 4)] + ["ident"], writes=("ps%d" % pb,))
                S.op("dve" if hf == 0 else "act",
                     (CALL("tensor_copy", out=xT[:, c, hf * 512:(hf + 1) * 512], in_=psum[:, pb, :]))
                     if hf == 0 else
                     (CALL("activation", out=xT[:, c, hf * 512:(hf + 1) * 512], in_=psum[:, pb, :],
                                                                 func=AF.Identity)),
                     reads=("ps%d" % pb,), writes=("xT%d" % c,))
        for c in range(8):
            sq_op(ti, c, ("xT%d" % c, "xTs") if ti == 0 else ("xT%d" % c,))
        for c in range(8):
            stats_mm(ti, c)
        rstd_ops(ti)

    def store_y(ti):
        S.fence(BIG_ALL, XIN_KEYS)
        for c in range(8):
            pb = c % 2
            S.op("dve", CALL("tensor_tensor", out=tmpf[:, pb, 0:T], in0=xT[:, c, 0:T],
                                                              in1=rstd[:, 0:T], op=ALU.mult),
                 reads=("xT%d" % c, "rstd"), writes=("tmpf%d" % pb,))
            S.op("act", CALL("activation", out=xT[:, c, 0:T], in_=tmpf[:, pb, 0:T], func=AF.Identity,
                                                           scale=colsT[:, 0, 100 + c:101 + c]),
                 reads=("tmpf%d" % pb, "colsT"), writes=("xT%d" % c,))
        if ti == 0:
            S.op("dve", CALL("tensor_tensor", out=xT[:, :, T:TS], in0=xT[:, :, T:TS],
                                                  in1=bcast(rstd[:, T:TS], 1, 8), op=ALU.mult),
                 reads=("xTs", "rstds"), writes=("xTs",))
            S.op("dve", CALL("tensor_tensor", out=xT[:, :, T:TS], in0=xT[:, :, T:TS],
                                                  in1=bcast(colsT[:, 0, 100:108], 2, NS), op=ALU.mult),
                 reads=("xTs", "colsT"), writes=("xTs",))
            S.op("pe", seq([(CALL("transpose", psum[0:NS, 6 + c // 4, (c % 4) * 128:(c % 4 + 1) * 128],
                                                          xT[:, c, T:TS], ident[:, :])) for c in range(8)]),
                 reads=("xTs", "ident"), writes=PS6 + ["ps7"])
            S.op("dve", CALL("tensor_copy", out=yrow_s[:, :], in_=psum[0:NS, 6:8, :].rearrange("p a b -> p (a b)")),
                 reads=PS6 + ["ps7"], writes=("yrow_s",))
            S.dma("sp", "outy%d" % ti, CALL("dma_start", out=y_s, in_=yrow_s[:, :]), reads=("yrow_s",))
        for b in range(8):
            pp = b % 3
            S.op("pe", seq([(CALL("transpose",
                psum[:, 2 * pp + c // 4, (c % 4) * 128:(c % 4 + 1) * 128], xT[:, c, b * 128:(b + 1) * 128], ident[:, :]))
                for c in range(8)]),
                 reads=["xT%d" % c for c in range(8)] + ["ident"], writes=("ps%d" % (2 * pp), "ps%d" % (2 * pp + 1)))
            if b % 2 == 0:
                S.op("dve", CALL("tensor_copy", out=xin[:, b, :], in_=ppair(pp)),
                     reads=("ps%d" % (2 * pp), "ps%d" % (2 * pp + 1)), writes=("xin%d" % b,))
            else:
                S.op("act", CALL("activation", out=xin[:, b, :], in_=ppair(pp), func=AF.Identity),
                     reads=("ps%d" % (2 * pp), "ps%d" % (2 * pp + 1)), writes=("xin%d" % b,))
            r0 = ti * T + b * 128
            S.dma("sp", "outy%d" % ti, CALL("dma_start", out=y_p[r0:r0 + 128, :], in_=xin[:, b, :]),
                  reads=("xin%d" % b,))

    def mixer(ti, l):
        modulate_ops(ti, l, 1)
        sg = segs(ti)
        last = (ti == NT - 1)
        S.fence(BIG_ALL, MIX_KEYS)
        cnt = [0]

        def fm_proj(wv, col0, dst_kind, gidx):
            pp = cnt[0] % 3
            sl = cnt[0]
            cnt[0] += 1
            fs = []
            for k in range(8):
                for (o, w, nm) in sg:
                    out = (psum[:, 2 * pp, :] if nm == "h0" else psum[:, 2 * pp + 1, :] if nm == "h1"
                           else SL(sl)[0])
                    fs.append(CALL("matmul",
                        out, wv[:, k, col0:col0 + 128], hT[:, k, o:o + w], start=(k == 0), stop=(k == 7)))
            wr = ["ps%d" % (2 * pp), "ps%d" % (2 * pp + 1)] + ([SL(sl)[1]] if ti == 0 else [])
            S.op("pe", seq(fs), reads=[wkey[0]] + hkeys(ti), writes=wr)
            pk = ("ps%d" % (2 * pp), "ps%d" % (2 * pp + 1))
            if dst_kind == "q":
                S.op("act", CALL("activation", out=qT[:, gidx, 0:T], in_=ppair(pp), func=AF.Identity, scale=0.125),
                     reads=pk, writes=("qT",))
            elif dst_kind == "k":
                S.op("act", CALL("activation", out=kT[:, 0:T], in_=ppair(pp), func=AF.Identity),
                     reads=pk, writes=("kT",))
                if last:
                    S.op("act", CALL("activation", out=kTf[:, :], in_=psum[:, 2 * pp + 1, 384:512], func=AF.Identity),
                         reads=pk, writes=("kTf",))
            elif dst_kind == "u":
                S.op("act" if gidx % 2 == 0 else "dve",
                     (CALL("activation", out=uT[:, gidx, 16:16 + T], in_=ppair(pp), func=AF.Identity))
                     if gidx % 2 == 0 else
                     (CALL("tensor_copy", out=uT[:, gidx, 16:16 + T], in_=ppair(pp))),
                     reads=pk, writes=("uT",))
            if ti == 0:
                zi = {"q": gidx, "k": 4, "v": 5, "u": 6 + gidx}[dst_kind]
                S.op("dve", CALL("tensor_copy", out=zsT[:, zi, :], in_=SL(sl)[0]),
                     reads=(SL(sl)[1],), writes=("zsT",))

        def tok_proj_sample(wv, col0, ncols, zcol0):
            fs = [(CALL("matmul", psum[0:NS, 7, 0:ncols], hT[:, k, T:TS], wv[:, k, col0:col0 + ncols],
                                           start=(k == 0), stop=(k == 7))) for k in range(8)]
            S.op("pe", seq(fs), reads=[wkey[0], "hTs"], writes=("ps7",))
            S.op("dve", CALL("tensor_copy", out=ztok[:, zcol0:zcol0 + ncols], in_=psum[0:NS, 7, 0:ncols]),
                 reads=("ps7",), writes=("ztok",))

        wkey = [None]
        src = w_in[l][:, 768:1280].rearrange("(k p) n -> p k n", p=128)
        s, _ = wload([(lambda v: v.rearrange("p (k n) -> p k n", k=8), src)])
        wkey[0] = "ws%d" % s
        wuu = wsl[:, s, :].rearrange("p (k n) -> p k n", k=8)
        S.op("dve", CALL("tensor_copy", out=uT[:, :, 0:16], in_=ucar[:, l, :, :]),
             reads=("ucar%d" % l,), writes=("uT",))
        for g in range(4):
            fm_proj(wuu, g * 128, "u", g)
        if ti == 0:
            tok_proj_sample(wuu, 0, 512, 768)
        S.op("dve", CALL("tensor_copy", out=ucar[:, l, :, :], in_=uT[:, :, T:T + 16]),
             reads=("uT",), writes=("ucar%d" % l,))

        for g in range(4):
            wdw = POOL_W[g]
            cur = uT[:, g, :]
            lvl = 1
            step = 0
            while lvl < wdw:
                dstb = tmpf[:, step % 2, :]
                lo = 2 * lvl - 1
                S.op("dve", CALL("tensor_tensor",
                    out=dstb[:, lo:16 + T], in0=cur[:, lo:16 + T], in1=cur[:, lo - lvl:16 + T - lvl], op=ALU.add),
                     reads=("uT", "tmpf0", "tmpf1"), writes=("tmpf%d" % (step % 2),))
                cur = dstb
                lvl *= 2
                step += 1
            S.op("dve", CALL("scalar_tensor_tensor",
                out=pooledT[:, g, 0:T], in0=cur[:, 16:16 + T], scalar=1.0 / wdw, in1=uT[:, g, 16:16 + T],
                op0=ALU.mult, op1=ALU.subtract), reads=("uT", "tmpf0", "tmpf1"), writes=("pooledT",))
            if ti == 0:
                S.op("dve", CALL("tensor_tensor", out=smt[:, 2, 0, :], in0=cur[:, 16:32],
                                                                    in1=invc[:, g, :], op=ALU.mult),
                     reads=("tmpf0", "tmpf1", "invc"), writes=("smt",))
                S.op("dve", CALL("tensor_tensor", out=pooledT[:, g, 0:16], in0=smt[:, 2, 0, :],
                                                           in1=uT[:, g, 16:32], op=ALU.subtract),
                     reads=("smt", "uT"), writes=("pooledT",))
        if last:
            S.op("pe", seq([(CALL("transpose", psum[0:15, 7, g * 128:(g + 1) * 128],
                                                          uT[:, g, 16 + T - 15:16 + T], ident[:, :])) for g in range(4)]),
                 reads=("uT", "ident"), writes=("ps7",))
            S.op("dve", CALL("tensor_copy", out=utok_rows[0:15, :], in_=psum[0:15, 7, :]),
                 reads=("ps7",), writes=("utok_rows",))
            S.dma("sp", "outl%d" % l, CALL("dma_start", out=o_pp[l], in_=utok_rows[0:15, :]), reads=("utok_rows",))
        src = w_in[l][:, 0:512].rearrange("(k p) n -> p k n", p=128)
        s, _ = wload([(lambda v: v.rearrange("p (k n) -> p k n", k=8), src)])
        wkey[0] = "ws%d" % s
        wq = wsl[:, s, :].rearrange("p (k n) -> p k n", k=8)
        for g in range(4):
            fm_proj(wq, g * 128, "q", g)
        if ti == 0:
            tok_proj_sample(wq, 0, 512, 0)
        src = w_in[l][:, 512:768].rearrange("(k p) n -> p k n", p=128)
        s, _ = wload([(lambda v: v[:, 0:2048].rearrange("p (k n) -> p k n", k=8), src)])
        wkey[0] = "ws%d" % s
        wkv = wsl[:, s, 0:2048].rearrange("p (k n) -> p k n", k=8)
        fm_proj(wkv, 0, "k", 0)
        for hf in range(2):
            pb = 2 * (cnt[0] % 3) + hf
            fs = []
            for bb in range(4):
                b = hf * 4 + bb
                for k in range(8):
                    fs.append(CALL("matmul",
                        psum[:, pb, bb * 128:(bb + 1) * 128], hT[:, k, b * 128:(b + 1) * 128], wkv[:, k, 128:256],
                        start=(k == 0), stop=(k == 7)))
            S.op("pe", seq(fs), reads=[wkey[0]] + H_KEYS[:8], writes=("ps%d" % pb,))
            S.op("act", CALL("activation",
                out=vtok[:, hf * 4:hf * 4 + 4, :], in_=psum[:, pb, :].rearrange("p (a b) -> p a b", a=4), func=AF.Identity),
                 reads=("ps%d" % pb,), writes=("vtok",))
            if last and hf == 1:
                S.op("act", CALL("activation", out=vtokf[:, :], in_=psum[:, pb, 384:512], func=AF.Identity),
                     reads=("ps%d" % pb,), writes=("vtokf",))
        cnt[0] += 1
        if ti == 0:
            sl = cnt[0]
            cnt[0] += 1
            S.op("pe", seq([(CALL("matmul", SL(sl)[0], wkv[:, k, 128:256],
                                                           hT[:, k, T:TS], start=(k == 0), stop=(k == 7)))
                            for k in range(8)]), reads=[wkey[0], "hTs"], writes=(SL(sl)[1],))
            S.op("dve", CALL("tensor_copy", out=zsT[:, 5, :], in_=SL(sl)[0]),
                 reads=(SL(sl)[1],), writes=("zsT",))
            tok_proj_sample(wkv, 0, 256, 512)
        if last:
            S.op("pe", CALL("transpose", psum[:, 7, 0:128], kTf[:, :], ident[:, :]),
                 reads=("kTf", "ident"), writes=("ps7",))
            S.op("dve", CALL("tensor_copy", out=kTf[:, :], in_=psum[:, 7, 0:128]), reads=("ps7",), writes=("kTf",))
            S.dma("sp", "outl%d" % l, CALL("dma_start", out=o_kp[l], in_=kTf[:, :]), reads=("kTf",))
            S.dma("sp", "outl%d" % l, CALL("dma_start", out=o_vp[l], in_=vtokf[:, :]), reads=("vtokf",))
        if ti == 0:
            sample_mixer(l)

        S.fence(H_KEYS, MX_KEYS)
        src = pool_w[l].rearrange("g p n -> p g n")
        s, _ = wload([(lambda v: v[:, 0:512].rearrange("p (g n) -> p g n", g=4), src)])
        wpl = wsl[:, s, 0:512].rearrange("p (g n) -> p g n", g=4)
        for g in range(4):
            pp = cnt[0] % 3
            sl = cnt[0]
            cnt[0] += 1
            fs = []
            for (o, w, nm) in sg:
                out = (psum[:, 2 * pp, :] if nm == "h0" else psum[:, 2 * pp + 1, :] if nm == "h1"
                       else SL(sl)[0])
                fs.append(CALL("matmul", out, wpl[:, g, :], pooledT[:, g, o:o + w],
                                                                      start=True, stop=True))
            wr = ["ps%d" % (2 * pp), "ps%d" % (2 * pp + 1)] + ([SL(sl)[1]] if ti == 0 else [])
            S.op("pe", seq(fs), reads=["ws%d" % s, "pooledT"] + (["pooledTs"] if ti == 0 else []), writes=wr)
            S.op("act", CALL("activation", out=mixedT[:, 4 + g, 0:T], in_=ppair(pp), func=AF.Identity,
                                                           scale=colsT[:, l, 96 + g:97 + g]),
                 reads=("ps%d" % (2 * pp), "ps%d" % (2 * pp + 1), "colsT"), writes=("mx%d" % (4 + g),))
            if ti == 0:
                S.op("act", CALL("activation", out=mixedT[:, 4 + g, T:TS],
                                                               in_=SL(sl)[0], func=AF.Identity,
                                                               scale=colsT[:, l, 96 + g:97 + g]),
                     reads=(SL(sl)[1], "colsT"), writes=("mxs",))

        def unit_info(n):
            qb, j = n // 2, n % 2
            first_blk = (ti == 0 and qb == 0)
            types = ([] if first_blk else [0]) + [1]
            return qb, j, j * 64, (j + 1) * 64, n % 2, types

        def pbuf_of(n):
            b4 = n % 4
            if b4 < 2:
                return pT[:, b4, :, :], "pT%d" % b4
            return sil[:, b4 - 2, 0:1024].rearrange("p (a b) -> p a b", a=2), "sil%d" % (b4 - 2)

        def sbuf_of(n):
            b4 = n % 4
            if b4 < 2:
                return sbf[:, b4, :, :], "sbf%d" % b4
            return tmpf[:, b4 - 2, 0:1024].rearrange("p (a b) -> p a b", a=2), "tmpf%d" % (b4 - 2)

        def att_front(n):
            qb, j, P0, P1, u, types = unit_info(n)
            fs = []
            for t in types:
                if t == 0:
                    kl = kcar[P0:P1, l, :] if qb == 0 else kT[P0:P1, (qb - 1) * 128:qb * 128]
                else:
                    kl = kT[P0:P1, qb * 128:(qb + 1) * 128]
                fs.append(CALL("matmul", psum[:, 2 * u + t, :], kl, qT[P0:P1, :, qb * 128:(qb + 1) * 128],
                               start=True, stop=False))
                fs.append(CALL("matmul", psum[:, 2 * u + t, :], identb[:, :],
                               biasTh[:, t, 4 * j:4 * j + 4, :].rearrange("p h q -> p (h q)"), start=False, stop=False))
                fs.append(CALL("matmul", psum[:, 2 * u + t, :], identb[:, :],
                               biasTl[:, t, 4 * j:4 * j + 4, :].rearrange("p h q -> p (h q)"), start=False, stop=True))
            S.op("pe", seq(fs), reads=("kT", "qT", "kcar%d" % l, "biasTh", "biasTl", "identb"),
                 writes=("ps%d" % (2 * u), "ps%d" % (2 * u + 1)))
            t0 = types[0]
            pbv, pbk = pbuf_of(n)
            S.op("act", CALL("activation", out=pbv[:, t0:2, :].rearrange("p a b -> p (a b)"),
                             in_=psum[:, 2 * u + t0:2 * u + 2, :].rearrange("p a b -> p (a b)"), func=AF.Exp),
                 reads=("ps%d" % (2 * u), "ps%d" % (2 * u + 1)), writes=(pbk,))

        def att_back(n):
            qb, j, P0, P1, u, types = unit_info(n)
            ob = 4 + (qb % 2)
            db = 6 + (qb % 2)
            dkeys = ["ps%d" % db]
            fs = []
            for n_, t in enumerate(types):
                if t == 0:
                    vl = vcar[:, l, P0:P1] if qb == 0 else vtok[:, qb - 1, P0:P1]
                else:
                    vl = vtok[:, qb, P0:P1]
                fs.append(CALL("matmul", psum[P0:P1, ob, :], vl, pbuf_of(n)[0][:, t, :],
                               start=(n_ == 0), stop=(n_ == len(types) - 1)))
            for n_, t in enumerate(types):
                fs.append(CALL("matmul", psum[P0:P1, db, :], onesb[:, 0:64], pbuf_of(n)[0][:, t, :],
                               start=(n_ == 0), stop=(n_ == len(types) - 1)))
            S.op("pe", seq(fs), reads=(pbuf_of(n)[1], "vtok", "vcar%d" % l, "onesb"), writes=["ps%d" % ob] + dkeys)
            if j == 1:
                r = qb % 2
                S.op("dve", CALL("tensor_tensor", out=rden[:, r, :].rearrange("p (g q) -> p g q", g=4),
                                 in0=psum[:, db, :].rearrange("p (g q) -> p g q", g=4),
                                 in1=bcast(esink[:, l, :], 2, 128), op=ALU.add),
                     reads=dkeys + ["esink"], writes=("rden%d" % r,))
                S.op("act", CALL("activation", out=rden[:, r, :], in_=rden[:, r, :], func=AF.Ln),
                     reads=("rden%d" % r,), writes=("rden%d" % r,))
                S.op("act", CALL("activation", out=rden[:, r, :], in_=rden[:, r, :], func=AF.Exp, scale=-1.0),
                     reads=("rden%d" % r,), writes=("rden%d" % r,))
                S.op("dve", CALL("tensor_tensor", out=mixedT[:, 0:4, qb * 128:(qb + 1) * 128],
                                 in0=psum[:, ob, :].rearrange("p (g q) -> p g q", g=4),
                                 in1=rden[:, r, :].rearrange("p (g q) -> p g q", g=4), op=ALU.mult),
                     reads=("ps%d" % ob, "rden%d" % r), writes=("mx0", "mx1", "mx2", "mx3"))

        LAG = 3
        for n in range(min(LAG, 16)):
            att_front(n)
        for n in range(16):
            if n + LAG < 16:
                att_front(n + LAG)
            att_back(n)
        S.op("act", CALL("activation", out=kcar[:, l, :], in_=kT[:, T - 128:T], func=AF.Identity),
             reads=("kT",), writes=("kcar%d" % l,))
        S.op("act", CALL("activation", out=vcar[:, l, :], in_=vtok[:, 7, :], func=AF.Identity),
             reads=("vtok",), writes=("vcar%d" % l,))

        S.fence(["pooledT", "pooledTs", "qT", "qTs"], XSQ_KEYS)
        wst = {}

        def lhs_of(k, m):
            return wst[m // 2][:, k, (m % 2) * 128:(m % 2 + 1) * 128]

        sgm = segs(ti)
        for m in range(8):
            if m % 2 == 0:
                src = w_out[l][:, m * 128:(m + 2) * 128].rearrange("(k p) n -> p k n", p=128)
                s, _ = wload([(lambda v: v[:, 0:2048].rearrange("p (k n) -> p k n", k=8), src)])
                wst[m // 2] = wsl[:, s, 0:2048].rearrange("p (k n) -> p k n", k=8)
                wst[("s", m // 2)] = s
            pp = m % 2
            fs = []
            for k in range(8):
                lt = lhs_of(k, m)
                for (o, w, nm) in sgm:
                    out = (psum[:, 2 * pp, :] if nm == "h0" else psum[:, 2 * pp + 1, :] if nm == "h1"
                           else psum[:, 6, 256 + m * 16:256 + (m + 1) * 16])
                    fs.append(CALL("matmul",
                        out, lt, mixedT[:, k, o:o + w], start=(k == 0), stop=(k == 7)))
            wr = ["ps%d" % (2 * pp), "ps%d" % (2 * pp + 1)] + (PS6 if ti == 0 else [])
            S.op("pe", seq(fs), reads=["ws%d" % wst[("s", m // 2)]] + (MX_KEYS if ti == 0 else MX_KEYS[:8]), writes=wr)
            residual_ops(ti, l, 1, m, pp)
            if ti != 0:
                sq_op(ti, m, ("xT%d" % m,), alt=True)
                if m >= 1:
                    stats_mm(ti, m - 1, alt=True)
        S.fence(MX_KEYS, H_KEYS)
        if ti != 0:
            stats_mm(ti, 7, alt=True)
        if ti == 0:
            residual_sample(ti, l, 1)
            for m in range(8):
                sq_op(ti, m, ("xT%d" % m, "xTs"), alt=True)
            for m in range(8):
                stats_mm(ti, m, alt=True)
        rstd_ops(ti)

    def sample_mixer(l):
        S.dma("pool", "kvc", CALL("dma_start", out=kc[:, :, :], in_=ck[l].rearrange("b j f -> j b f")),
              writes=("kc",))
        S.dma("pool", "kvc", CALL("dma_start", out=vc[:, :, :], in_=cv[l].rearrange("b j f -> j b f")),
              writes=("vc",))
        S.lastw["kc"] = ("kvc", S.cnt["kvc"])
        S.lastw["vc"] = ("kvc", S.cnt["kvc"])
        S.dma("sp", "outs%d" % l, CALL("dma_start", out=o_ks[l, :, 0:127, :], in_=ck[l, :, 1:128, :]))
        S.dma("sp", "outs%d" % l, CALL("dma_start", out=o_vs[l, :, 0:127, :], in_=cv[l, :, 1:128, :]))
        S.dma("sp", "outs%d" % l, CALL("dma_start", out=o_ps[l, :, 0:14, :], in_=spool[l, :, 1:15, :]))
        S.dma("sp", "outs%d" % l, CALL("dma_start", out=o_ks[l, :, 127, :], in_=ztok[:, 512:640]), reads=("ztok",))
        S.dma("sp", "outs%d" % l, CALL("dma_start", out=o_vs[l, :, 127, :], in_=ztok[:, 640:768]), reads=("ztok",))
        S.dma("sp", "outs%d" % l, CALL("dma_start", out=o_ps[l, :, 14, :], in_=ztok[:, 768:1280]), reads=("ztok",))
        S.fence(BIG_ALL, ["hrows"])
        for a in range(2):
            S.dma("sp", "hist", CALL("dma_start",
                out=hrows[:, a, :], in_=spool[l, a * 8:(a + 1) * 8, :, :].rearrange("b s f -> (b s) f")),
                  writes=("hrows",))
        for g in range(4):
            S.op("pe", seq([(CALL("transpose", psum[:, 7, a * 120:(a + 1) * 120],
                                                               hrows[:, a, g * 128:(g + 1) * 128], ident[0:120, 0:120]))
                            for a in range(2)]), reads=("hrows", "ident"), writes=("ps7",))
            S.op("dve", CALL("tensor_copy", out=uhist[:, g, :], in_=psum[:, 7, 0:240]),
                 reads=("ps7",), writes=("uhist",))
        for g in range(4):
            wdw = POOL_W[g]
            hv = uhist[:, g, :].rearrange("p (b s) -> p b s", s=15)[:, :, 15 - (wdw - 1):15]
            S.op("dve", CALL("tensor_reduce", out=smt[:, 2, 0, :], in_=hv, axis=AX.X, op=ALU.add),
                 reads=("uhist",), writes=("smt",))
            S.op("dve", CALL("tensor_tensor", out=smt[:, 2, 1, :], in0=smt[:, 2, 0, :], in1=zsT[:, 6 + g, :],
                                                       op=ALU.add), reads=("smt", "zsT"), writes=("smt",))
            S.op("dve", CALL("scalar_tensor_tensor",
                out=pooledT[:, g, T:TS], in0=smt[:, 2, 1, :], scalar=1.0 / wdw, in1=zsT[:, 6 + g, :],
                op0=ALU.mult, op1=ALU.subtract), reads=("smt", "zsT"), writes=("pooledTs",))
        S.op("dve", CALL("tensor_copy", out=qtokb[:, :], in_=ztok[:, 0:512]), reads=("ztok",), writes=("qtokb",))
        for b in range(NS):
            pb = b % 2
            S.op("pe", CALL("matmul", psum[:, pb, :], selb[:, b, :], qtokb[:, :], start=True, stop=True),
                 reads=("selb", "qtokb"), writes=("ps%d" % pb,))
            S.op("dve", CALL("tensor_tensor",
                out=prod[:, :].rearrange("p (g hh d) -> p g hh d", g=4, hh=2),
                in0=psum[:, pb, :].rearrange("p (g hh d) -> p g hh d", g=4, hh=2),
                in1=bcast(kc[:, b, :].rearrange("p (hh d) -> p hh d", hh=2), 1, 4), op=ALU.mult),
                 reads=("ps%d" % pb, "kc"), writes=("tmpf0",))
            S.op("dve", CALL("tensor_reduce", out=s_s[:, b, :], in_=prod[:, :].rearrange("p (h d) -> p h d", d=64),
                                                       axis=AX.X, op=ALU.add), reads=("tmpf0",), writes=("s_s",))
        S.op("dve", CALL("scalar_tensor_tensor", out=s_s[:, :, :], in0=s_s[:, :, :], scalar=0.125,
                                                     in1=bcast(bias_s[:, :], 1, NS), op0=ALU.mult, op1=ALU.add),
             reads=("s_s", "bias_s"), writes=("s_s",))
        S.op("act", CALL("activation", out=pT_s[:, :, :], in_=s_s[:, :, :], func=AF.Exp),
             reads=("s_s",), writes=("pT_s",))
        S.op("pe", seq([(CALL("matmul", psum[:, 2, b * 8:(b + 1) * 8], vc[:, b, :], pT_s[:, b, :],
                                                  start=True, stop=True)) for b in range(NS)]
                       + [CALL("matmul", psum[:, 3, 0:128], onesb[:, :], pT_s[:, :, :].rearrange("p b h -> p (b h)"),
                                             start=True, stop=True)]),
             reads=("vc", "pT_s", "onesb"), writes=("ps2", "ps3"))
        S.op("dve", CALL("tensor_tensor", out=smt[:, 3, :, :], in0=zsT[:, 0:4, :], in1=bcast(zsT[:, 4, :], 1, 4),
                                              op=ALU.mult), reads=("zsT",), writes=("smt3",))
        S.op("pe", CALL("matmul", psum[:, 7, 0:64], blk1[:, :], smt[:, 3, :, :].rearrange("p g b -> p (g b)"),
                                      start=True, stop=True), reads=("blk1", "smt3"), writes=("ps7",))
        for g in range(4):
            S.op("act", CALL("activation", out=smt[:, 4, g, :], in_=psum[:, 7, g * 16:(g + 1) * 16], func=AF.Exp,
                                                    scale=0.125, bias=bias0[:, g:g + 1]),
                 reads=("ps7", "bias0"), writes=("smt4",))
        for hh in range(2):
            P = slice(hh * 64, (hh + 1) * 64)
            oc_sel = psum[P, 2, 0:128].rearrange("p (b g hh) -> p b g hh", b=NS, g=4)[:, :, :, hh].rearrange("p b g -> p g b")
            dc_sel = psum[P, 3, 0:128].rearrange("p (b g hh) -> p b g hh", b=NS, g=4)[:, :, :, hh].rearrange("p b g -> p g b")
            S.op("dve", CALL("tensor_tensor", out=smt[P, 5, :, :], in0=smt[P, 4, :, :],
                                                       in1=bcast(zsT[P, 5, :], 1, 4), op=ALU.mult),
                 reads=("smt4", "zsT"), writes=("smt5",))
            S.op("dve", CALL("tensor_tensor", out=smt[P, 5, :, :], in0=smt[P, 5, :, :],
                                                                      in1=oc_sel, op=ALU.add),
                 reads=("smt5", "ps2"), writes=("smt5",))
            S.op("dve", CALL("tensor_tensor", out=smt[P, 3, :, :], in0=smt[P, 4, :, :],
                                                                      in1=dc_sel, op=ALU.add),
                 reads=("smt4", "ps3", "smt3"), writes=("smt3",))
            S.op("dve", CALL("tensor_tensor", out=smt[P, 3, :, :], in0=smt[P, 3, :, :],
                                                       in1=bcast(esink[P, l, :], 2, NS), op=ALU.add),
                 reads=("smt3", "esink"), writes=("smt3",))
            S.op("dve", CALL("reciprocal", out=smt[P, 3, :, :], in_=smt[P, 3, :, :]),
                 reads=("smt3",), writes=("smt3",))
            S.fence(H_KEYS, ["mxs"])
            S.op("dve", CALL("tensor_tensor", out=mixedT[P, 0:4, T:TS], in0=smt[P, 5, :, :],
                                                       in1=smt[P, 3, :, :], op=ALU.mult),
                 reads=("smt5", "smt3"), writes=("mxs",))

    phases = []
    for ti in range(NT):
        phases.append(lambda ti=ti: load_x(ti))
        for l in range(DEPTH):
            phases.append(lambda ti=ti, l=l: ffn(ti, l, 0, 0))
            phases.append(lambda ti=ti, l=l: mixer(ti, l))
            phases.append(lambda ti=ti, l=l: ffn(ti, l, 1, 2))
        phases.append(lambda ti=ti: store_y(ti))
    def flush_ada():
        while pending:
            pop_pending()
        ada_finish(1)

    phases.insert(4, flush_ada)
    for ph in phases[:DEBUG_PHASES]:
        ph()

    fin = [(s, v) for s, v in S.cnt.items() if s.startswith("out")]
    S.wait_all("sp", fin)

    sems = {}
    for name in sorted(S.semnames):
        sems[name] = es.enter_context(nc.semaphore("s_" + name))
    block = es.enter_context(nc.Block())

    def replay(e, items):
        for it in items:
            if it[0] == "wait":
                e.wait_ge(sems[it[1]], it[2])
            else:
                ins = it[1](e)
                ins.then_inc(sems[it[2]], it[3])

    @block.tensor
    def _(e):
        replay(e, S.q["pe"])

    @block.scalar
    def _(e):
        replay(e, S.q["act"])

    @block.vector
    def _(e):
        replay(e, S.q["dve"])

    @block.gpsimd
    def _(e):
        replay(e, S.q["pool"])

    @block.sync
    def _(e):
        replay(e, S.q["sp"])

    es.close()
    return nc


_CACHE = {}


def _consts():
    c = {}
    c["c_ident"] = np.eye(128, dtype=np.float32)
    c["c_J"] = np.ascontiguousarray(np.eye(128, dtype=np.float32)[::-1])
    k = np.arange(128)[:, None]
    q = np.arange(128)[None, :]
    neg = np.zeros((128, 2, 128), np.float32)
    neg[:, 0, :] = np.where(q <= k, 0.0, NEG)
    neg[:, 1, :] = np.where(q >= k, 0.0, NEG)
    c["c_neg"] = neg
    invc = np.zeros((128, 4, 16), np.float32)
    for g, w in enumerate(POOL_W):
        invc[:, g, :] = 1.0 / np.minimum(np.arange(16) + 1, w).astype(np.float32)
    c["c_invc"] = invc
    oh = np.zeros((32, 640), np.float32)
    for e in range(255):
        d_prev = e + 1
        if d_prev <= 128:
            oh[int(t5_bucket_np(np.array(d_prev))), e] = 1.0
        d_own = e - 127
        if d_own >= 0:
            oh[int(t5_bucket_np(np.array(d_own))), 256 + e] = 1.0
    for j in range(128):
        oh[int(t5_bucket_np(np.array(128 - j))), 512 + j] = 1.0
    c["c_oh"] = oh
    sel = np.zeros((16, 16, 128), np.float32)
    for b in range(16):
        sel[b, b, :] = 1.0
    c["c_sel"] = sel.reshape(16, 2048)
    return c


def kernel(x_prompt, x_sample, c_prompt, c_sample, cache_k, cache_v, state_pool,
           w_ada, b_ada, norm_gain, w_in, w_out, sinks, rel_bias, pool_w, pool_scale,
           ffn1_wg, ffn1_wu, ffn1_wd, ffn2_wg, ffn2_wu, ffn2_wd, final_gain):
    f = lambda a: np.ascontiguousarray(np.asarray(a, dtype=np.float32))
    if "nc" not in _CACHE:
        _CACHE["nc"] = build_program()
    nc = _CACHE["nc"]
    qperm = np.concatenate([np.concatenate([np.arange(g * 64, g * 64 + 64), np.arange((4 + g) * 64, (4 + g) * 64 + 64)])
                            for g in range(4)])
    w_in_p = f(w_in).copy()
    w_in_p[:, :, 0:512] = f(w_in)[:, :, qperm]
    w_out_p = f(w_out).copy()
    w_out_p[:, 0:512, :] = f(w_out)[:, qperm, :]
    shared = {
        "w_ada": f(w_ada), "b_ada": f(b_ada).reshape(DEPTH, 72, 128), "ngain": f(norm_gain).reshape(DEPTH, 24, 128),
        "w_in": w_in_p, "w_out": w_out_p, "sinks": f(sinks), "relb": f(rel_bias), "pool_w": f(pool_w),
        "pscale": f(pool_scale).reshape(DEPTH, 4, 128),
        "f1g": f(ffn1_wg), "f1u": f(ffn1_wu), "f1d": f(ffn1_wd), "f2g": f(ffn2_wg), "f2u": f(ffn2_wu), "f2d": f(ffn2_wd),
        "fgain": f(final_gain).reshape(8, 128),
    }
    shared.update(_consts())
    xp_, xs_, cp_, cs_ = f(x_prompt), f(x_sample), f(c_prompt), f(c_sample)
    ck_, cv_, sp_ = f(cache_k), f(cache_v), f(state_pool)
    in_maps = []
    for i in range(NCORES):
        m = dict(shared)
        sl = slice(i * NS, (i + 1) * NS)
        m["xp"] = xp_[i]
        m["xs"] = xs_[sl, 0, :]
        m["cp"] = cp_[i].reshape(8, 128)
        m["cs"] = cs_[sl]
        m["ck"] = np.ascontiguousarray(ck_[:, sl].reshape(DEPTH, NS, 128, 128))
        m["cv"] = np.ascontiguousarray(cv_[:, sl].reshape(DEPTH, NS, 128, 128))
        m["spool"] = np.ascontiguousarray(sp_[:, sl])
        in_maps.append(m)
    res = run_bass_kernel_spmd(nc, in_maps, core_ids=list(range(NCORES)))
    R = res.results
    y_prompt = np.stack([R[i]["y_p"] for i in range(NCORES)], axis=0)
    y_sample = np.concatenate([R[i]["y_s"] for i in range(NCORES)], axis=0).reshape(NCORES * NS, 1, D)
    nkp = np.stack([R[i]["o_kp"] for i in range(NCORES)], axis=1).reshape(DEPTH, NCORES, 128, 2, 64)
    nvp = np.stack([R[i]["o_vp"] for i in range(NCORES)], axis=1).reshape(DEPTH, NCORES, 128, 2, 64)
    npp = np.stack([R[i]["o_pp"] for i in range(NCORES)], axis=1)
    nks = np.concatenate([R[i]["o_ks"] for i in range(NCORES)], axis=1).reshape(DEPTH, NCORES * NS, 128, 2, 64)
    nvs = np.concatenate([R[i]["o_vs"] for i in range(NCORES)], axis=1).reshape(DEPTH, NCORES * NS, 128, 2, 64)
    nps = np.concatenate([R[i]["o_ps"] for i in range(NCORES)], axis=1)
    return (y_prompt.astype(np.float32), y_sample.astype(np.float32), nkp.astype(np.float32), nvp.astype(np.float32),
            npp.astype(np.float32), nks.astype(np.float32), nvs.astype(np.float32), nps.astype(np.float32))
```

```python
import math
from contextlib import ExitStack

import numpy as np
import concourse.bass as bass
import concourse.mybir as mybir
from concourse.bass_utils import run_bass_kernel_spmd

F32 = mybir.dt.float32
BF16 = mybir.dt.bfloat16
AF = mybir.ActivationFunctionType
ALU = mybir.AluOpType
AX = mybir.AxisListType

NCORES = 8
D = 1024
SEQ = 4096
T = 1024
NT = SEQ // T
NS = 16
TS = T + NS
FF = 2816
NFF = FF // 128
DEPTH = 2
EPS = 1e-6
NEG = -30000.0
WSLOT = 4096
NSLOT = 3
POOL_W = (2, 4, 8, 16)
DEBUG_PHASES = 10 ** 6
DEBUG_MIX = 10 ** 6
DEBUG_SMIX = 10 ** 6


def t5_bucket_np(dist):
    n = np.maximum(dist, 0)
    nf = np.maximum(n, 1).astype(np.float32)
    large = 16 + (np.log(nf / np.float32(16)) / np.float32(math.log(128 / 16)) * np.float32(16)).astype(np.int32)
    large = np.minimum(large, 31)
    return np.where(n < 16, n, large)


class Sched:
    ENGS = ("pe", "act", "dve", "pool", "sp")

    def __init__(self):
        self.q = {e: [] for e in self.ENGS}
        self.cnt = {}
        self.waited = {e: {} for e in self.ENGS}
        self.lastw = {}
        self.readers = {}
        self.semnames = set(self.ENGS)

    def _deps(self, reads, writes):
        toks = []
        for k in reads:
            if k in self.lastw:
                toks.append(self.lastw[k])
        for k in writes:
            if k in self.lastw:
                toks.append(self.lastw[k])
            toks.extend(self.readers.get(k, {}).items())
        return toks

    def _waits(self, eng, toks):
        best = {}
        for s, v in toks:
            if v > best.get(s, 0):
                best[s] = v
        for s, v in best.items():
            if s == "pe" and eng == "pe":
                continue
            if s not in self.ENGS:
                v = self.cnt[s]
            if self.waited[eng].get(s, 0) >= v:
                continue
            self.waited[eng][s] = v
            self.q[eng].append(("wait", s, v))

    def _record(self, tok, reads, writes):
        for k in reads:
            d = self.readers.setdefault(k, {})
            if tok[1] > d.get(tok[0], 0):
                d[tok[0]] = tok[1]
        for k in writes:
            self.lastw[k] = tok
            self.readers[k] = {}

    def op(self, eng, fn, reads=(), writes=(), extra=()):
        toks = self._deps(reads, writes) + list(extra)
        self._waits(eng, toks)
        self.cnt[eng] = self.cnt.get(eng, 0) + 1
        tok = (eng, self.cnt[eng])
        self.q[eng].append(("inst", fn, eng, 1))
        self._record(tok, reads, writes)
        return tok

    def dma(self, queue, sem, fn, reads=(), writes=(), extra=()):
        self.semnames.add(sem)
        toks = self._deps(reads, writes) + list(extra)
        self._waits(queue, toks)
        self.cnt[sem] = self.cnt.get(sem, 0) + 16
        tok = (sem, self.cnt[sem])
        self.q[queue].append(("inst", fn, sem, 16))
        self._record(tok, reads, writes)
        return tok

    def fence(self, src, dst):
        best = {}
        for k in src:
            if k in self.lastw:
                s_, v_ = self.lastw[k]
                if v_ > best.get(s_, 0):
                    best[s_] = v_
            for s_, v_ in self.readers.get(k, {}).items():
                if v_ > best.get(s_, 0):
                    best[s_] = v_
        for k in dst:
            d = self.readers.setdefault(k, {})
            for s_, v_ in best.items():
                if v_ > d.get(s_, 0):
                    d[s_] = v_

    def wait_all(self, eng, toks):
        self._waits(eng, toks)


def CALL(name, *args, **kw):
    def run(e):
        return getattr(e, name)(*args, **kw)
    return run


def seq(fs):
    def run(e):
        last = None
        for f in fs:
            last = f(e)
        return last
    return run


def bcast(ap, pos, count):
    dims = [list(d) for d in ap.ap]
    dims.insert(pos, [0, count])
    return bass.AP(ap.tensor, ap.offset, dims)


def build_program():
    nc = bass.Bass("TRN2", target_bir_lowering=False)
    S = Sched()

    def din(name, shape):
        return nc.dram_tensor(name, list(shape), F32, kind="ExternalInput").ap()

    def dout(name, shape):
        return nc.dram_tensor(name, list(shape), F32, kind="ExternalOutput").ap()

    xp = din("xp", [SEQ, D])
    xs = din("xs", [NS, D])
    cp = din("cp", [8, 128])
    cs = din("cs", [NS, D])
    ck = din("ck", [DEPTH, NS, 128, 128])
    cv = din("cv", [DEPTH, NS, 128, 128])
    spool = din("spool", [DEPTH, NS, 15, 512])
    w_ada = din("w_ada", [DEPTH, D, 9 * D])
    b_ada = din("b_ada", [DEPTH, 72, 128])
    ngain = din("ngain", [DEPTH, 24, 128])
    w_in = din("w_in", [DEPTH, D, 1280])
    w_out = din("w_out", [DEPTH, D, D])
    sinks = din("sinks", [DEPTH, 8])
    relb = din("relb", [32, 8])
    pool_w = din("pool_w", [DEPTH, 4, 128, 128])
    pscale = din("pscale", [DEPTH, 4, 128])
    wg = [din("f1g", [DEPTH, D, FF]), din("f2g", [DEPTH, D, FF])]
    wu = [din("f1u", [DEPTH, D, FF]), din("f2u", [DEPTH, D, FF])]
    wd = [din("f1d", [DEPTH, FF, D]), din("f2d", [DEPTH, FF, D])]
    fgain = din("fgain", [8, 128])
    c_ident = din("c_ident", [128, 128])
    c_J = din("c_J", [128, 128])
    c_neg = din("c_neg", [128, 2, 128])
    c_invc = din("c_invc", [128, 4, 16])
    c_oh = din("c_oh", [32, 640])
    c_sel = din("c_sel", [16, 16 * 128])

    y_p = dout("y_p", [SEQ, D])
    y_s = dout("y_s", [NS, D])
    o_kp = dout("o_kp", [DEPTH, 128, 128])
    o_vp = dout("o_vp", [DEPTH, 128, 128])
    o_pp = dout("o_pp", [DEPTH, 15, 512])
    o_ks = dout("o_ks", [DEPTH, NS, 128, 128])
    o_vs = dout("o_vs", [DEPTH, NS, 128, 128])
    o_ps = dout("o_ps", [DEPTH, NS, 15, 512])
    scr = nc.dram_tensor("scr", [2, 8, 256], F32, kind="Internal").ap()

    es = ExitStack()

    def sb(name, shape, dt):
        return es.enter_context(nc.sbuf_tensor(name, list(shape), dt))

    xT = sb("xT", [128, 8, TS], F32)
    hT = sb("hT", [128, 8, TS], BF16)
    big = sb("big", [128, 23040], BF16)
    tmpf = sb("tmpf", [128, 2, 1056], F32)
    rstd = sb("rstd", [128, TS], F32)
    sil = sb("sil", [128, 2, TS], BF16)
    kT = sb("kT", [128, T + NS], BF16)
    kTf = sb("kTf", [128, 128], F32)
    vtok = sb("vtok", [128, 8, 128], BF16)
    vtokf = sb("vtokf", [128, 128], F32)
    kcar = sb("kcar", [128, DEPTH, 128], BF16)
    vcar = sb("vcar", [128, DEPTH, 128], BF16)
    ucar = sb("ucar", [128, DEPTH, 4, 16], F32)
    pT = sb("pT", [128, 2, 2, 512], BF16)
    biasTh = sb("biasTh", [128, 2, 8, 128], BF16)
    biasTl = sb("biasTl", [128, 2, 8, 128], BF16)
    identb = sb("identb", [128, 128], BF16)
    rden = sb("rden", [128, 2, 512], F32)
    wsl = sb("wsl", [128, NSLOT, WSLOT], BF16)
    modT = sb("modT", [128, DEPTH, 72, 17], F32)
    gmul = sb("gmul", [128, DEPTH, 3, 8, 17], F32)
    gateT = sb("gateT", [128, DEPTH, 3, 8, 17], F32)
    colsT = sb("colsT", [128, DEPTH, 116], F32)
    rows = sb("rows", [128, DEPTH, 128], F32)
    cT = sb("cT", [128, 8, 17], F32)
    scT = sb("scT", [128, 8, 17], BF16)
    ident = sb("ident", [128, 128], F32)
    Jm = sb("Jm", [128, 128], F32)
    onesb = sb("onesb", [128, 128], BF16)
    blk1 = sb("blk1", [128, 128], F32)
    negm = sb("negm", [128, 2, 128], F32)
    invc = sb("invc", [128, 4, 16], F32)
    oh = sb("oh", [32, 640], F32)
    rb = sb("rb", [32, 8], F32)
    selb = sb("selb", [16, 16, 128], BF16)
    esink = sb("esink", [128, DEPTH, 4], F32)
    bias0 = sb("bias0", [128, 4], F32)
    bias_s = sb("bias_s", [128, 8], F32)
    epst = sb("epst", [128, 1], F32)
    kc = sb("kc", [128, NS, 128], BF16)
    vc = sb("vc", [128, NS, 128], BF16)
    uhist = sb("uhist", [128, 4, NS * 15], F32)
    zsT = sb("zsT", [128, 10, NS], F32)
    ztok = sb("ztok", [NS, 1280], F32)
    qtokb = sb("qtokb", [NS, 512], BF16)
    s_s = sb("s_s", [128, NS, 8], F32)
    pT_s = sb("pT_s", [128, NS, 8], BF16)
    smt = sb("smt", [128, 6, 4, NS], F32)

    psum = es.enter_context(nc.psum_tensor("psum", [128, 8, 512], F32))

    bigf = big.bitcast(F32)
    aT = big[:, 0:NFF * TS].rearrange("p (c t) -> p c t", c=NFF)
    xin = bigf[:, 0:8 * T].rearrange("p (b f) -> p b f", b=8)
    uT = bigf[:, 0:4 * 1056].rearrange("p (g t) -> p g t", g=4)
    pooledT = big[:, 8448:8448 + 4 * TS].rearrange("p (g t) -> p g t", g=4)
    sbf = bigf[:, 6304:6304 + 2048].rearrange("p (a b c) -> p a b c", a=2, b=2)
    qT = big[:, 16704:16704 + 4 * TS].rearrange("p (g t) -> p g t", g=4)
    mixedT = hT
    ystg = hT[:, :, :].rearrange("p c t -> p (c t)").bitcast(F32)[:, 0:4096].rearrange("p (a f) -> p a f", a=4)
    YSTG_KEYS = ["ystg%d" % i for i in range(4)]
    Hk = bigf[:, 0:2048].rearrange("p (t h q) -> p t h q", t=2, h=8)
    xsr = bigf[0:NS, 2048:3072]
    csr = bigf[0:NS, 3072:4096]
    hrows = bigf[0:120, 10432:10432 + 1024].rearrange("p (a f) -> p a f", a=2)
    utok_rows = bigf[0:NS, 10432:10432 + 512]
    self32 = bigf[0:16, 4096:6144]
    bvs = bigf[0:8, 6144:6656].rearrange("p (t e) -> p t e", t=2)
    yrow_s = bigf[0:NS, 8192:9216]
    biasT = bigf[:, 7168:9216].rearrange("p (t h q) -> p t h q", t=2, h=8)
    prod = tmpf[:, 0, 0:512]

    AT_KEYS = ["aT%d" % j for j in range(NFF)] + ["aTs"]
    XSQ_KEYS = ["xsq%d" % c for c in range(8)] + ["xsqs%d" % c for c in range(8)]
    MIX_KEYS = ["uT", "uTs", "pooledT", "pooledTs", "sbf0", "sbf1", "qT", "qTs", "hrows", "utok_rows"] + XSQ_KEYS
    XIN_KEYS = ["xin%d" % b for b in range(8)] + ["yrow_s"]
    SETUP_KEYS = ["Hk", "xsr", "csr", "self32", "bvs", "biasT"]
    H_KEYS = ["hT%d" % c for c in range(8)] + ["hTs"]
    BIG_ALL = AT_KEYS + MIX_KEYS + XIN_KEYS + SETUP_KEYS
    MX_KEYS = ["mx%d" % c for c in range(8)] + ["mxs"]

    def pbank(b):
        return psum[:, b, :]

    def ppair(p):
        return psum[:, 2 * p:2 * p + 2, :].rearrange("p a b -> p (a b)")

    PS6 = ["ps6"]

    def SL(n):
        bank = 6 + (n % 2)
        col = ((n // 2) % 8) * 32
        return psum[:, bank, col:col + NS], "ps%d" % bank

    def ps6(i, n=1):
        return psum[:, 6, i * 32:(i + n) * 32]

    wstate = {"n": 0}

    def wload(parts):
        s = wstate["n"] % NSLOT
        wstate["n"] += 1
        tok = None
        for dst, src in parts:
            d = dst(wsl[:, s, :])
            tok = S.dma("pool", "w%d" % s, (CALL("dma_start", out=d, in_=src)),
                        reads=(), writes=("ws%d" % s,))
        return s, tok

    def ld(dst, src, key, sem="cst", queue="sp"):
        return S.dma(queue, sem, (CALL("dma_start", out=dst, in_=src)), writes=(key,))

    ld(ident[:], c_ident, "ident")
    ld(Jm[:], c_J, "Jm")
    ld(negm[:], c_neg, "negm")
    ld(invc[:], c_invc, "invc")
    ld(oh[:], c_oh, "oh")
    ld(rb[:], relb, "rb")
    ld(self32[:], c_sel, "self32")
    for l in range(DEPTH):
        ld(rows[0:72, l, :], b_ada[l], "rows")
        ld(rows[72:96, l, :], ngain[l], "rows")
        ld(rows[96:100, l, :], pscale[l], "rows")
        ld(rows[100:108, l, :], fgain, "rows")
        ld(rows[108:116, l, :], cp, "rows")
    ld(xsr, xs, "xsr")
    ld(csr, cs, "csr")
    for l in range(DEPTH):
        for hh in range(2):
            src = bass.AP(sinks.tensor, l * 8 + hh * 4, [[0, 64], [1, 4]])
            ld(esink[hh * 64:(hh + 1) * 64, l, :], src, "esink")
    for hh in range(2):
        src = bass.AP(relb.tensor, hh * 4, [[0, 64], [1, 4]])
        ld(bias0[hh * 64:(hh + 1) * 64, :], src, "bias0")

    for key in ("ident", "Jm", "negm", "invc", "oh", "rb", "self32", "rows", "xsr", "csr", "esink", "bias0"):
        S.lastw[key] = ("cst", S.cnt["cst"])
    S.op("dve", CALL("memset", onesb[:], 1.0), writes=("onesb",))
    S.op("dve", CALL("memset", blk1[:], 0.0), writes=("blk1",))
    S.op("dve", CALL("memset", blk1[0:64, 0:64], 1.0), writes=("blk1",))
    S.op("dve", CALL("memset", blk1[64:128, 64:128], 1.0), writes=("blk1",))
    S.op("dve", CALL("memset", epst[:], EPS), writes=("epst",))
    S.op("dve", CALL("memset", kcar[:], 0.0), writes=("kcar0", "kcar1"))
    S.op("dve", CALL("memset", vcar[:], 0.0), writes=("vcar0", "vcar1"))
    S.op("dve", CALL("memset", ucar[:], 0.0), writes=("ucar0", "ucar1"))
    S.op("dve", CALL("tensor_copy", out=selb[:].rearrange("p a b -> p (a b)"), in_=self32[:]),
         reads=("self32",), writes=("selb",))
    S.op("act", CALL("activation", out=esink[:], in_=esink[:], func=AF.Exp), reads=("esink",), writes=("esink",))

    for l in range(DEPTH):
        S.op("pe", CALL("transpose", psum[:, 7, 0:116], rows[0:116, l, :], ident[0:116, 0:116]),
             reads=("rows", "ident"), writes=("ps7",))
        S.op("dve", CALL("tensor_copy", out=colsT[:, l, :], in_=psum[:, 7, 0:116]),
             reads=("ps7",), writes=("colsT",))
    S.op("pe", seq([(CALL("transpose", psum[:, 7, c * 16:(c + 1) * 16], csr[:, c * 128:(c + 1) * 128],
                                                  ident[0:NS, 0:NS])) for c in range(8)]),
         reads=("csr", "ident"), writes=("ps7",))
    S.op("dve", CALL("tensor_copy", out=cT[:, :, 1:17],
                                        in_=psum[:, 7, 0:128].rearrange("p (c b) -> p c b", c=8)),
         reads=("ps7",), writes=("cTs",))
    S.op("dve", CALL("tensor_copy", out=cT[:, :, 0], in_=colsT[:, 0, 108:116]), reads=("colsT",), writes=("cTp",))
    S.op("pe", seq([(CALL("transpose", psum[:, 7, c * 16:(c + 1) * 16], xsr[:, c * 128:(c + 1) * 128],
                                                  ident[0:NS, 0:NS])) for c in range(8)]),
         reads=("xsr", "ident"), writes=("ps7",))
    S.op("dve", CALL("tensor_copy", out=xT[:, :, T:TS],
                                        in_=psum[:, 7, 0:128].rearrange("p (c b) -> p c b", c=8)),
         reads=("ps7",), writes=("xTs",))
    S.op("act", CALL("activation", out=scT[:], in_=cT[:], func=AF.Silu), reads=("cTs", "cTp"), writes=("scT",))

    def ada_stage(l, st, bank):
        src = w_ada[l][:, st * 512:(st + 1) * 512].rearrange("(k p) n -> p k n", p=128)
        s, _ = wload([(lambda v: v.rearrange("p (k n) -> p k n", k=8), src)])
        wv = wsl[:, s, :].rearrange("p (k n) -> p k n", k=8)
        fs = []
        for oc in range(4):
            for k in range(8):
                fs.append(CALL("matmul", psum[:, bank, oc * 17:(oc + 1) * 17], wv[:, k, oc * 128:(oc + 1) * 128],
                               scT[:, k, :], start=(k == 0), stop=(k == 7)))
        S.op("pe", seq(fs), reads=("ws%d" % s, "scT"), writes=("ps%d" % bank,))
        S.op("dve", CALL("tensor_tensor", out=modT[:, l, st * 4:(st + 1) * 4, :],
                         in0=psum[:, bank, 0:68].rearrange("p (a b) -> p a b", a=4),
                         in1=bcast(colsT[:, l, st * 4:(st + 1) * 4], 2, 17), op=ALU.add),
             reads=("ps%d" % bank, "colsT"), writes=("modT%d" % l,))

    def ada_finish(l):
        for i in range(3):
            S.op("dve", CALL("scalar_tensor_tensor", out=gmul[:, l, i, :, :],
                             in0=modT[:, l, (3 * i + 1) * 8:(3 * i + 2) * 8, :], scalar=1.0,
                             in1=bcast(colsT[:, l, 72 + i * 8:72 + (i + 1) * 8], 2, 17), op0=ALU.add, op1=ALU.mult),
                 reads=("modT%d" % l, "colsT"), writes=("gmul%d" % l,))
            S.op("dve", CALL("tensor_scalar", out=gateT[:, l, i, :, :],
                             in0=modT[:, l, (3 * i + 2) * 8:(3 * i + 3) * 8, :],
                             scalar1=(1.0 if i == 1 else 0.5), scalar2=None, op0=ALU.mult),
                 reads=("modT%d" % l,), writes=("gateT%d" % l,))

    for st in range(18):
        ada_stage(0, st, st % 6)
    ada_finish(0)
    pending = [(lambda st=st: ada_stage(1, st, 7)) for st in range(18)]

    def pop_pending():
        if pending:
            pending.pop(0)()

    def shiftp(l, i, c):
        return modT[:, l, 3 * i * 8 + c, 0:1]

    def shifts(l, i):
        return modT[:, l, 3 * i * 8:3 * i * 8 + 8, 1:17]

    for t in range(2):
        S.op("pe", CALL("matmul", psum[0:8, 7, 0:256], rb[:, :], oh[:, t * 256:(t + 1) * 256],
                                           start=True, stop=True), reads=("rb", "oh"), writes=("ps7",))
        S.op("dve", CALL("tensor_copy", out=bvs[:, t, :], in_=psum[0:8, 7, 0:256]),
             reads=("ps7",), writes=("bvs",))
    S.dma("sp", "scrw", CALL("dma_start", out=scr.rearrange("t h e -> h t e"), in_=bvs[:]),
          reads=("bvs",), writes=("scr",))
    S.fence(SETUP_KEYS, ["Hk"])
    for t in range(2):
        src = bass.AP(scr.tensor, t * 2048, [[1, 128], [256, 8], [1, 128]])
        S.dma("sp", "scrr", CALL("dma_start", out=Hk[:, t, :, :], in_=src),
              reads=("scr",), writes=("Hk",))
    for t in range(2):
        hv = Hk[:, t, :, :].rearrange("p h q -> p (h q)")
        S.op("pe", seq([(CALL("matmul", psum[:, hf, :], Jm[:, :], hv[:, hf * 512:(hf + 1) * 512],
                                                        start=True, stop=True)) for hf in range(2)]),
             reads=("Hk", "Jm"), writes=("ps0", "ps1"))
        S.op("dve", CALL("tensor_tensor",
            out=biasT[:, t, :, :], in0=ppair(0).rearrange("p (h q) -> p h q", h=8),
            in1=bcast(negm[:, t, :], 1, 8), op=ALU.add), reads=("ps0", "ps1", "negm"), writes=("biasT",))
    S.op("pe", CALL("matmul", psum[:, 7, 0:8].rearrange("p (g hh) -> p g hh", g=4), oh[:, 512:640],
                                  rb[:, :].rearrange("b (hh g) -> b g hh", hh=2), start=True, stop=True),
         reads=("rb", "oh"), writes=("ps7",))
    S.op("dve", CALL("tensor_copy", out=bias_s[:], in_=psum[:, 7, 0:8]), reads=("ps7",), writes=("bias_s",))
    S.op("dve", CALL("tensor_copy", out=biasTh[:], in_=biasT), reads=("biasT",), writes=("biasTh",))
    S.op("dve", CALL("tensor_tensor", out=biasT, in0=biasT, in1=biasTh[:], op=ALU.subtract),
         reads=("biasT", "biasTh"), writes=("biasT",))
    S.op("dve", CALL("tensor_copy", out=biasTl[:], in_=biasT), reads=("biasT",), writes=("biasTl",))
    S.op("dve", CALL("tensor_copy", out=identb[:], in_=ident[:]), reads=("ident",), writes=("identb",))

    def segs(ti):
        sg = [(0, 512, "h0"), (512, 512, "h1")]
        if ti == 0:
            sg.append((T, NS, "s"))
        return sg

    def sq_buf(c, alt):
        if not alt:
            return hT[:, c, :], ("hT%d" % c,), ("hTs",)
        if c < 4:
            return pooledT[:, c, :], ("xsq%d" % c,), ("xsqs%d" % c,)
        return qT[:, c - 4, :], ("xsq%d" % c,), ("xsqs%d" % c,)

    def sq_op(ti, c, xkeys, alt=False):
        n = TS if ti == 0 else T
        buf, k1, k2 = sq_buf(c, alt)
        S.op("dve", CALL("tensor_tensor", out=buf[:, 0:n], in0=xT[:, c, 0:n], in1=xT[:, c, 0:n], op=ALU.mult),
             reads=xkeys, writes=k1 + (k2 if ti == 0 else ()))

    def stats_mm(ti, c, alt=False):
        buf, k1, k2 = sq_buf(c, alt)
        fs = []
        for (o, w, nm) in segs(ti):
            out = psum[:, 4, :] if nm == "h0" else (psum[:, 5, :] if nm == "h1" else psum[:, 7, 0:NS])
            fs.append(CALL("matmul", out, onesb[:, :], buf[:, o:o + w], start=(c == 0), stop=(c == 7)))
        wr = ("ps4", "ps5") + (("ps7",) if ti == 0 else ())
        S.op("pe", seq(fs), reads=k1 + ("onesb",) + (k2 if ti == 0 else ()), writes=wr)

    def norm_stats_ops(ti, c, xkeys):
        sq_op(ti, c, xkeys)
        stats_mm(ti, c)

    def rstd_ops(ti):
        n = TS if ti == 0 else T
        S.op("act", CALL("activation", out=rstd[:, 0:T], in_=ppair(2), func=AF.Ln, bias=epst[:, 0:1],
                                           scale=1.0 / D), reads=("ps4", "ps5", "epst"), writes=("rstd",))
        if ti == 0:
            S.op("act", CALL("activation", out=rstd[:, T:TS], in_=psum[:, 7, 0:NS], func=AF.Ln,
                                               bias=epst[:, 0:1], scale=1.0 / D),
                 reads=("ps7", "epst"), writes=("rstds",))
        S.op("act", CALL("activation", out=rstd[:, 0:n], in_=rstd[:, 0:n], func=AF.Exp, scale=-0.5),
             reads=("rstd", "rstds"), writes=("rstd", "rstds"))

    def modulate_ops(ti, l, i):
        for c in range(8):
            pb = c % 2
            S.op("dve", CALL("tensor_tensor", out=tmpf[:, pb, 0:T], in0=xT[:, c, 0:T],
                                                              in1=rstd[:, 0:T], op=ALU.mult),
                 reads=("xT%d" % c, "rstd"), writes=("tmpf%d" % pb,))
            S.op("act", CALL("activation", out=hT[:, c, 0:T], in_=tmpf[:, pb, 0:T],
                                                           func=AF.Identity, scale=gmul[:, l, i, c, 0:1],
                                                           bias=shiftp(l, i, c)),
                 reads=("tmpf%d" % pb, "gmul%d" % l, "modT%d" % l), writes=("hT%d" % c,))
        if ti == 0:
            S.op("dve", CALL("tensor_tensor", out=smt[:, 0:2, :, :].rearrange("p a g b -> p (a g) b"),
                                                  in0=xT[:, :, T:TS], in1=bcast(rstd[:, T:TS], 1, 8), op=ALU.mult),
                 reads=("xTs", "rstds"), writes=("smt",))
            S.op("dve", CALL("tensor_tensor", out=smt[:, 0:2, :, :].rearrange("p a g b -> p (a g) b"),
                                                  in0=smt[:, 0:2, :, :].rearrange("p a g b -> p (a g) b"),
                                                  in1=gmul[:, l, i, :, 1:17], op=ALU.mult),
                 reads=("smt", "gmul%d" % l), writes=("smt",))
            S.op("dve", CALL("tensor_tensor", out=hT[:, :, T:TS],
                                                  in0=smt[:, 0:2, :, :].rearrange("p a g b -> p (a g) b"),
                                                  in1=shifts(l, i), op=ALU.add),
                 reads=("smt", "modT%d" % l), writes=("hTs",))

    def hkeys(ti):
        return H_KEYS if ti == 0 else H_KEYS[:8]

    def residual_ops(ti, l, i, m, pp):
        S.op("dve", CALL("scalar_tensor_tensor", out=xT[:, m, 0:T], in0=ppair(pp), scalar=gateT[:, l, i, m, 0:1],
                                                     in1=xT[:, m, 0:T], op0=ALU.mult, op1=ALU.add),
             reads=("ps%d" % (2 * pp), "ps%d" % (2 * pp + 1), "gateT%d" % l, "xT%d" % m), writes=("xT%d" % m,))

    def residual_sample(ti, l, i):
        if ti != 0:
            return
        dsl = psum[:, 6, 256:256 + 128].rearrange("p (m b) -> p m b", m=8)
        S.op("dve", CALL("tensor_tensor", out=smt[:, 0:2, :, :].rearrange("p a g b -> p (a g) b"), in0=dsl,
                                              in1=gateT[:, l, i, :, 1:17], op=ALU.mult),
             reads=PS6 + ["gateT%d" % l], writes=("smt",))
        S.op("dve", CALL("tensor_tensor", out=xT[:, :, T:TS], in0=xT[:, :, T:TS],
                                              in1=smt[:, 0:2, :, :].rearrange("p a g b -> p (a g) b"), op=ALU.add),
             reads=("smt", "xTs"), writes=("xTs",))

    def ffn(ti, l, fi, i):
        modulate_ops(ti, l, i)
        sg = segs(ti)
        n = TS if ti == 0 else T
        S.fence(BIG_ALL, AT_KEYS)
        cnt = 0
        for st in range(NFF // 2):
            if ti == 0 and l == 0:
                pop_pending()
            srcg = wg[fi][l][:, st * 256:(st + 1) * 256].rearrange("(k p) n -> p k n", p=128)
            srcu = wu[fi][l][:, st * 256:(st + 1) * 256].rearrange("(k p) n -> p k n", p=128)
            s, _ = wload([(lambda v: v[:, 0:2048].rearrange("p (k n) -> p k n", k=8), srcg),
                          (lambda v: v[:, 2048:4096].rearrange("p (k n) -> p k n", k=8), srcu)])
            wgv = wsl[:, s, 0:2048].rearrange("p (k n) -> p k n", k=8)
            wuv = wsl[:, s, 2048:4096].rearrange("p (k n) -> p k n", k=8)
            pre_done = set()
            if st == 0 and ti != 0:
                units3 = [(0, wgv, 0), (0, wuv, 1), (1, wgv, 2)]
                for k in range(8):
                    fs = []
                    for (jj3, wv3, pp3) in units3:
                        for (o, w, nm) in sg:
                            out = psum[:, 2 * pp3, :] if nm == "h0" else psum[:, 2 * pp3 + 1, :]
                            fs.append(CALL("matmul", out, wv3[:, k, jj3 * 128:(jj3 + 1) * 128], hT[:, k, o:o + w],
                                           start=(k == 0), stop=(k == 7)))
                    S.op("pe", seq(fs), reads=["ws%d" % s, "hT%d" % k], writes=["ps%d" % b for b in range(6)])
                pre_done = {(0, 0), (0, 1), (1, 0)}
            for jj in range(2):
                j = st * 2 + jj
                outs = []
                for which, wv in ((0, wgv), (1, wuv)):
                    pp = cnt % 3
                    sl = cnt
                    cnt += 1
                    if (jj, which) in pre_done:
                        outs.append((pp, sl))
                        continue
                    fs = []
                    for k in range(8):
                        for (o, w, nm) in sg:
                            out = (psum[:, 2 * pp, :] if nm == "h0" else psum[:, 2 * pp + 1, :] if nm == "h1"
                                   else SL(sl)[0])
                            fs.append(CALL("matmul",
                                out, wv[:, k, jj * 128:(jj + 1) * 128], hT[:, k, o:o + w],
                                start=(k == 0), stop=(k == 7)))
                    wr = ["ps%d" % (2 * pp), "ps%d" % (2 * pp + 1)] + ([SL(sl)[1]] if ti == 0 else [])
                    S.op("pe", seq(fs), reads=["ws%d" % s] + hkeys(ti), writes=wr)
                    outs.append((pp, sl))
                (pg, sg_), (pu, su_) = outs
                sb_ = j % 2
                S.op("act", CALL("activation", out=sil[:, sb_, 0:T], in_=ppair(pg), func=AF.Silu),
                     reads=("ps%d" % (2 * pg), "ps%d" % (2 * pg + 1)), writes=("sil%d" % sb_,))
                S.op("dve", CALL("tensor_tensor", out=aT[:, j, 0:T], in0=sil[:, sb_, 0:T],
                                                                       in1=ppair(pu), op=ALU.mult),
                     reads=("sil%d" % sb_, "ps%d" % (2 * pu), "ps%d" % (2 * pu + 1)), writes=("aT%d" % j,))
                if ti == 0:
                    S.op("act", CALL("activation", out=sil[:, sb_, T:TS],
                                                                         in_=SL(sg_)[0],
                                                                         func=AF.Silu),
                         reads=(SL(sg_)[1],), writes=("sils%d" % sb_,))
                    S.op("dve", CALL("tensor_tensor",
                        out=aT[:, j, T:TS], in0=sil[:, sb_, T:TS], in1=SL(su_)[0],
                        op=ALU.mult), reads=("sils%d" % sb_, SL(su_)[1]), writes=("aTs",))
        sgm = segs(ti)
        for m in range(8):
            src = wd[fi][l][:, m * 128:(m + 1) * 128].rearrange("(k p) n -> p k n", p=128)
            s1, _ = wload([(lambda v: v[:, 0:NFF * 128].rearrange("p (k n) -> p k n", k=NFF), src)])
            wv = wsl[:, s1, 0:NFF * 128].rearrange("p (k n) -> p k n", k=NFF)
            pp = m % 2
            fs = []
            for k in range(NFF):
                for (o, w, nm) in sgm:
                    out = (psum[:, 2 * pp, :] if nm == "h0" else psum[:, 2 * pp + 1, :] if nm == "h1"
                           else psum[:, 6, 256 + m * 16:256 + (m + 1) * 16])
                    fs.append(CALL("matmul",
                        out, wv[:, k, :], aT[:, k, o:o + w], start=(k == 0), stop=(k == NFF - 1)))
            wr = ["ps%d" % (2 * pp), "ps%d" % (2 * pp + 1)] + (PS6 if ti == 0 else [])
            S.op("pe", seq(fs), reads=["ws%d" % s1] + (AT_KEYS if ti == 0 else AT_KEYS[:NFF]), writes=wr)
            residual_ops(ti, l, i, m, pp)
            if ti != 0:
                sq_op(ti, m, ("xT%d" % m,))
                if m >= 1:
                    stats_mm(ti, m - 1)
        if ti != 0:
            stats_mm(ti, 7)
        if ti == 0:
            residual_sample(ti, l, i)
            for m in range(8):
                sq_op(ti, m, ("xT%d" % m, "xTs"))
            for m in range(8):
                stats_mm(ti, m)
        rstd_ops(ti)

    def load_x_dma(ti):
        S.fence(BIG_ALL, XIN_KEYS[:8])
        for b in range(8):
            r0 = ti * T + b * 128
            S.dma("sp", "xin%d" % b, CALL("dma_start", out=xin[:, b, :], in_=xp[r0:r0 + 128, :]),
                  writes=("xin%d" % b,))

    def load_x(ti):
        for c in range(8):
            for hf in range(2):
                pb = (c * 2 + hf) % 4
                S.op("pe", seq([(CALL("transpose",
                    psum[:, pb, (b % 4) * 128:(b % 4 + 1) * 128], xin[:, b, c * 128:(c + 1) * 128], ident[:, :]))
                    for b in range(hf * 4, hf * 4 + 4)]),
                     reads=["xin%d" % b for b in range(hf * 4, hf * 4 + 4)] + ["ident"], writes=("ps%d" % pb,))
                S.op("dve" if hf == 0 else "act",
                     (CALL("tensor_copy", out=xT[:, c, hf * 512:(hf + 1) * 512], in_=psum[:, pb, :]))
                     if hf == 0 else
                     (CALL("activation", out=xT[:, c, hf * 512:(hf + 1) * 512], in_=psum[:, pb, :],
                                                                 func=AF.Identity)),
                     reads=("ps%d" % pb,), writes=("xT%d" % c,))
        for c in range(8):
            sq_op(ti, c, ("xT%d" % c, "xTs") if ti == 0 else ("xT%d" % c,))
        for c in range(8):
            stats_mm(ti, c)
        rstd_ops(ti)

    def store_y(ti):
        S.fence(BIG_ALL, ["yrow_s"])
        S.fence(H_KEYS + MX_KEYS, YSTG_KEYS)
        for c in range(8):
            pb = c % 2
            S.op("dve", CALL("tensor_tensor", out=tmpf[:, pb, 0:T], in0=xT[:, c, 0:T],
                                                              in1=rstd[:, 0:T], op=ALU.mult),
                 reads=("xT%d" % c, "rstd"), writes=("tmpf%d" % pb,))
            S.op("act", CALL("activation", out=xT[:, c, 0:T], in_=tmpf[:, pb, 0:T], func=AF.Identity,
                                                           scale=colsT[:, 0, 100 + c:101 + c]),
                 reads=("tmpf%d" % pb, "colsT"), writes=("xT%d" % c,))
        if ti == 0:
            S.op("dve", CALL("tensor_tensor", out=xT[:, :, T:TS], in0=xT[:, :, T:TS],
                                                  in1=bcast(rstd[:, T:TS], 1, 8), op=ALU.mult),
                 reads=("xTs", "rstds"), writes=("xTs",))
            S.op("dve", CALL("tensor_tensor", out=xT[:, :, T:TS], in0=xT[:, :, T:TS],
                                                  in1=bcast(colsT[:, 0, 100:108], 2, NS), op=ALU.mult),
                 reads=("xTs", "colsT"), writes=("xTs",))
            S.op("pe", seq([(CALL("transpose", psum[0:NS, 6 + c // 4, (c % 4) * 128:(c % 4 + 1) * 128],
                                                          xT[:, c, T:TS], ident[:, :])) for c in range(8)]),
                 reads=("xTs", "ident"), writes=PS6 + ["ps7"])
            S.op("dve", CALL("tensor_copy", out=yrow_s[:, :], in_=psum[0:NS, 6:8, :].rearrange("p a b -> p (a b)")),
                 reads=PS6 + ["ps7"], writes=("yrow_s",))
            S.dma("sp", "outy%d" % ti, CALL("dma_start", out=y_s, in_=yrow_s[:, :]), reads=("yrow_s",))
        for b in range(8):
            pp = b % 3
            S.op("pe", seq([(CALL("transpose",
                psum[:, 2 * pp + c // 4, (c % 4) * 128:(c % 4 + 1) * 128], xT[:, c, b * 128:(b + 1) * 128], ident[:, :]))
                for c in range(8)]),
                 reads=["xT%d" % c for c in range(8)] + ["ident"], writes=("ps%d" % (2 * pp), "ps%d" % (2 * pp + 1)))
            yk = "ystg%d" % (b % 4)
            if b % 2 == 0:
                S.op("dve", CALL("tensor_copy", out=ystg[:, b % 4, :], in_=ppair(pp)),
                     reads=("ps%d" % (2 * pp), "ps%d" % (2 * pp + 1)), writes=(yk,))
            else:
                S.op("act", CALL("activation", out=ystg[:, b % 4, :], in_=ppair(pp), func=AF.Identity),
                     reads=("ps%d" % (2 * pp), "ps%d" % (2 * pp + 1)), writes=(yk,))
            r0 = ti * T + b * 128
            S.dma("sp", "outy%d" % ti, CALL("dma_start", out=y_p[r0:r0 + 128, :], in_=ystg[:, b % 4, :]),
                  reads=(yk,))
        S.fence(YSTG_KEYS, H_KEYS + MX_KEYS)

    def mixer(ti, l):
        modulate_ops(ti, l, 1)
        sg = segs(ti)
        last = (ti == NT - 1)
        S.fence(BIG_ALL, MIX_KEYS)
        cnt = [0]

        def fm_proj(wv, col0, dst_kind, gidx, skip_pe=False):
            pp = cnt[0] % 3
            sl = cnt[0]
            cnt[0] += 1
            fs = []
            for k in range(8):
                for (o, w, nm) in sg:
                    out = (psum[:, 2 * pp, :] if nm == "h0" else psum[:, 2 * pp + 1, :] if nm == "h1"
                           else SL(sl)[0])
                    fs.append(CALL("matmul",
                        out, wv[:, k, col0:col0 + 128], hT[:, k, o:o + w], start=(k == 0), stop=(k == 7)))
            wr = ["ps%d" % (2 * pp), "ps%d" % (2 * pp + 1)] + ([SL(sl)[1]] if ti == 0 else [])
            if not skip_pe:
                S.op("pe", seq(fs), reads=[wkey[0]] + hkeys(ti), writes=wr)
            pk = ("ps%d" % (2 * pp), "ps%d" % (2 * pp + 1))
            if dst_kind == "q":
                S.op("act", CALL("activation", out=qT[:, gidx, 0:T], in_=ppair(pp), func=AF.Identity, scale=0.125),
                     reads=pk, writes=("qT",))
            elif dst_kind == "k":
                S.op("act", CALL("activation", out=kT[:, 0:T], in_=ppair(pp), func=AF.Identity),
                     reads=pk, writes=("kT",))
                if last:
                    S.op("act", CALL("activation", out=kTf[:, :], in_=psum[:, 2 * pp + 1, 384:512], func=AF.Identity),
                         reads=pk, writes=("kTf",))
            elif dst_kind == "u":
                S.op("act" if gidx % 2 == 0 else "dve",
                     (CALL("activation", out=uT[:, gidx, 16:16 + T], in_=ppair(pp), func=AF.Identity))
                     if gidx % 2 == 0 else
                     (CALL("tensor_copy", out=uT[:, gidx, 16:16 + T], in_=ppair(pp))),
                     reads=pk, writes=("uT",))
            if ti == 0:
                zi = {"q": gidx, "k": 4, "v": 5, "u": 6 + gidx}[dst_kind]
                S.op("dve", CALL("tensor_copy", out=zsT[:, zi, :], in_=SL(sl)[0]),
                     reads=(SL(sl)[1],), writes=("zsT",))

        def tok_proj_sample(wv, col0, ncols, zcol0):
            fs = [(CALL("matmul", psum[0:NS, 7, 0:ncols], hT[:, k, T:TS], wv[:, k, col0:col0 + ncols],
                                           start=(k == 0), stop=(k == 7))) for k in range(8)]
            S.op("pe", seq(fs), reads=[wkey[0], "hTs"], writes=("ps7",))
            S.op("dve", CALL("tensor_copy", out=ztok[:, zcol0:zcol0 + ncols], in_=psum[0:NS, 7, 0:ncols]),
                 reads=("ps7",), writes=("ztok",))

        wkey = [None]
        src = w_in[l][:, 768:1280].rearrange("(k p) n -> p k n", p=128)
        s, _ = wload([(lambda v: v.rearrange("p (k n) -> p k n", k=8), src)])
        wkey[0] = "ws%d" % s
        wuu = wsl[:, s, :].rearrange("p (k n) -> p k n", k=8)
        S.op("dve", CALL("tensor_copy", out=uT[:, :, 0:16], in_=ucar[:, l, :, :]),
             reads=("ucar%d" % l,), writes=("uT",))
        if ti != 0:
            for k in range(8):
                fs = []
                for g3 in range(3):
                    for (o, w, nm) in sg:
                        out = psum[:, 2 * g3, :] if nm == "h0" else psum[:, 2 * g3 + 1, :]
                        fs.append(CALL("matmul", out, wuu[:, k, g3 * 128:(g3 + 1) * 128], hT[:, k, o:o + w],
                                       start=(k == 0), stop=(k == 7)))
                S.op("pe", seq(fs), reads=[wkey[0], "hT%d" % k], writes=["ps%d" % b for b in range(6)])
        for g in range(4):
            fm_proj(wuu, g * 128, "u", g, skip_pe=(ti != 0 and g < 3))
        if ti == 0:
            tok_proj_sample(wuu, 0, 512, 768)
        S.op("dve", CALL("tensor_copy", out=ucar[:, l, :, :], in_=uT[:, :, T:T + 16]),
             reads=("uT",), writes=("ucar%d" % l,))

        for g in range(4):
            wdw = POOL_W[g]
            cur = uT[:, g, :]
            lvl = 1
            step = 0
            while lvl < wdw:
                dstb = tmpf[:, step % 2, :]
                lo = 2 * lvl - 1
                S.op("dve", CALL("tensor_tensor",
                    out=dstb[:, lo:16 + T], in0=cur[:, lo:16 + T], in1=cur[:, lo - lvl:16 + T - lvl], op=ALU.add),
                     reads=("uT", "tmpf0", "tmpf1"), writes=("tmpf%d" % (step % 2),))
                cur = dstb
                lvl *= 2
                step += 1
            S.op("dve", CALL("scalar_tensor_tensor",
                out=pooledT[:, g, 0:T], in0=cur[:, 16:16 + T], scalar=1.0 / wdw, in1=uT[:, g, 16:16 + T],
                op0=ALU.mult, op1=ALU.subtract), reads=("uT", "tmpf0", "tmpf1"), writes=("pooledT",))
            if ti == 0:
                S.op("dve", CALL("tensor_tensor", out=smt[:, 2, 0, :], in0=cur[:, 16:32],
                                                                    in1=invc[:, g, :], op=ALU.mult),
                     reads=("tmpf0", "tmpf1", "invc"), writes=("smt",))
                S.op("dve", CALL("tensor_tensor", out=pooledT[:, g, 0:16], in0=smt[:, 2, 0, :],
                                                           in1=uT[:, g, 16:32], op=ALU.subtract),
                     reads=("smt", "uT"), writes=("pooledT",))
        if last:
            S.op("pe", seq([(CALL("transpose", psum[0:15, 7, g * 128:(g + 1) * 128],
                                                          uT[:, g, 16 + T - 15:16 + T], ident[:, :])) for g in range(4)]),
                 reads=("uT", "ident"), writes=("ps7",))
            S.op("dve", CALL("tensor_copy", out=utok_rows[0:15, :], in_=psum[0:15, 7, :]),
                 reads=("ps7",), writes=("utok_rows",))
            S.dma("sp", "outl%d" % l, CALL("dma_start", out=o_pp[l], in_=utok_rows[0:15, :]), reads=("utok_rows",))
        src = w_in[l][:, 0:512].rearrange("(k p) n -> p k n", p=128)
        s, _ = wload([(lambda v: v.rearrange("p (k n) -> p k n", k=8), src)])
        wkey[0] = "ws%d" % s
        wq = wsl[:, s, :].rearrange("p (k n) -> p k n", k=8)
        for g in range(4):
            fm_proj(wq, g * 128, "q", g)
        if ti == 0:
            tok_proj_sample(wq, 0, 512, 0)
        src = w_in[l][:, 512:768].rearrange("(k p) n -> p k n", p=128)
        s, _ = wload([(lambda v: v[:, 0:2048].rearrange("p (k n) -> p k n", k=8), src)])
        wkey[0] = "ws%d" % s
        wkv = wsl[:, s, 0:2048].rearrange("p (k n) -> p k n", k=8)
        fm_proj(wkv, 0, "k", 0)
        for hf in range(2):
            pb = 2 * (cnt[0] % 3) + hf
            fs = []
            for bb in range(4):
                b = hf * 4 + bb
                for k in range(8):
                    fs.append(CALL("matmul",
                        psum[:, pb, bb * 128:(bb + 1) * 128], hT[:, k, b * 128:(b + 1) * 128], wkv[:, k, 128:256],
                        start=(k == 0), stop=(k == 7)))
            S.op("pe", seq(fs), reads=[wkey[0]] + H_KEYS[:8], writes=("ps%d" % pb,))
            S.op("act", CALL("activation",
                out=vtok[:, hf * 4:hf * 4 + 4, :], in_=psum[:, pb, :].rearrange("p (a b) -> p a b", a=4), func=AF.Identity),
                 reads=("ps%d" % pb,), writes=("vtok",))
            if last and hf == 1:
                S.op("act", CALL("activation", out=vtokf[:, :], in_=psum[:, pb, 384:512], func=AF.Identity),
                     reads=("ps%d" % pb,), writes=("vtokf",))
        cnt[0] += 1
        if ti == 0:
            sl = cnt[0]
            cnt[0] += 1
            S.op("pe", seq([(CALL("matmul", SL(sl)[0], wkv[:, k, 128:256],
                                                           hT[:, k, T:TS], start=(k == 0), stop=(k == 7)))
                            for k in range(8)]), reads=[wkey[0], "hTs"], writes=(SL(sl)[1],))
            S.op("dve", CALL("tensor_copy", out=zsT[:, 5, :], in_=SL(sl)[0]),
                 reads=(SL(sl)[1],), writes=("zsT",))
            tok_proj_sample(wkv, 0, 256, 512)
        if last:
            S.op("pe", CALL("transpose", psum[:, 7, 0:128], kTf[:, :], ident[:, :]),
                 reads=("kTf", "ident"), writes=("ps7",))
            S.op("dve", CALL("tensor_copy", out=kTf[:, :], in_=psum[:, 7, 0:128]), reads=("ps7",), writes=("kTf",))
            S.dma("sp", "outl%d" % l, CALL("dma_start", out=o_kp[l], in_=kTf[:, :]), reads=("kTf",))
            S.dma("sp", "outl%d" % l, CALL("dma_start", out=o_vp[l], in_=vtokf[:, :]), reads=("vtokf",))
        if ti == 0:
            sample_mixer(l)

        S.fence(H_KEYS, MX_KEYS)
        src = pool_w[l].rearrange("g p n -> p g n")
        s, _ = wload([(lambda v: v[:, 0:512].rearrange("p (g n) -> p g n", g=4), src)])
        wpl = wsl[:, s, 0:512].rearrange("p (g n) -> p g n", g=4)
        for g in range(4):
            pp = cnt[0] % 3
            sl = cnt[0]
            cnt[0] += 1
            fs = []
            for (o, w, nm) in sg:
                out = (psum[:, 2 * pp, :] if nm == "h0" else psum[:, 2 * pp + 1, :] if nm == "h1"
                       else SL(sl)[0])
                fs.append(CALL("matmul", out, wpl[:, g, :], pooledT[:, g, o:o + w],
                                                                      start=True, stop=True))
            wr = ["ps%d" % (2 * pp), "ps%d" % (2 * pp + 1)] + ([SL(sl)[1]] if ti == 0 else [])
            S.op("pe", seq(fs), reads=["ws%d" % s, "pooledT"] + (["pooledTs"] if ti == 0 else []), writes=wr)
            S.op("act", CALL("activation", out=mixedT[:, 4 + g, 0:T], in_=ppair(pp), func=AF.Identity,
                                                           scale=colsT[:, l, 96 + g:97 + g]),
                 reads=("ps%d" % (2 * pp), "ps%d" % (2 * pp + 1), "colsT"), writes=("mx%d" % (4 + g),))
            if ti == 0:
                S.op("act", CALL("activation", out=mixedT[:, 4 + g, T:TS],
                                                               in_=SL(sl)[0], func=AF.Identity,
                                                               scale=colsT[:, l, 96 + g:97 + g]),
                     reads=(SL(sl)[1], "colsT"), writes=("mxs",))

        def unit_info(n):
            qb, j = n // 2, n % 2
            first_blk = (ti == 0 and qb == 0)
            types = ([] if first_blk else [0]) + [1]
            return qb, j, j * 64, (j + 1) * 64, n % 2, types

        def pbuf_of(n):
            b4 = n % 4
            if b4 < 2:
                return pT[:, b4, :, :], "pT%d" % b4
            return sil[:, b4 - 2, 0:1024].rearrange("p (a b) -> p a b", a=2), "sil%d" % (b4 - 2)

        def sbuf_of(n):
            b4 = n % 4
            if b4 < 2:
                return sbf[:, b4, :, :], "sbf%d" % b4
            return tmpf[:, b4 - 2, 0:1024].rearrange("p (a b) -> p a b", a=2), "tmpf%d" % (b4 - 2)

        def att_front(n):
            qb, j, P0, P1, u, types = unit_info(n)
            fs = []
            for t in types:
                if t == 0:
                    kl = kcar[P0:P1, l, :] if qb == 0 else kT[P0:P1, (qb - 1) * 128:qb * 128]
                else:
                    kl = kT[P0:P1, qb * 128:(qb + 1) * 128]
                fs.append(CALL("matmul", psum[:, 2 * u + t, :], kl, qT[P0:P1, :, qb * 128:(qb + 1) * 128],
                               start=True, stop=False))
                fs.append(CALL("matmul", psum[:, 2 * u + t, :], identb[:, :],
                               biasTh[:, t, 4 * j:4 * j + 4, :].rearrange("p h q -> p (h q)"), start=False, stop=False))
                fs.append(CALL("matmul", psum[:, 2 * u + t, :], identb[:, :],
                               biasTl[:, t, 4 * j:4 * j + 4, :].rearrange("p h q -> p (h q)"), start=False, stop=True))
            S.op("pe", seq(fs), reads=("kT", "qT", "kcar%d" % l, "biasTh", "biasTl", "identb"),
                 writes=("ps%d" % (2 * u), "ps%d" % (2 * u + 1)))
            t0 = types[0]
            pbv, pbk = pbuf_of(n)
            S.op("act", CALL("activation", out=pbv[:, t0:2, :].rearrange("p a b -> p (a b)"),
                             in_=psum[:, 2 * u + t0:2 * u + 2, :].rearrange("p a b -> p (a b)"), func=AF.Exp),
                 reads=("ps%d" % (2 * u), "ps%d" % (2 * u + 1)), writes=(pbk,))

        def att_back(n):
            qb, j, P0, P1, u, types = unit_info(n)
            ob = 4 + (qb % 2)
            db = 6 + (qb % 2)
            dkeys = ["ps%d" % db]
            fs = []
            for n_, t in enumerate(types):
                if t == 0:
                    vl = vcar[:, l, P0:P1] if qb == 0 else vtok[:, qb - 1, P0:P1]
                else:
                    vl = vtok[:, qb, P0:P1]
                fs.append(CALL("matmul", psum[P0:P1, ob, :], vl, pbuf_of(n)[0][:, t, :],
                               start=(n_ == 0), stop=(n_ == len(types) - 1)))
            for n_, t in enumerate(types):
                fs.append(CALL("matmul", psum[P0:P1, db, :], onesb[:, 0:64], pbuf_of(n)[0][:, t, :],
                               start=(n_ == 0), stop=(n_ == len(types) - 1)))
            S.op("pe", seq(fs), reads=(pbuf_of(n)[1], "vtok", "vcar%d" % l, "onesb"), writes=["ps%d" % ob] + dkeys)
            if j == 1:
                r = qb % 2
                S.op("dve", CALL("tensor_tensor", out=rden[:, r, :].rearrange("p (g q) -> p g q", g=4),
                                 in0=psum[:, db, :].rearrange("p (g q) -> p g q", g=4),
                                 in1=bcast(esink[:, l, :], 2, 128), op=ALU.add),
                     reads=dkeys + ["esink"], writes=("rden%d" % r,))
                S.op("act", CALL("activation", out=rden[:, r, :], in_=rden[:, r, :], func=AF.Ln),
                     reads=("rden%d" % r,), writes=("rden%d" % r,))
                S.op("act", CALL("activation", out=rden[:, r, :], in_=rden[:, r, :], func=AF.Exp, scale=-1.0),
                     reads=("rden%d" % r,), writes=("rden%d" % r,))
                S.op("dve", CALL("tensor_tensor", out=mixedT[:, 0:4, qb * 128:(qb + 1) * 128],
                                 in0=psum[:, ob, :].rearrange("p (g q) -> p g q", g=4),
                                 in1=rden[:, r, :].rearrange("p (g q) -> p g q", g=4), op=ALU.mult),
                     reads=("ps%d" % ob, "rden%d" % r), writes=("mx0", "mx1", "mx2", "mx3"))

        LAG = 3
        for n in range(min(LAG, 16)):
            att_front(n)
        for n in range(16):
            if n + LAG < 16:
                att_front(n + LAG)
            att_back(n)
        S.op("act", CALL("activation", out=kcar[:, l, :], in_=kT[:, T - 128:T], func=AF.Identity),
             reads=("kT",), writes=("kcar%d" % l,))
        S.op("act", CALL("activation", out=vcar[:, l, :], in_=vtok[:, 7, :], func=AF.Identity),
             reads=("vtok",), writes=("vcar%d" % l,))

        S.fence(["pooledT", "pooledTs", "qT", "qTs"], XSQ_KEYS)
        wst = {}

        def lhs_of(k, m):
            return wst[m // 2][:, k, (m % 2) * 128:(m % 2 + 1) * 128]

        sgm = segs(ti)
        for m in range(8):
            if m % 2 == 0:
                src = w_out[l][:, m * 128:(m + 2) * 128].rearrange("(k p) n -> p k n", p=128)
                s, _ = wload([(lambda v: v[:, 0:2048].rearrange("p (k n) -> p k n", k=8), src)])
                wst[m // 2] = wsl[:, s, 0:2048].rearrange("p (k n) -> p k n", k=8)
                wst[("s", m // 2)] = s
            pp = m % 2
            fs = []
            for k in range(8):
                lt = lhs_of(k, m)
                for (o, w, nm) in sgm:
                    out = (psum[:, 2 * pp, :] if nm == "h0" else psum[:, 2 * pp + 1, :] if nm == "h1"
                           else psum[:, 6, 256 + m * 16:256 + (m + 1) * 16])
                    fs.append(CALL("matmul",
                        out, lt, mixedT[:, k, o:o + w], start=(k == 0), stop=(k == 7)))
            wr = ["ps%d" % (2 * pp), "ps%d" % (2 * pp + 1)] + (PS6 if ti == 0 else [])
            S.op("pe", seq(fs), reads=["ws%d" % wst[("s", m // 2)]] + (MX_KEYS if ti == 0 else MX_KEYS[:8]), writes=wr)
            residual_ops(ti, l, 1, m, pp)
            if ti != 0:
                sq_op(ti, m, ("xT%d" % m,), alt=True)
                if m >= 1:
                    stats_mm(ti, m - 1, alt=True)
        S.fence(MX_KEYS, H_KEYS)
        if ti != 0:
            stats_mm(ti, 7, alt=True)
        if ti == 0:
            residual_sample(ti, l, 1)
            for m in range(8):
                sq_op(ti, m, ("xT%d" % m, "xTs"), alt=True)
            for m in range(8):
                stats_mm(ti, m, alt=True)
        rstd_ops(ti)

    def sample_mixer(l):
        S.dma("pool", "kvc", CALL("dma_start", out=kc[:, :, :], in_=ck[l].rearrange("b j f -> j b f")),
              writes=("kc",))
        S.dma("pool", "kvc", CALL("dma_start", out=vc[:, :, :], in_=cv[l].rearrange("b j f -> j b f")),
              writes=("vc",))
        S.lastw["kc"] = ("kvc", S.cnt["kvc"])
        S.lastw["vc"] = ("kvc", S.cnt["kvc"])
        S.dma("sp", "outs%d" % l, CALL("dma_start", out=o_ks[l, :, 0:127, :], in_=ck[l, :, 1:128, :]))
        S.dma("sp", "outs%d" % l, CALL("dma_start", out=o_vs[l, :, 0:127, :], in_=cv[l, :, 1:128, :]))
        S.dma("sp", "outs%d" % l, CALL("dma_start", out=o_ps[l, :, 0:14, :], in_=spool[l, :, 1:15, :]))
        S.dma("sp", "outs%d" % l, CALL("dma_start", out=o_ks[l, :, 127, :], in_=ztok[:, 512:640]), reads=("ztok",))
        S.dma("sp", "outs%d" % l, CALL("dma_start", out=o_vs[l, :, 127, :], in_=ztok[:, 640:768]), reads=("ztok",))
        S.dma("sp", "outs%d" % l, CALL("dma_start", out=o_ps[l, :, 14, :], in_=ztok[:, 768:1280]), reads=("ztok",))
        S.fence(BIG_ALL, ["hrows"])
        for a in range(2):
            S.dma("sp", "hist", CALL("dma_start",
                out=hrows[:, a, :], in_=spool[l, a * 8:(a + 1) * 8, :, :].rearrange("b s f -> (b s) f")),
                  writes=("hrows",))
        for g in range(4):
            S.op("pe", seq([(CALL("transpose", psum[:, 7, a * 120:(a + 1) * 120],
                                                               hrows[:, a, g * 128:(g + 1) * 128], ident[0:120, 0:120]))
                            for a in range(2)]), reads=("hrows", "ident"), writes=("ps7",))
            S.op("dve", CALL("tensor_copy", out=uhist[:, g, :], in_=psum[:, 7, 0:240]),
                 reads=("ps7",), writes=("uhist",))
        for g in range(4):
            wdw = POOL_W[g]
            hv = uhist[:, g, :].rearrange("p (b s) -> p b s", s=15)[:, :, 15 - (wdw - 1):15]
            S.op("dve", CALL("tensor_reduce", out=smt[:, 2, 0, :], in_=hv, axis=AX.X, op=ALU.add),
                 reads=("uhist",), writes=("smt",))
            S.op("dve", CALL("tensor_tensor", out=smt[:, 2, 1, :], in0=smt[:, 2, 0, :], in1=zsT[:, 6 + g, :],
                                                       op=ALU.add), reads=("smt", "zsT"), writes=("smt",))
            S.op("dve", CALL("scalar_tensor_tensor",
                out=pooledT[:, g, T:TS], in0=smt[:, 2, 1, :], scalar=1.0 / wdw, in1=zsT[:, 6 + g, :],
                op0=ALU.mult, op1=ALU.subtract), reads=("smt", "zsT"), writes=("pooledTs",))
        S.op("dve", CALL("tensor_copy", out=qtokb[:, :], in_=ztok[:, 0:512]), reads=("ztok",), writes=("qtokb",))
        for b in range(NS):
            pb = b % 2
            S.op("pe", CALL("matmul", psum[:, pb, :], selb[:, b, :], qtokb[:, :], start=True, stop=True),
                 reads=("selb", "qtokb"), writes=("ps%d" % pb,))
            S.op("dve", CALL("tensor_tensor",
                out=prod[:, :].rearrange("p (g hh d) -> p g hh d", g=4, hh=2),
                in0=psum[:, pb, :].rearrange("p (g hh d) -> p g hh d", g=4, hh=2),
                in1=bcast(kc[:, b, :].rearrange("p (hh d) -> p hh d", hh=2), 1, 4), op=ALU.mult),
                 reads=("ps%d" % pb, "kc"), writes=("tmpf0",))
            S.op("dve", CALL("tensor_reduce", out=s_s[:, b, :], in_=prod[:, :].rearrange("p (h d) -> p h d", d=64),
                                                       axis=AX.X, op=ALU.add), reads=("tmpf0",), writes=("s_s",))
        S.op("dve", CALL("scalar_tensor_tensor", out=s_s[:, :, :], in0=s_s[:, :, :], scalar=0.125,
                                                     in1=bcast(bias_s[:, :], 1, NS), op0=ALU.mult, op1=ALU.add),
             reads=("s_s", "bias_s"), writes=("s_s",))
        S.op("act", CALL("activation", out=pT_s[:, :, :], in_=s_s[:, :, :], func=AF.Exp),
             reads=("s_s",), writes=("pT_s",))
        S.op("pe", seq([(CALL("matmul", psum[:, 2, b * 8:(b + 1) * 8], vc[:, b, :], pT_s[:, b, :],
                                                  start=True, stop=True)) for b in range(NS)]
                       + [CALL("matmul", psum[:, 3, 0:128], onesb[:, :], pT_s[:, :, :].rearrange("p b h -> p (b h)"),
                                             start=True, stop=True)]),
             reads=("vc", "pT_s", "onesb"), writes=("ps2", "ps3"))
        S.op("dve", CALL("tensor_tensor", out=smt[:, 3, :, :], in0=zsT[:, 0:4, :], in1=bcast(zsT[:, 4, :], 1, 4),
                                              op=ALU.mult), reads=("zsT",), writes=("smt3",))
        S.op("pe", CALL("matmul", psum[:, 7, 0:64], blk1[:, :], smt[:, 3, :, :].rearrange("p g b -> p (g b)"),
                                      start=True, stop=True), reads=("blk1", "smt3"), writes=("ps7",))
        for g in range(4):
            S.op("act", CALL("activation", out=smt[:, 4, g, :], in_=psum[:, 7, g * 16:(g + 1) * 16], func=AF.Exp,
                                                    scale=0.125, bias=bias0[:, g:g + 1]),
                 reads=("ps7", "bias0"), writes=("smt4",))
        for hh in range(2):
            P = slice(hh * 64, (hh + 1) * 64)
            oc_sel = psum[P, 2, 0:128].rearrange("p (b g hh) -> p b g hh", b=NS, g=4)[:, :, :, hh].rearrange("p b g -> p g b")
            dc_sel = psum[P, 3, 0:128].rearrange("p (b g hh) -> p b g hh", b=NS, g=4)[:, :, :, hh].rearrange("p b g -> p g b")
            S.op("dve", CALL("tensor_tensor", out=smt[P, 5, :, :], in0=smt[P, 4, :, :],
                                                       in1=bcast(zsT[P, 5, :], 1, 4), op=ALU.mult),
                 reads=("smt4", "zsT"), writes=("smt5",))
            S.op("dve", CALL("tensor_tensor", out=smt[P, 5, :, :], in0=smt[P, 5, :, :],
                                                                      in1=oc_sel, op=ALU.add),
                 reads=("smt5", "ps2"), writes=("smt5",))
            S.op("dve", CALL("tensor_tensor", out=smt[P, 3, :, :], in0=smt[P, 4, :, :],
                                                                      in1=dc_sel, op=ALU.add),
                 reads=("smt4", "ps3", "smt3"), writes=("smt3",))
            S.op("dve", CALL("tensor_tensor", out=smt[P, 3, :, :], in0=smt[P, 3, :, :],
                                                       in1=bcast(esink[P, l, :], 2, NS), op=ALU.add),
                 reads=("smt3", "esink"), writes=("smt3",))
            S.op("dve", CALL("reciprocal", out=smt[P, 3, :, :], in_=smt[P, 3, :, :]),
                 reads=("smt3",), writes=("smt3",))
            S.fence(H_KEYS, ["mxs"])
            S.op("dve", CALL("tensor_tensor", out=mixedT[P, 0:4, T:TS], in0=smt[P, 5, :, :],
                                                       in1=smt[P, 3, :, :], op=ALU.mult),
                 reads=("smt5", "smt3"), writes=("mxs",))

    phases = []
    for ti in range(NT):
        if ti == 0:
            phases.append(lambda: load_x_dma(0))
        phases.append(lambda ti=ti: load_x(ti))
        for l in range(DEPTH):
            phases.append(lambda ti=ti, l=l: ffn(ti, l, 0, 0))
            phases.append(lambda ti=ti, l=l: mixer(ti, l))
            phases.append(lambda ti=ti, l=l: ffn(ti, l, 1, 2))
        if ti + 1 < NT:
            phases.append(lambda ti=ti: load_x_dma(ti + 1))
        phases.append(lambda ti=ti: store_y(ti))
    def flush_ada():
        while pending:
            pop_pending()
        ada_finish(1)

    phases.insert(5, flush_ada)
    for ph in phases[:DEBUG_PHASES]:
        ph()

    fin = [(s, v) for s, v in S.cnt.items() if s.startswith("out")]
    S.wait_all("sp", fin)

    sems = {}
    for name in sorted(S.semnames):
        sems[name] = es.enter_context(nc.semaphore("s_" + name))
    block = es.enter_context(nc.Block())

    def replay(e, items):
        for it in items:
            if it[0] == "wait":
                e.wait_ge(sems[it[1]], it[2])
            else:
                ins = it[1](e)
                ins.then_inc(sems[it[2]], it[3])

    @block.tensor
    def _(e):
        replay(e, S.q["pe"])

    @block.scalar
    def _(e):
        replay(e, S.q["act"])

    @block.vector
    def _(e):
        replay(e, S.q["dve"])

    @block.gpsimd
    def _(e):
        replay(e, S.q["pool"])

    @block.sync
    def _(e):
        replay(e, S.q["sp"])

    es.close()
    return nc


_CACHE = {}


def _consts():
    c = {}
    c["c_ident"] = np.eye(128, dtype=np.float32)
    c["c_J"] = np.ascontiguousarray(np.eye(128, dtype=np.float32)[::-1])
    k = np.arange(128)[:, None]
    q = np.arange(128)[None, :]
    neg = np.zeros((128, 2, 128), np.float32)
    neg[:, 0, :] = np.where(q <= k, 0.0, NEG)
    neg[:, 1, :] = np.where(q >= k, 0.0, NEG)
    c["c_neg"] = neg
    invc = np.zeros((128, 4, 16), np.float32)
    for g, w in enumerate(POOL_W):
        invc[:, g, :] = 1.0 / np.minimum(np.arange(16) + 1, w).astype(np.float32)
    c["c_invc"] = invc
    oh = np.zeros((32, 640), np.float32)
    for e in range(255):
        d_prev = e + 1
        if d_prev <= 128:
            oh[int(t5_bucket_np(np.array(d_prev))), e] = 1.0
        d_own = e - 127
        if d_own >= 0:
            oh[int(t5_bucket_np(np.array(d_own))), 256 + e] = 1.0
    for j in range(128):
        oh[int(t5_bucket_np(np.array(128 - j))), 512 + j] = 1.0
    c["c_oh"] = oh
    sel = np.zeros((16, 16, 128), np.float32)
    for b in range(16):
        sel[b, b, :] = 1.0
    c["c_sel"] = sel.reshape(16, 2048)
    return c


def kernel(x_prompt, x_sample, c_prompt, c_sample, cache_k, cache_v, state_pool,
           w_ada, b_ada, norm_gain, w_in, w_out, sinks, rel_bias, pool_w, pool_scale,
           ffn1_wg, ffn1_wu, ffn1_wd, ffn2_wg, ffn2_wu, ffn2_wd, final_gain):
    f = lambda a: np.ascontiguousarray(np.asarray(a, dtype=np.float32))
    if "nc" not in _CACHE:
        _CACHE["nc"] = build_program()
    nc = _CACHE["nc"]
    qperm = np.concatenate([np.concatenate([np.arange(g * 64, g * 64 + 64), np.arange((4 + g) * 64, (4 + g) * 64 + 64)])
                            for g in range(4)])
    w_in_p = f(w_in).copy()
    w_in_p[:, :, 0:512] = f(w_in)[:, :, qperm]
    w_out_p = f(w_out).copy()
    w_out_p[:, 0:512, :] = f(w_out)[:, qperm, :]
    shared = {
        "w_ada": f(w_ada), "b_ada": f(b_ada).reshape(DEPTH, 72, 128), "ngain": f(norm_gain).reshape(DEPTH, 24, 128),
        "w_in": w_in_p, "w_out": w_out_p, "sinks": f(sinks), "relb": f(rel_bias), "pool_w": f(pool_w),
        "pscale": f(pool_scale).reshape(DEPTH, 4, 128),
        "f1g": f(ffn1_wg), "f1u": f(ffn1_wu), "f1d": f(ffn1_wd), "f2g": f(ffn2_wg), "f2u": f(ffn2_wu), "f2d": f(ffn2_wd),
        "fgain": f(final_gain).reshape(8, 128),
    }
    shared.update(_consts())
    xp_, xs_, cp_, cs_ = f(x_prompt), f(x_sample), f(c_prompt), f(c_sample)
    ck_, cv_, sp_ = f(cache_k), f(cache_v), f(state_pool)
    in_maps = []
    for i in range(NCORES):
        m = dict(shared)
        sl = slice(i * NS, (i + 1) * NS)
        m["xp"] = xp_[i]
        m["xs"] = xs_[sl, 0, :]
        m["cp"] = cp_[i].reshape(8, 128)
        m["cs"] = cs_[sl]
        m["ck"] = np.ascontiguousarray(ck_[:, sl].reshape(DEPTH, NS, 128, 128))
        m["cv"] = np.ascontiguousarray(cv_[:, sl].reshape(DEPTH, NS, 128, 128))
        m["spool"] = np.ascontiguousarray(sp_[:, sl])
        in_maps.append(m)
    res = run_bass_kernel_spmd(nc, in_maps, core_ids=list(range(NCORES)))
    R = res.results
    y_prompt = np.stack([R[i]["y_p"] for i in range(NCORES)], axis=0)
    y_sample = np.concatenate([R[i]["y_s"] for i in range(NCORES)], axis=0).reshape(NCORES * NS, 1, D)
    nkp = np.stack([R[i]["o_kp"] for i in range(NCORES)], axis=1).reshape(DEPTH, NCORES, 128, 2, 64)
    nvp = np.stack([R[i]["o_vp"] for i in range(NCORES)], axis=1).reshape(DEPTH, NCORES, 128, 2, 64)
    npp = np.stack([R[i]["o_pp"] for i in range(NCORES)], axis=1)
    nks = np.concatenate([R[i]["o_ks"] for i in range(NCORES)], axis=1).reshape(DEPTH, NCORES * NS, 128, 2, 64)
    nvs = np.concatenate([R[i]["o_vs"] for i in range(NCORES)], axis=1).reshape(DEPTH, NCORES * NS, 128, 2, 64)
    nps = np.concatenate([R[i]["o_ps"] for i in range(NCORES)], axis=1)
    return (y_prompt.astype(np.float32), y_sample.astype(np.float32), nkp.astype(np.float32), nvp.astype(np.float32),
            npp.astype(np.float32), nks.astype(np.float32), nvs.astype(np.float32), nps.astype(np.float32))
```
